# Optimizing a Trainium2 kernel written in Bass

```python
import jax, jax.numpy as jnp
from jax import lax
import numpy as np

D_MODEL = 1024
BATCH = 16
SEQ = 4096
DEPTH = 2

HEAD_DIM = 64
SWA_HEADS = 8
SWA_KV_HEADS = 2
SWA_WINDOW = 128
SWA_BLOCK = SWA_WINDOW
RWKV_HEADS = 8
RWKV_DIM = RWKV_HEADS * HEAD_DIM
RWKV_W_LORA = 64
RWKV_A_LORA = 64
RWKV_G_LORA = 128
RWKV_GN_EPS = 64e-5
NSA_HEADS = 16
NSA_KV_HEADS = 4
CMP_LEN = 32
CMP_STRIDE = 16
CMP_HIDDEN = 256
SEL_LEN = 64
N_SEL = 8
NSA_WINDOW = 256
NSA_QBLOCK = 64
N_BRANCH = 3
D_FF = 2816
CONV_WIDTH = 3
NORM_EPS = 1e-6

N_EVEN = (DEPTH + 1) // 2
N_ODD = DEPTH // 2

SWA_Q_COLS = SWA_HEADS * HEAD_DIM
SWA_KV_COLS = SWA_KV_HEADS * HEAD_DIM
SWA_COLS = SWA_Q_COLS + 2 * SWA_KV_COLS
RWKV_SPLITS = (RWKV_DIM, RWKV_DIM, RWKV_DIM, RWKV_W_LORA, RWKV_A_LORA, RWKV_G_LORA)
RWKV_COLS = sum(RWKV_SPLITS)
HY_COLS = SWA_COLS + RWKV_COLS
HY_OUT = SWA_Q_COLS + RWKV_DIM
NSA_KV_COLS = NSA_KV_HEADS * HEAD_DIM
NSA_SPLITS = (NSA_HEADS * HEAD_DIM,) + (NSA_KV_COLS,) * 6 + (NSA_HEADS * N_BRANCH,)
NSA_COLS = sum(NSA_SPLITS)
NSA_OUT = NSA_HEADS * HEAD_DIM

kernel_name = "hybrid_swa_rwkv7_nsa_convffn"


def split_cols(z, sizes):
    return jnp.split(z, [int(c) for c in np.cumsum(sizes)[:-1]], axis=-1)


def rms_norm(x, g):
    xf = x.astype(jnp.float32)
    y = xf * lax.rsqrt(jnp.mean(xf * xf, axis=-1, keepdims=True) + NORM_EPS)
    return (y * g.astype(jnp.float32)).astype(x.dtype)


def alibi_slopes(n_heads):
    return jnp.exp2(-8.0 * jnp.arange(1, n_heads + 1, dtype=jnp.float32) / n_heads)


def masked_softmax(s, valid):
    s = jnp.where(valid, s, -jnp.inf)
    m = jnp.max(s, axis=-1, keepdims=True)
    m = jnp.where(jnp.isfinite(m), m, 0.0)
    p = jnp.where(valid, jnp.exp(s - m), 0.0)
    d = jnp.sum(p, axis=-1, keepdims=True)
    return p / jnp.where(d > 0, d, 1.0)


def swa_sink_attention(q, k, v, sinks, slopes):
    B, T, _ = q.shape
    G, H, D, L = SWA_KV_HEADS, SWA_HEADS, HEAD_DIM, SWA_BLOCK
    R = H // G
    nb = T // L
    f32 = jnp.float32
    qb = q.reshape(B, nb, L, G, R, D).astype(f32) * D ** -0.5

    def band(a):
        a = jnp.pad(a.reshape(B, T, G, D), ((0, 0), (L, 0), (0, 0), (0, 0))).reshape(B, nb + 1, L, G, D)
        return jnp.concatenate([a[:, :-1], a[:, 1:]], axis=2).astype(f32)

    kb, vb = band(k), band(v)
    qpos = jnp.arange(nb)[:, None, None] * L + jnp.arange(L)[None, :, None]
    kpos = jnp.arange(nb)[:, None, None] * L - L + jnp.arange(2 * L)[None, None, :]
    dist = qpos - kpos
    valid = (dist >= 0) & (dist < SWA_WINDOW) & (kpos >= 0)
    s = jnp.einsum('bnqgrd,bnkgd->bngrqk', qb, kb)
    s = s - slopes.reshape(G, R)[None, None, :, :, None, None] * dist.astype(f32)[None, :, None, None]
    s = jnp.where(valid[None, :, None, None], s, -jnp.inf)
    sink = sinks.astype(f32).reshape(G, R)[None, None, :, :, None, None]
    m = jnp.maximum(jnp.max(s, axis=-1, keepdims=True), sink)
    p = jnp.exp(s - m)
    p = p / (jnp.sum(p, axis=-1, keepdims=True) + jnp.exp(sink - m))
    o = jnp.einsum('bngrqk,bnkgd->bnqgrd', p, vb)
    return o.reshape(B, T, H * D)


def rwkv7_time_mix(z, mu, w0, w2, a0, a2, g2, k_k, k_a, r_k, ln_g, ln_b):
    B, T, _ = z.shape
    H, N = RWKV_HEADS, HEAD_DIM
    f32 = jnp.float32
    z_prev = jnp.pad(z, ((0, 0), (1, 0), (0, 0)))[:, :-1]
    z = z + (z_prev - z) * mu
    r, k, v, w_lo, a_lo, g_lo = split_cols(z, RWKV_SPLITS)
    log_w = -jax.nn.softplus(-(w0 + jnp.tanh(w_lo) @ w2).astype(f32)) - 0.5
    decay = jnp.exp(-jnp.exp(log_w))
    a = jax.nn.sigmoid((a0 + a_lo @ a2).astype(f32))
    g = (jax.nn.sigmoid(g_lo) @ g2).astype(f32)
    heads = lambda t: t.astype(f32).reshape(B, T, H, N)
    r, k, v, decay, a = heads(r), heads(k), heads(v), heads(decay), heads(a)
    kk = k * k_k.astype(f32).reshape(H, N)
    kk = kk / jnp.maximum(jnp.linalg.norm(kk, axis=-1, keepdims=True), 1e-12)
    k = k * (1.0 + (a - 1.0) * k_a.astype(f32).reshape(H, N))

    def step(S, inp):
        r_t, w_t, k_t, v_t, kk_t, a_t = inp
        sa = jnp.einsum('bhij,bhj->bhi', S, kk_t)
        S = S * w_t[:, :, None, :] - sa[..., None] * (kk_t * a_t)[:, :, None, :] + v_t[..., None] * k_t[:, :, None, :]
        return S, jnp.einsum('bhij,bhj->bhi', S, r_t)

    seq = tuple(jnp.moveaxis(t, 1, 0) for t in (r, decay, k, v, kk, a))
    _, y = lax.scan(step, jnp.zeros((B, H, N, N), f32), seq)
    y = jnp.moveaxis(y, 0, 1)
    mean = jnp.mean(y, axis=-1, keepdims=True)
    var = jnp.mean(jnp.square(y - mean), axis=-1, keepdims=True)
    y = ((y - mean) * lax.rsqrt(var + RWKV_GN_EPS)).reshape(B, T, RWKV_DIM)
    y = y * ln_g.astype(f32) + ln_b.astype(f32)
    bonus = jnp.sum(r * k * r_k.astype(f32), axis=-1, keepdims=True) * v
    return (y + bonus.reshape(B, T, RWKV_DIM)) * g


def hybrid_swa_rwkv(h, w_in, w_out, sinks, mu, w0, w2, a0, a2, g2, k_k, k_a, r_k, ln_g, ln_b):
    z = h @ w_in
    q, k, v = split_cols(z[..., :SWA_COLS], (SWA_Q_COLS, SWA_KV_COLS, SWA_KV_COLS))
    o_a = swa_sink_attention(q, k, v, sinks, alibi_slopes(SWA_HEADS))
    o_b = rwkv7_time_mix(z[..., SWA_COLS:], mu, w0, w2, a0, a2, g2, k_k, k_a, r_k, ln_g, ln_b)
    o = jnp.concatenate([o_a, o_b], axis=-1).astype(h.dtype)
    return o @ w_out


def compress_blocks(a, idx, pos, w1, w2):
    blocks = a[:, idx] + pos[None, None, :, None, :]
    hdn = jax.nn.gelu(jnp.einsum('bclgd,ldf->bcgf', blocks, w1), approximate=True)
    return jnp.einsum('bcgf,fd->bcgd', hdn, w2)


def nsa_layer(h, w_in, w_out, pos_k, w1_k, w2_k, pos_v, w1_v, w2_v):
    B, T, _ = h.shape
    G, H, D = NSA_KV_HEADS, NSA_HEADS, HEAD_DIM
    R = H // G
    Q = NSA_QBLOCK
    f32 = jnp.float32
    z = h @ w_in
    q, kc, vc, ks, vs, kw, vw, gl = split_cols(z, NSA_SPLITS)
    q = q.reshape(B, T, G, R, D).astype(f32) * D ** -0.5
    kv = lambda t: t.reshape(B, T, G, D)
    kc, vc, ks, vs, kw, vw = kv(kc), kv(vc), kv(ks), kv(vs), kv(kw), kv(vw)
    gates = jax.nn.sigmoid(gl.astype(f32)).reshape(B, T, G, R, N_BRANCH)
    slopes = alibi_slopes(H).reshape(G, R)

    n_cmp = (T - CMP_LEN) // CMP_STRIDE + 1
    cmp_idx = np.arange(n_cmp)[:, None] * CMP_STRIDE + np.arange(CMP_LEN)[None, :]
    k_cmp = compress_blocks(kc, cmp_idx, pos_k, w1_k, w2_k).astype(f32)
    v_cmp = compress_blocks(vc, cmp_idx, pos_v, w1_v, w2_v).astype(f32)
    cmp_end = jnp.asarray(cmp_idx[:, -1], jnp.int32)

    n_blk = T // SEL_LEN
    k_sel = min(N_SEL, n_blk)
    ks_blk = ks.reshape(B, n_blk, SEL_LEN, G, D).transpose(0, 3, 1, 2, 4).astype(f32)
    vs_blk = vs.reshape(B, n_blk, SEL_LEN, G, D).transpose(0, 3, 1, 2, 4).astype(f32)
    ratio, span = SEL_LEN // CMP_STRIDE, CMP_LEN // CMP_STRIDE
    ov = (ratio * np.arange(n_blk)[:, None, None] + np.arange(ratio)[None, :, None]
          - np.arange(span)[None, None, :]).reshape(n_blk, ratio * span)
    ov_valid = jnp.asarray((ov >= 0) & (ov < n_cmp), f32)
    ov = jnp.asarray(np.clip(ov, 0, n_cmp - 1), jnp.int32)
    blk_ids = jnp.arange(n_blk)
    b_ix = jnp.arange(B)[:, None, None, None]
    g_ix = jnp.arange(G)[None, :, None, None]

    kw_pad = jnp.pad(kw, ((0, 0), (NSA_WINDOW, 0), (0, 0), (0, 0))).astype(f32)
    vw_pad = jnp.pad(vw, ((0, 0), (NSA_WINDOW, 0), (0, 0), (0, 0))).astype(f32)

    def block(i):
        t0 = i * Q
        t = t0 + jnp.arange(Q)
        qi = lax.dynamic_slice_in_dim(q, t0, Q, axis=1)
        g_blk = lax.dynamic_slice_in_dim(gates, t0, Q, axis=1)
        dist_c = t[:, None] - cmp_end[None, :]
        s = jnp.einsum('bqgrd,bcgd->bgrqc', qi, k_cmp) - slopes[:, :, None, None] * dist_c.astype(f32)
        p_cmp = masked_softmax(s, dist_c >= 0)
        o_cmp = jnp.einsum('bgrqc,bcgd->bqgrd', p_cmp, v_cmp)
        imp = jnp.sum(jnp.sum(p_cmp, axis=2)[..., ov] * ov_valid, axis=-1)
        cur = (t // SEL_LEN)[:, None]
        forced = (blk_ids == 0) | (blk_ids == cur) | (blk_ids == cur - 1)
        imp = jnp.where(forced, jnp.inf, jnp.where(blk_ids > cur, -jnp.inf, imp))
        _, sel = lax.top_k(imp, k_sel)
        k_g = ks_blk[b_ix, g_ix, sel]
        v_g = vs_blk[b_ix, g_ix, sel]
        kpos = sel[..., None] * SEL_LEN + jnp.arange(SEL_LEN)
        dist_s = (t[:, None, None] - kpos)[:, :, None]
        s = jnp.einsum('bqgrd,bgqnkd->bgrqnk', qi, k_g) - slopes[None, :, :, None, None, None] * dist_s.astype(f32)
        p = masked_softmax(s.reshape(B, G, R, Q, k_sel * SEL_LEN), (dist_s >= 0).reshape(B, G, 1, Q, k_sel * SEL_LEN))
        o_slc = jnp.einsum('bgrqnk,bgqnkd->bqgrd', p.reshape(s.shape), v_g)
        kwi = lax.dynamic_slice_in_dim(kw_pad, t0, NSA_WINDOW + Q, axis=1)
        vwi = lax.dynamic_slice_in_dim(vw_pad, t0, NSA_WINDOW + Q, axis=1)
        kp = t0 - NSA_WINDOW + jnp.arange(NSA_WINDOW + Q)
        dist_w = t[:, None] - kp[None, :]
        valid_w = (dist_w >= 0) & (dist_w < NSA_WINDOW) & (kp[None, :] >= 0)
        s = jnp.einsum('bqgrd,bkgd->bgrqk', qi, kwi) - slopes[:, :, None, None] * dist_w.astype(f32)
        o_win = jnp.einsum('bgrqk,bkgd->bqgrd', masked_softmax(s, valid_w), vwi)
        return g_blk[..., 0, None] * o_cmp + g_blk[..., 1, None] * o_slc + g_blk[..., 2, None] * o_win

    o = lax.map(block, jnp.arange(T // Q))
    o = jnp.moveaxis(o, 0, 1).reshape(B, T, NSA_OUT).astype(h.dtype)
    return o @ w_out


def conv_ffn(h, w_up, conv_w, conv_b, w_down):
    T = h.shape[1]
    gate, val = jnp.split(h @ w_up, 2, axis=-1)
    gp = jnp.pad(gate, ((0, 0), (CONV_WIDTH - 1, 0), (0, 0)))
    conv = conv_b + gp[:, CONV_WIDTH - 1:] * conv_w[CONV_WIDTH - 1]
    for j in range(CONV_WIDTH - 1):
        conv = conv + gp[:, j:j + T] * conv_w[j]
    return (jax.nn.gelu(conv, approximate=True) * val) @ w_down


def setup_inputs(seed: int = 0) -> dict:
    key = jax.random.key(seed)
    keys = iter(jax.random.split(key, 48))
    f32 = jnp.float32
    nrm = lambda shape, scale: jax.random.normal(next(keys), shape, f32) * scale
    uni = lambda shape, lo, hi: jax.random.uniform(next(keys), shape, f32, lo, hi)
    NE, NO = N_EVEN, N_ODD
    return {
        "x": nrm((BATCH, SEQ, D_MODEL), 1.0),
        "mix_pre_g": 1.0 + nrm((DEPTH, D_MODEL), 0.05),
        "mix_post_g": 1.0 + nrm((DEPTH, D_MODEL), 0.05),
        "ffn_pre_g": 1.0 + nrm((DEPTH, D_MODEL), 0.05),
        "ffn_post_g": 1.0 + nrm((DEPTH, D_MODEL), 0.05),
        "hy_w_in": nrm((NE, D_MODEL, HY_COLS), D_MODEL ** -0.5),
        "hy_w_out": nrm((NE, HY_OUT, D_MODEL), HY_OUT ** -0.5),
        "swa_sinks": nrm((NE, SWA_HEADS), 0.5),
        "rwkv_mu": uni((NE, RWKV_COLS), 0.0, 1.0),
        "rwkv_w0": uni((NE, RWKV_DIM), -5.0, -1.0),
        "rwkv_w2": nrm((NE, RWKV_W_LORA, RWKV_DIM), 0.1),
        "rwkv_a0": nrm((NE, RWKV_DIM), 0.1),
        "rwkv_a2": nrm((NE, RWKV_A_LORA, RWKV_DIM), 0.1),
        "rwkv_g2": nrm((NE, RWKV_G_LORA, RWKV_DIM), RWKV_G_LORA ** -0.5),
        "rwkv_k_k": 0.85 + nrm((NE, RWKV_DIM), 0.05),
        "rwkv_k_a": 1.0 + nrm((NE, RWKV_DIM), 0.05),
        "rwkv_r_k": nrm((NE, RWKV_HEADS, HEAD_DIM), 0.1),
        "rwkv_ln_g": 1.0 + nrm((NE, RWKV_DIM), 0.05),
        "rwkv_ln_b": nrm((NE, RWKV_DIM), 0.01),
        "nsa_w_in": nrm((NO, D_MODEL, NSA_COLS), D_MODEL ** -0.5),
        "nsa_w_out": nrm((NO, NSA_OUT, D_MODEL), NSA_OUT ** -0.5),
        "nsa_cmp_pos_k": nrm((NO, CMP_LEN, HEAD_DIM), 0.1),
        "nsa_cmp_w1_k": nrm((NO, CMP_LEN, HEAD_DIM, CMP_HIDDEN), (CMP_LEN * HEAD_DIM) ** -0.5),
        "nsa_cmp_w2_k": nrm((NO, CMP_HIDDEN, HEAD_DIM), CMP_HIDDEN ** -0.5),
        "nsa_cmp_pos_v": nrm((NO, CMP_LEN, HEAD_DIM), 0.1),
        "nsa_cmp_w1_v": nrm((NO, CMP_LEN, HEAD_DIM, CMP_HIDDEN), (CMP_LEN * HEAD_DIM) ** -0.5),
        "nsa_cmp_w2_v": nrm((NO, CMP_HIDDEN, HEAD_DIM), CMP_HIDDEN ** -0.5),
        "ffn_w_up": nrm((DEPTH, D_MODEL, 2 * D_FF), D_MODEL ** -0.5),
        "ffn_conv_w": nrm((DEPTH, CONV_WIDTH, D_FF), CONV_WIDTH ** -0.5),
        "ffn_conv_b": nrm((DEPTH, D_FF), 0.01),
        "ffn_w_down": nrm((DEPTH, D_FF, D_MODEL), D_FF ** -0.5),
    }


def reference(x, mix_pre_g, mix_post_g, ffn_pre_g, ffn_post_g, hy_w_in, hy_w_out, swa_sinks,
              rwkv_mu, rwkv_w0, rwkv_w2, rwkv_a0, rwkv_a2, rwkv_g2, rwkv_k_k, rwkv_k_a, rwkv_r_k,
              rwkv_ln_g, rwkv_ln_b, nsa_w_in, nsa_w_out, nsa_cmp_pos_k, nsa_cmp_w1_k, nsa_cmp_w2_k,
              nsa_cmp_pos_v, nsa_cmp_w1_v, nsa_cmp_w2_v, ffn_w_up, ffn_conv_w, ffn_conv_b, ffn_w_down):
    for layer in range(DEPTH):
        i = layer // 2
        h = rms_norm(x, mix_pre_g[layer])
        if layer % 2 == 0:
            m = hybrid_swa_rwkv(h, hy_w_in[i], hy_w_out[i], swa_sinks[i], rwkv_mu[i], rwkv_w0[i],
                                rwkv_w2[i], rwkv_a0[i], rwkv_a2[i], rwkv_g2[i], rwkv_k_k[i],
                                rwkv_k_a[i], rwkv_r_k[i], rwkv_ln_g[i], rwkv_ln_b[i])
        else:
            m = nsa_layer(h, nsa_w_in[i], nsa_w_out[i], nsa_cmp_pos_k[i], nsa_cmp_w1_k[i],
                          nsa_cmp_w2_k[i], nsa_cmp_pos_v[i], nsa_cmp_w1_v[i], nsa_cmp_w2_v[i])
        x = x + rms_norm(m, mix_post_g[layer])
        h = rms_norm(x, ffn_pre_g[layer])
        x = x + rms_norm(conv_ffn(h, ffn_w_up[layer], ffn_conv_w[layer], ffn_conv_b[layer], ffn_w_down[layer]),
                         ffn_post_g[layer])
    return x
```

```python
import contextlib
import numpy as np
import ml_dtypes
import concourse.bass as bass
import concourse.mybir as mybir
from concourse.bass_utils import run_bass_kernel_spmd

F32 = mybir.dt.float32
BF16 = mybir.dt.bfloat16
AF = mybir.ActivationFunctionType
ALU = mybir.AluOpType
AX = mybir.AxisListType
NPBF = ml_dtypes.bfloat16

D = 1024
TT = 512
NS = 4
HD = 64
DFF = 2816
NFC = 22
HY_COLS = 2560
NSA_COLS = 2608
EPS = 1e-6
GN_EPS = 64e-5


class Prog:
    def __init__(self, nc, n_dma_sems=16):
        self.nc = nc
        self.eng = {"pe": nc.tensor, "act": nc.scalar, "dve": nc.vector, "pool": nc.gpsimd, "sp": nc.sync}
        self.psem = {k: nc.alloc_semaphore(name=f"prog_{k}") for k in self.eng}
        self.cnt = {k: 0 for k in self.eng}
        self.waited = {k: {} for k in self.eng}
        self.dsems = {}
        for q in ("sp", "act", "pool"):
            self.dsems[q] = [[nc.alloc_semaphore(name=f"dma_{q}_{i}"), 0] for i in range(n_dma_sems)]
        self.dnext = {q: 0 for q in self.dsems}
        self.bufs = {}
        self.ninst = 0
        self.pe_base = 0
        self.psum_names = set()

    def _st(self, ap):
        nm = ap.tensor.name
        st = self.bufs.get(nm)
        if st is None:
            st = {"w": None, "r": {}}
            self.bufs[nm] = st
        return st

    def _wait(self, e, deps):
        eng = self.eng[e]
        best = {}
        for d in deps:
            if d is None:
                continue
            s, v, key = d
            if key == e and e == "pe":
                continue
            if best.get(key, (None, 0))[1] < v:
                best[key] = (s, v)
        for key, (s, v) in best.items():
            if self.waited[e].get(key, 0) < v:
                eng.wait_ge(s, v)
                self.waited[e][key] = v

    def _deps(self, reads, writes):
        deps = []
        for ap in reads:
            st = self._st(ap)
            deps.append(st["w"])
            if ap.tensor.name in self.psum_names:
                deps.extend(st["r"].values())
        for ap in writes:
            st = self._st(ap)
            deps.append(st["w"])
            deps.extend(st["r"].values())
        return deps

    def _commit(self, ev, reads, writes):
        for ap in reads:
            self._st(ap)["r"][ev[2]] = ev
        for ap in writes:
            st = self._st(ap)
            st["w"] = ev
            st["r"] = {}

    def op(self, e, fn, *args, **kw):
        writes = [args[0]]
        reads = [a for a in args[1:] if isinstance(a, bass.AP)]
        for k, v in kw.items():
            if isinstance(v, bass.AP):
                (writes if k == "accum_out" else reads).append(v)
        self._wait(e, self._deps(reads, writes))
        if e == "pe":
            base = (fn, args[1].start_partition(), args[0].start_partition(), args[1].shape[0], args[1].shape[-1] if len(args[1].shape) == 2 else -1, args[0].tensor.name, str(args[1].dtype))
            if base != self.pe_base and self.cnt["pe"] > 0:
                self.eng["pe"].wait_ge(self.psem["pe"], self.cnt["pe"])
            self.pe_base = base
        ins = getattr(self.eng[e], fn)(*args, **kw)
        self.cnt[e] += 1
        ins.then_inc(self.psem[e], 1)
        self._commit((self.psem[e], self.cnt[e], e), reads, writes)
        self.ninst += 1
        return ins

    def dma(self, q, out, in_, **kw):
        reads, writes = [in_], [out]
        deps = self._deps(reads, writes)
        slot = self.dnext[q]
        self.dnext[q] = (slot + 1) % len(self.dsems[q])
        ent = self.dsems[q][slot]
        key = f"d_{q}_{slot}"
        if ent[1] > 0:
            deps.append((ent[0], ent[1], key))
        self._wait(q, deps)
        ins = self.eng[q].dma_start(out=out, in_=in_, **kw)
        assert ent[1] + 16 <= 240, "DMA semaphore would exceed its usable range; add a barrier()"
        ent[1] += 16
        ins.then_inc(ent[0], 16)
        self._commit((ent[0], ent[1], key), reads, writes)
        self.ninst += 1
        return ins

    def barrier(self):
        deps = []
        for qq, lst in self.dsems.items():
            for i, ent in enumerate(lst):
                if ent[1] > 0:
                    deps.append((ent[0], ent[1], f"d_{qq}_{i}"))
        for k in self.eng:
            if self.cnt[k] > 0:
                deps.append((self.psem[k], self.cnt[k], k))
        for e in self.eng:
            self._wait(e, deps)
        used = any(ent[1] > 0 for lst in self.dsems.values() for ent in lst)
        if used:
            arr = []
            for e in self.eng:
                if e == "sp":
                    continue
                self.eng[e].sem_inc(self.psem[e], 1)
                self.cnt[e] += 1
                arr.append((self.psem[e], self.cnt[e], e))
            self._wait("sp", arr)
            for qq, lst in self.dsems.items():
                for i, ent in enumerate(lst):
                    if ent[1] > 0:
                        self.eng["sp"].sem_clear(ent[0])
                        ent[1] = 0
                    for e in self.eng:
                        self.waited[e].pop(f"d_{qq}_{i}", None)
            self.eng["sp"].sem_inc(self.psem["sp"], 1)
            self.cnt["sp"] += 1
            ev = (self.psem["sp"], self.cnt["sp"], "sp")
            for e in self.eng:
                if e != "sp":
                    self._wait(e, [ev])
        self.bufs = {}

    def mm(self, out, lhsT, rhs, start=True, stop=True, sgc=False):
        if sgc:
            return self.op("pe", "matmul", out, lhsT, rhs, start=start, stop=stop, skip_group_check=True)
        return self.op("pe", "matmul", out, lhsT, rhs, start=start, stop=stop)

    def tr(self, out, in_, ident):
        return self.op("pe", "transpose", out, in_, ident)

    def act(self, out, in_, func=AF.Copy, **kw):
        return self.op("act", "activation", out, in_, func, **kw)

    def tt(self, e, out, in0, in1, op):
        return self.op(e, "tensor_tensor", out, in0, in1, op)

    def ts(self, e, out, in0, s1, s2, op0, op1=None):
        if op1 is None:
            return self.op(e, "tensor_scalar", out, in0, s1, None, op0)
        return self.op(e, "tensor_scalar", out, in0, s1, s2, op0, op1)

    def stt(self, e, out, in0, scalar, in1, op0, op1):
        return self.op("dve", "scalar_tensor_tensor", out, in0, scalar, in1, op0, op1)

    def cp(self, e, out, in_):
        if e == "act":
            return self.act(out, in_, AF.Copy)
        return self.op(e, "tensor_copy", out, in_)


def bc(ap, shape):
    return ap.broadcast_to(list(shape))


class K:
    pass


def build(T=4096, NB=2, n_layers=2, debug=(), stop=None):
    nc = bass.Bass("TRN2", target_bir_lowering=False)
    p = Prog(nc)
    NT = T // TT
    k = K()
    k.nc, k.p, k.T, k.NB, k.NT = nc, p, T, NB, NT
    dbg = {}

    def din(name, shape, dt=F32):
        return nc.dram_tensor(name, list(shape), dt, kind="ExternalInput").ap()

    def dscr(name, shape, dt):
        return nc.dram_tensor(name, list(shape), dt, kind="Internal").ap()

    x = din("x", [NB, T, D])
    out = nc.dram_tensor("out", [NB, T, D], F32, kind="ExternalOutput").ap()
    gains = din("gains", [8, D])
    w_in0 = din("hy_w_in", [D, HY_COLS])
    w_out0 = din("hy_w_out", [D, D])
    w_up = din("ffn_w_up", [2, D, 2 * DFF])
    w_down = din("ffn_w_down", [2, DFF, D])
    ffn_pp = din("ffn_pp", [2, 128, NFC * 4])
    pp0 = din("pp0", [128, 42])
    sinks = din("swa_sinks", [1, 8])
    rw_w2 = din("rwkv_w2", [64, 512])
    rw_a2 = din("rwkv_a2", [64, 512])
    rw_g2 = din("rwkv_g2", [128, 512])
    c_identb = din("c_identb", [128, 128], BF16)
    c_identf = din("c_identf", [128, 128])
    c_onesblk = din("c_onesblk", [128, 128])
    c_mask01 = din("c_mask01", [128, TT])
    c_maskG = din("c_maskG", [128, 128])
    c_maskAT = din("c_maskAT", [128, 64])
    c_swa_q = din("c_swa_q", [2, 8 * TT], BF16)
    c_swa_ko = din("c_swa_ko", [2, TT], BF16)
    c_swa_kp = din("c_swa_kp", [2, 128], BF16)
    c_swa_mo = din("c_swa_mo", [128, 128], BF16)
    c_swa_mp = din("c_swa_mp", [128, 128], BF16)
    if n_layers > 1:
        nsa_w_in = din("nsa_w_in", [D, NSA_COLS])
        nsa_w_out = din("nsa_w_out", [D, D])
        cw1 = [din("cmp_w1_k", [2048, 256]), din("cmp_w1_v", [2048, 256])]
        cw2 = [din("cmp_w2_k", [256, 64]), din("cmp_w2_v", [256, 64])]
        cposT = din("cmp_posT", [128, 32])
        c_nsa_q = din("c_nsa_q", [5, 16 * T], BF16)
        c_nsa_k = din("c_nsa_k", [5, T], BF16)
        c_cmp_k = din("c_cmp_k", [5, 384], BF16)
        c_cmp_mask = din("c_cmp_mask", [3, 128, T], BF16)
        c_mpool = din("c_mpool", [128, 3 * 64], BF16)
        c_sel_F = din("c_sel_F", [T // 128, 128, 64])
        c_E = din("c_E", [64, T], BF16)
        nsa_w_inb = dscr("nsa_w_inb", [D, NSA_COLS], BF16)
        nsa_w_outb = dscr("nsa_w_outb", [D, D], BF16)
        cw1b = [dscr("cmp_w1_kb", [2048, 256], BF16), dscr("cmp_w1_vb", [2048, 256], BF16)]

    w_in0b = dscr("w_in0b", [D, HY_COLS], BF16)
    w_out0b = dscr("w_out0b", [D, D], BF16)
    w_upb = dscr("w_upb", [2, D, 2 * DFF], BF16)
    w_downb = dscr("w_downb", [2, DFF, D], BF16)
    x1 = dscr("x1", [NB, T, D], F32)

    for nm in debug:
        dbg[nm] = None
    k.dbg_out = {}

    def dbg_dump(name, shape, src_ap, dt=F32):
        if name not in debug:
            return
        if name not in k.dbg_out:
            k.dbg_out[name] = nc.dram_tensor("dbg_" + name, list(shape), dt, kind="ExternalOutput").ap()
        return k.dbg_out[name]

    es = contextlib.ExitStack()

    uid = [0]

    def sb(name, shape, dt, stack=None):
        uid[0] += 1
        return (stack or es).enter_context(nc.sbuf_tensor(f"{name}_{uid[0]}", list(shape), dt))

    def ps(name, shape, dt=F32, stack=None):
        uid[0] += 1
        p.psum_names.add(f"{name}_{uid[0]}")
        return (stack or es).enter_context(nc.psum_tensor(f"{name}_{uid[0]}", list(shape), dt))

    with contextlib.ExitStack() as stc:
        cin = [sb(f"cast_in{i}", [128, 2 * DFF], F32, stc) for i in range(2)]
        cout = [sb(f"cast_out{i}", [128, 2 * DFF], BF16, stc) for i in range(2)]
        ccnt = [0]

        def cast_w(dst, src, rows, cols):
            for r0 in range(0, rows, 128):
                i = ccnt[0] % 2
                ccnt[0] += 1
                p.dma("sp", cin[i][:, 0:cols], src[r0:r0 + 128, :])
                h = cols // 2
                p.cp("dve", cout[i][:, 0:h], cin[i][:, 0:h])
                p.cp("pool", cout[i][:, h:cols], cin[i][:, h:cols])
                p.dma("act", dst[r0:r0 + 128, :], cout[i][:, 0:cols])

        cast_w(w_in0b, w_in0, D, HY_COLS)
        cast_w(w_out0b, w_out0, D, D)
        for l in range(n_layers):
            cast_w(w_upb[l], w_up[l], D, 2 * DFF)
            cast_w(w_downb[l], w_down[l], DFF, D)
        p.barrier()
        if n_layers > 1:
            cast_w(nsa_w_inb, nsa_w_in, D, NSA_COLS)
            cast_w(nsa_w_outb, nsa_w_out, D, D)
            for kv in range(2):
                cast_w(cw1b[kv], cw1[kv], 2048, 256)
            p.barrier()
    k.stop = stop

    class Stop(Exception):
        pass

    def maybe_stop(tag, src_ap=None):
        if k.stop == tag:
            if src_ap is not None:
                p.dma("sp", out[0, 0:TT, :].rearrange("(s p) d -> p s d", p=128), src_ap)
            p.barrier()
            raise Stop()

    identb = sb("identb", [128, 128], BF16)
    identf = sb("identf", [128, 128], F32)
    onesblk = sb("onesblk", [128, 128], F32)
    mask01 = sb("mask01", [128, TT], F32)
    maskG = sb("maskG", [128, 128], F32)
    maskAT = sb("maskAT", [128, 64], F32)
    mo = sb("swa_mo", [128, 128], BF16)
    mp = sb("swa_mp", [128, 128], BF16)
    gbc = sb("gbc", [128, 4, D], F32)
    ppt = sb("ppt", [128, 42], F32)
    omu = sb("omu", [128, 14], F32)
    omka = sb("omka", [128, 4], F32)
    fpp = sb("fpp", [128, NFC * 4], F32)
    esink = sb("esink", [128, 8], F32)
    for dst, src in ((identb, c_identb), (identf, c_identf), (onesblk, c_onesblk), (mask01, c_mask01),
                     (maskG, c_maskG), (maskAT, c_maskAT), (mo, c_swa_mo), (mp, c_swa_mp), (ppt, pp0)):
        p.dma("sp", dst[:], src)
    p.dma("sp", esink[:], sinks[0].partition_broadcast(128))
    p.act(esink[:], esink[:], AF.Exp)
    p.ts("dve", omu[:], ppt[:, 0:14], -1.0, 1.0, ALU.mult, ALU.add)
    p.ts("dve", omka[:], ppt[:, 26:30], -1.0, 1.0, ALU.mult, ALU.add)
    MU, W0, A0, KK, KA, RK, LG, LB = 0, 14, 18, 22, 26, 30, 34, 38

    w2b = sb("w2b", [128, 512], BF16)
    g2b = sb("g2b", [128, 512], BF16)
    with contextlib.ExitStack() as st0:
        tmpf = sb("tmpf_l", [128, 512], F32, st0)
        p.dma("sp", tmpf[0:64, :], rw_w2)
        p.dma("sp", tmpf[64:128, :], rw_a2)
        p.cp("dve", w2b[:], tmpf[:])
        tmpg = sb("tmpg_l", [128, 512], F32, st0)
        p.dma("sp", tmpg[:], rw_g2)
        p.cp("dve", g2b[:], tmpg[:])
        p.barrier()

    def load_gains(layer):
        for i in range(4):
            p.dma("sp", gbc[:, i, :], gains[layer * 4 + i].partition_broadcast(128))

    def rmsnorm_to_hT(src, gi, hT, stk, pst):
        junk = sb("nrm_junk", [128, D], BF16, stk)
        ss = sb("nrm_ss", [128, NS], F32, stk)
        hb = sb("nrm_hb", [128, NS, D], BF16, stk)
        for s in range(NS):
            p.act(junk[:], src[:, s, :], AF.Square, accum_out=ss[:, s:s + 1])
        p.ts("dve", ss[:], ss[:], 1.0 / D, EPS, ALU.mult, ALU.add)
        p.act(ss[:], ss[:], AF.Sqrt)
        p.op("dve", "reciprocal", ss[:], ss[:])
        for s in range(NS):
            p.stt("dve" if s % 2 == 0 else "pool", hb[:, s, :], src[:, s, :], ss[:, s:s + 1], gbc[:, gi, :],
                  ALU.mult, ALU.mult)
        for kc in range(8):
            pt = pst[kc % 2]
            for s in range(NS):
                p.tr(pt[:, s * 128:(s + 1) * 128], hb[:, s, kc * 128:(kc + 1) * 128], identb[:])
            p.cp("act" if kc % 2 == 0 else "dve", hT[:, kc, :], pt[:, 0:TT])

    def post_norm_residual(pm, xres, gi, s, stk_tiles):
        junk, ss1, tmp = stk_tiles
        p.act(junk[:], pm, AF.Square, accum_out=ss1[:, s:s + 1])
        p.ts("dve", ss1[:, s:s + 1], ss1[:, s:s + 1], 1.0 / D, EPS, ALU.mult, ALU.add)
        p.act(ss1[:, s:s + 1], ss1[:, s:s + 1], AF.Sqrt)
        p.op("dve", "reciprocal", ss1[:, s:s + 1], ss1[:, s:s + 1])
        p.stt("dve", tmp[:], pm, ss1[:, s:s + 1], gbc[:, gi, :], ALU.mult, ALU.mult)
        p.tt("pool", xres[:, s, :], xres[:, s, :], tmp[:], ALU.add)

    def ffn(layer, xres, gcarry, first_tile):
        with contextlib.ExitStack() as stk:
            aT = sb("f_aT", [128, NFC, TT], BF16, stk)
            with contextlib.ExitStack() as stk2:
                pst = [ps(f"f_pt{i}", [128, 2 * TT], BF16, stk2) for i in range(2)]
                hT = sb("f_hT", [128, 8, TT], BF16, stk2)
                rmsnorm_to_hT(xres, 2, hT, stk2, pst)
                k.maybe_stop("f1")
                wu = [sb(f"f_wu{i}", [128, 8, 256], BF16, stk2) for i in range(3)]
                gb = [sb(f"f_gb{i}", [128, TT + 2], F32, stk2) for i in range(2)]
                cv = [sb(f"f_cv{i}", [128, TT], F32, stk2) for i in range(2)]
                ge = [sb(f"f_ge{i}", [128, TT], F32, stk2) for i in range(2)]
                pg = [ps(f"f_pg{i}", [128, TT], F32, stk2) for i in range(2)]
                pv = [ps(f"f_pv{i}", [128, TT], F32, stk2) for i in range(2)]
                wup = w_upb[layer].rearrange("(kc p) c -> p kc c", p=128)
                for fc in range(NFC):
                    w = wu[fc % 3]
                    p.dma("sp", w[:, :, 0:128], wup[:, :, fc * 128:(fc + 1) * 128])
                    p.dma("sp", w[:, :, 128:256], wup[:, :, DFF + fc * 128:DFF + (fc + 1) * 128])
                    g_, v_, b_, c_, e_ = pg[fc % 2], pv[fc % 2], gb[fc % 2], cv[fc % 2], ge[fc % 2]
                    for kc in range(8):
                        p.mm(g_[:], w[:, kc, 0:128], hT[:, kc, :], start=(kc == 0), stop=(kc == 7))
                    for kc in range(8):
                        p.mm(v_[:], w[:, kc, 128:256], hT[:, kc, :], start=(kc == 0), stop=(kc == 7))
                    if first_tile:
                        p.op("pool", "memset", b_[:, 0:2], 0.0)
                    else:
                        p.cp("pool", b_[:, 0:2], gcarry[:, fc, :])
                    p.cp("dve", b_[:, 2:TT + 2], g_[:])
                    p.cp("pool", gcarry[:, fc, :], b_[:, TT:TT + 2])
                    pc = fpp[:, fc * 4:fc * 4 + 4]
                    p.ts("dve", c_[:], g_[:], pc[:, 2:3], pc[:, 3:4], ALU.mult, ALU.add)
                    p.stt("dve", c_[:], b_[:, 1:TT + 1], pc[:, 1:2], c_[:], ALU.mult, ALU.add)
                    p.stt("pool", c_[:], b_[:, 0:TT], pc[:, 0:1], c_[:], ALU.mult, ALU.add)
                    p.act(e_[:], c_[:], AF.Gelu_apprx_tanh)
                    p.tt("dve", aT[:, fc, :], e_[:], v_[:], ALU.mult)
                    k.maybe_stop("f2")
                p.barrier()
            pacc = [ps(f"f_pa{i}", [128, 2, 512], F32, stk) for i in range(2)]
            k.maybe_stop("f3")
            wd = [sb(f"f_wd{i}", [128, 2, D], BF16, stk) for i in range(3)]
            junk = sb("f_junk", [128, D], BF16, stk)
            ss1 = sb("f_ss1", [128, NS], F32, stk)
            tmp = sb("f_tmp", [128, D], F32, stk)
            wdn = w_downb[layer].rearrange("(fc p) c -> p fc c", p=128)
            wi = 0
            for half in range(2):
                for f0 in range(0, NFC, 2):
                    w = wd[wi % 3]
                    wi += 1
                    p.dma("sp", w[:], wdn[:, f0:f0 + 2, :])
                    for ff in range(2):
                        fc = f0 + ff
                        for si in range(2):
                            s = half * 2 + si
                            for nh in range(2):
                                p.mm(pacc[si][:, nh, :], aT[:, fc, s * 128:(s + 1) * 128], w[:, ff, nh * 512:(nh + 1) * 512],
                                     start=(fc == 0), stop=(fc == NFC - 1))
                for si in range(2):
                    post_norm_residual(pacc[si][:], xres, 3, half * 2 + si, (junk, ss1, tmp))
                    k.maybe_stop("f4")
            p.barrier()

    def layer0():
        with contextlib.ExitStack() as L0:
            layer0_body(L0)
            p.barrier()

    def layer0_body(L0):
        k.zs = [sb(f"zs{j}", [128, TT], F32, L0) for j in range(14)]
        load_gains(0)
        p.dma("sp", fpp[:], ffn_pp[0])
        KaO = sb("KaO", [66, 2, TT], BF16, L0)
        KaP = sb("KaP", [66, 2, TT + 128], BF16, L0)
        Vsw = sb("Vsw", [128, NS + 1, 2, 65], BF16, L0)
        for g in range(2):
            p.dma("sp", KaO[64:66, g, :], c_swa_ko)
            for s in range(NS + 1):
                p.dma("sp", KaP[64:66, g, s * 128:(s + 1) * 128], c_swa_kp)
        p.op("pool", "memset", Vsw[:, :, :, 64:65], 1.0)
        zcarry = sb("zcarry", [128, 14], F32, L0)
        ST = sb("ST", [128, 4, 64], F32, L0)
        gcarry = sb("gcarry", [128, NFC, 2], F32, L0)
        oT = sb("oT", [128, 8, TT], BF16, L0)
        xt = sb("xt", [128, NS, D], F32, L0)
        win = w_in0b.rearrange("(kc p) c -> p kc c", p=128)

        for b in range(NB):
            for ti in range(NT):
                first = (ti == 0)
                t0 = ti * TT
                with contextlib.ExitStack() as stk:
                    pst = [ps(f"p1_pt{i}", [128, 2 * TT], BF16, stk) for i in range(2)]
                    pz = [ps(f"p1_pz{i}", [128, TT], F32, stk) for i in range(2)]
                    Qa = sb("Qa", [66, 8, TT], BF16, stk)
                    p.dma("sp", Qa[64:66, :, :], c_swa_q.rearrange("r (h t) -> r h t", h=8))
                    p.dma("sp", xt[:], x[b, t0:t0 + TT, :].rearrange("(s p) d -> p s d", p=128))
                    hT = sb("p1_hT", [128, 8, TT], BF16, stk)
                    rmsnorm_to_hT(xt, 0, hT, stk, pst)
                    maybe_stop("p1a", xt[:])
                    wq = [sb(f"p1_w{i}", [128, 8, 128], BF16, stk) for i in range(4)]
                    wcnt = [0]

                    def wload(c0, n):
                        w = wq[wcnt[0] % 4]
                        wcnt[0] += 1
                        p.dma("sp", w[:, :, 0:n], win[:, :, c0:c0 + n])
                        return w

                    def proj_fm(w, c0, n, dst_ps):
                        for kc in range(8):
                            p.mm(dst_ps, w[:, kc, c0:c0 + n], hT[:, kc, :], start=(kc == 0), stop=(kc == 7))

                    for hp in range(4):
                        w = wload(hp * 128, 128)
                        for hh in range(2):
                            h = hp * 2 + hh
                            z = pz[h % 2]
                            proj_fm(w, hh * 64, 64, z[0:64, :])
                            p.act(Qa[0:64, h, :], z[0:64, :], AF.Copy, scale=0.125)
                    maybe_stop("p1b1", xt[:])
                    if not first:
                        for g in range(2):
                            p.cp("pool", KaP[0:64, g, 0:128], KaP[0:64, g, TT:TT + 128])
                        p.cp("pool", Vsw[:, 0, :, 0:64], Vsw[:, NS, :, 0:64])
                    w = wload(512, 128)
                    for g in range(2):
                        z = pz[g % 2]
                        proj_fm(w, g * 64, 64, z[0:64, :])
                        p.cp("dve", KaO[0:64, g, :], z[0:64, :])
                        p.cp("dve", KaP[0:64, g, 128:128 + TT], z[0:64, :])
                    maybe_stop("p1b2", xt[:])
                    w = wload(640, 128)
                    for s in range(NS):
                        z = pz[s % 2]
                        for kc in range(8):
                            p.mm(z[:, 0:128], hT[:, kc, s * 128:(s + 1) * 128], w[:, kc, :], start=(kc == 0), stop=(kc == 7))
                        p.cp("dve", Vsw[:, s + 1, :, 0:64], z[:, 0:128].rearrange("p (g d) -> p g d", g=2))
                    maybe_stop("p1b", xt[:])
                    zs = k.zs
                    zb = [sb(f"p1_zb{i}", [128, TT + 1], F32, stk) for i in range(2)]
                    ztmp = [sb(f"p1_zt{i}", [128, TT], F32, stk) for i in range(2)]
                    for j in range(14):
                        w = wload(768 + j * 128, 128)
                        z = pz[j % 2]
                        proj_fm(w, 0, 128, z[:])
                        b_ = zb[j % 2]
                        tm = ztmp[j % 2]
                        p.act(b_[:, 1:TT + 1], z[:], AF.Copy)
                        if first:
                            p.op("pool", "memset", b_[:, 0:1], 0.0)
                        else:
                            p.cp("pool", b_[:, 0:1], zcarry[:, j:j + 1])
                        p.cp("pool", zcarry[:, j:j + 1], b_[:, TT:TT + 1])
                        p.ts("dve", tm[:], b_[:, 0:TT], ppt[:, MU + j:MU + j + 1], None, ALU.mult)
                        p.stt("dve", zs[j][:], b_[:, 1:TT + 1], omu[:, j:j + 1], tm[:], ALU.mult, ALU.add)
                    maybe_stop("p1c", xt[:])
                    pss = [ps(f"p1_ps{i}", [128, 4, 128], F32, stk) for i in range(2)]
                    po = [ps(f"p1_po{i}", [128, 4, 65], F32, stk) for i in range(2)]
                    pe_ = [sb(f"p1_pe{i}", [128, 4, 128], BF16, stk) for i in range(3)]
                    oa = sb("p1_oa", [128, 8, 64], BF16, stk)
                    den = sb("p1_den", [128, 8], F32, stk)
                    ecnt = 0
                    for s in range(NS):
                        has_prev = not (first and s == 0)
                        for g in range(2):
                            pog = po[g]
                            steps = ([(KaP[:, g, s * 128:(s + 1) * 128], Vsw[:, s, g, :], mp)] if has_prev else []) + \
                                    [(KaO[:, g, s * 128:(s + 1) * 128], Vsw[:, s + 1, g, :], mo)]
                            for si, (kap, vap, msk) in enumerate(steps):
                                sc = pss[ecnt % 2]
                                pt_ = pe_[ecnt % 3]
                                ecnt += 1
                                p.mm(sc[:], kap, Qa[:, g * 4:(g + 1) * 4, s * 128:(s + 1) * 128])
                                p.act(pt_[:], sc[:], AF.Exp)
                                p.tt("dve", pt_[:], pt_[:], bc(msk[:].unsqueeze(1), [128, 4, 128]), ALU.mult)
                                for hh in range(4):
                                    p.mm(pog[:, hh, :], pt_[:, hh, :], vap, start=(si == 0 and hh == 0), stop=(si == len(steps) - 1), sgc=True)
                            p.tt("dve", den[:, g * 4:(g + 1) * 4], pog[:, :, 64], esink[:, g * 4:(g + 1) * 4], ALU.add)
                            p.op("dve", "reciprocal", den[:, g * 4:(g + 1) * 4], den[:, g * 4:(g + 1) * 4])
                            p.tt("dve", oa[:, g * 4:(g + 1) * 4, :], pog[:, :, 0:64],
                                 bc(den[:, g * 4:(g + 1) * 4].unsqueeze(2), [128, 4, 64]), ALU.mult)
                        pt = pst[s % 2]
                        for kc in range(4):
                            p.tr(pt[:, kc * 128:(kc + 1) * 128], oa[:, 2 * kc:2 * kc + 2, :].rearrange("p h d -> p (h d)"), identb[:])
                        p.cp("act", oT[:, 0:4, s * 128:(s + 1) * 128], pt[:, 0:512].rearrange("p (c t) -> p c t", c=4))
                    p.barrier()
                maybe_stop("p1", xt[:])
                for hg in range(2):
                    with contextlib.ExitStack() as stk:
                        rwkv_tile(first, oT, ST, stk, hg)
                        p.barrier()
                maybe_stop("p2", xt[:])
                with contextlib.ExitStack() as stk:
                    pacc = [ps(f"p3_pa{i}", [128, 2, 512], F32, stk) for i in range(2)]
                    wout = sb("wout", [128, 8, D], BF16, stk)
                    p.dma("sp", wout[:], w_out0b.rearrange("(kc p) c -> p kc c", p=128))
                    junk = sb("p3_junk", [128, D], BF16, stk)
                    ss1 = sb("p3_ss1", [128, NS], F32, stk)
                    tmp = sb("p3_tmp", [128, D], F32, stk)
                    for s in range(NS):
                        pm = pacc[s % 2]
                        for nh in range(2):
                            for kc in range(8):
                                p.mm(pm[:, nh, :], oT[:, kc, s * 128:(s + 1) * 128], wout[:, kc, nh * 512:(nh + 1) * 512],
                                     start=(kc == 0), stop=(kc == 7))
                        post_norm_residual(pm[:], xt, 1, s, (junk, ss1, tmp))
                    p.barrier()
                maybe_stop("p3", xt[:])
                ffn(0, xt, gcarry, first)
                dst = x1 if n_layers > 1 else out
                p.dma("sp", dst[b, t0:t0 + TT, :].rearrange("(s p) d -> p s d", p=128), xt[:])
                p.barrier()


    def layer1():
        with contextlib.ExitStack() as L1:
            layer1_body(L1)
            p.barrier()

    def layer1_body(L1):
        load_gains(1)
        p.dma("sp", fpp[:], ffn_pp[1])
        NKT = T // 128
        xt = sb("n_xt", [128, NS, D], F32, L1)
        oT = sb("n_oT", [128, 8, TT], BF16, L1)
        gcarry = sb("n_gcarry", [128, NFC, 2], F32, L1)
        Ks = sb("n_Ks", [69, 4, T], BF16, L1)
        Vs = sb("n_Vs", [128, NKT, 4, 65], BF16, L1)
        Kw = sb("n_Kw", [69, 4, 6 * 128], BF16, L1)
        Vw = sb("n_Vw", [128, 6, 4, 65], BF16, L1)
        Kc = sb("n_Kc", [69, 4, 3, 128], BF16, L1)
        Vc = sb("n_Vc", [128, 3, 4, 129], BF16, L1)
        KC2 = sb("n_KC2", [128, 8, 16 + TT], BF16, L1)
        Ebl = sb("n_E", [64, T], BF16, L1)
        hbias = sb("n_hbias", [128, 4], F32, L1)
        w2c = sb("n_w2c", [128, 2, 2, 64], BF16, L1)
        p.dma("sp", Ebl[:], c_E)
        p.op("pool", "memset", Vs[:, :, :, 64:65], 1.0)
        p.op("pool", "memset", Vw[:, :, :, 64:65], 1.0)
        p.op("pool", "memset", Kc[:], 0.0)
        p.op("pool", "memset", Vc[:], 0.0)
        for g in range(4):
            p.dma("sp", Ks[64:69, g, :], c_nsa_k)
            p.dma("sp", Kc[64:69, g, :, :], c_cmp_k.rearrange("r (c s) -> r c s", c=3))
            for ch in range(3):
                p.dma("sp", Vc[:, ch, g, 65:129], c_mpool[:, ch * 64:(ch + 1) * 64])
        p.op("pool", "memset", Vc[:, :, :, 64:65], 1.0)
        win = nsa_w_inb.rearrange("(kc p) c -> p kc c", p=128)
        with contextlib.ExitStack() as stk:
            w1t = sb("n_w1t", [128, 16, 256], BF16, stk)
            posf = sb("n_posf", [128, 32], F32, stk)
            posb = sb("n_posb", [128, 32], BF16, stk)
            w2f = sb("n_w2f", [128, 2, 2, 64], F32, stk)
            pb_ = ps("n_pb", [128, 8], F32, stk)
            p.dma("sp", posf[:], cposT)
            p.cp("dve", posb[:], posf[:])
            for kv in range(2):
                p.dma("sp", w2f[:, kv, :, :], cw2[kv].rearrange("(c p) d -> p c d", p=128))
            p.cp("dve", w2c[:], w2f[:])
            for kv in range(2):
                p.dma("sp", w1t[:], cw1b[kv].rearrange("(m p) f -> p m f", p=128))
                for fx in range(2):
                    for m in range(16):
                        p.mm(pb_[:, kv * 2 + fx:kv * 2 + fx + 1], w1t[:, m, fx * 128:(fx + 1) * 128],
                             posb[:, kv * 16 + m:kv * 16 + m + 1], start=(m == 0), stop=(m == 15))
            p.cp("dve", hbias[:], pb_[:, 0:4])
            p.barrier()

        for b in range(NB):
            for ti in range(NT):
                first = (ti == 0)
                t0 = ti * TT
                with contextlib.ExitStack() as stkQ:
                    Qa = sb("n_Qa", [69, 16, TT], BF16, stkQ)
                    gates = sb("n_gates", [128, NS, 48], F32, stkQ)
                    with contextlib.ExitStack() as stk:
                        pst = [ps(f"n1_pt{i}", [128, 2 * TT], BF16, stk) for i in range(2)]
                        pz = [ps(f"n1_pz{i}", [128, TT], F32, stk) for i in range(2)]
                        p.dma("sp", Qa[64:69, :, :], c_nsa_q.rearrange("r (h t) -> r h t", h=16)[:, :, t0:t0 + TT])
                        p.dma("sp", xt[:], x1[b, t0:t0 + TT, :].rearrange("(s p) d -> p s d", p=128))
                        hT = sb("n1_hT", [128, 8, TT], BF16, stk)
                        rmsnorm_to_hT(xt, 0, hT, stk, pst)
                        wq = [sb(f"n1_w{i}", [128, 8, 128], BF16, stk) for i in range(4)]
                        wcnt = [0]

                        def wload(cols):
                            w = wq[wcnt[0] % 4]
                            wcnt[0] += 1
                            o = 0
                            for c0, n in cols:
                                p.dma("sp", w[:, :, o:o + n], win[:, :, c0:c0 + n])
                                o += n
                            return w

                        def proj_fm(w, c0, n, dst_ps):
                            for kc in range(8):
                                p.mm(dst_ps, w[:, kc, c0:c0 + n], hT[:, kc, :], start=(kc == 0), stop=(kc == 7))

                        zi = 0
                        for hp in range(8):
                            w = wload([(hp * 128, 128)])
                            for hh in range(2):
                                z = pz[zi % 2]; zi += 1
                                proj_fm(w, hh * 64, 64, z[0:64, :])
                                p.act(Qa[0:64, hp * 2 + hh, :], z[0:64, :], AF.Copy, scale=0.125)
                        for kv in range(2):
                            for g in range(4):
                                c0 = 1024 + kv * 256 + g * 64
                                w = wload([(c0, 64), (c0, 64)])
                                z = pz[zi % 2]; zi += 1
                                proj_fm(w, 0, 128, z[:])
                                kt_ = KC2[:, kv * 4 + g, :]
                                if first:
                                    p.op("pool", "memset", kt_[:, 0:16], 0.0)
                                else:
                                    p.cp("pool", kt_[0:64, 0:16], kt_[0:64, TT:TT + 16])
                                    p.cp("pool", kt_[64:128, 0:15], kt_[64:128, TT:TT + 15])
                                p.cp("dve", kt_[0:64, 16:16 + TT], z[0:64, :])
                                p.cp("dve", kt_[64:128, 15:15 + TT], z[64:128, :])
                        for which, dstK, cbase in (("s", Ks, 1536), ("w", Kw, 2048)):
                            if which == "w":
                                if not first:
                                    p.cp("pool", Kw[:, :, 0:256], Kw[:, :, 512:768])
                                    p.cp("pool", Vw[:, 0:2, :, 0:64], Vw[:, 4:6, :, 0:64])
                                for g in range(4):
                                    p.dma("sp", Kw[64:69, g, 256:768], c_nsa_k[:, t0:t0 + TT])
                                    if not first:
                                        p.dma("sp", Kw[64:69, g, 0:256], c_nsa_k[:, t0 - 256:t0])
                            for gp in range(2):
                                w = wload([(cbase + gp * 128, 128)])
                                for gg in range(2):
                                    g = gp * 2 + gg
                                    z = pz[zi % 2]; zi += 1
                                    proj_fm(w, gg * 64, 64, z[0:64, :])
                                    if which == "s":
                                        p.cp("dve", Ks[0:64, g, t0:t0 + TT], z[0:64, :])
                                    else:
                                        p.cp("dve", Kw[0:64, g, 256:768], z[0:64, :])
                        for which, cbase in (("s", 1792), ("w", 2304)):
                            wa = wload([(cbase, 128)])
                            wb_ = wload([(cbase + 128, 128)])
                            for s in range(NS):
                                z = pz[zi % 2]; zi += 1
                                for half, w in enumerate((wa, wb_)):
                                    for kc in range(8):
                                        p.mm(z[:, half * 128:(half + 1) * 128], hT[:, kc, s * 128:(s + 1) * 128], w[:, kc, :],
                                             start=(kc == 0), stop=(kc == 7))
                                src_ = z[:, 0:256].rearrange("p (g d) -> p g d", g=4)
                                if which == "s":
                                    p.cp("dve", Vs[:, ti * NS + s, :, 0:64], src_)
                                else:
                                    p.cp("dve", Vw[:, 2 + s, :, 0:64], src_)
                        w = wload([(2560, 48)])
                        for s in range(NS):
                            z = pz[zi % 2]; zi += 1
                            for kc in range(8):
                                p.mm(z[:, 0:48], hT[:, kc, s * 128:(s + 1) * 128], w[:, kc, 0:48], start=(kc == 0), stop=(kc == 7))
                            p.act(gates[:, s, :], z[:, 0:48], AF.Sigmoid)
                        p.barrier()
                    with contextlib.ExitStack() as stk:
                        w1t = sb("nc_w1t", [128, 16, 256], BF16, stk)
                        hid = sb("nc_hid", [128, 2, 32], BF16, stk)
                        ph = [ps(f"nc_ph{i}", [128, 32], F32, stk) for i in range(2)]
                        pk = ps("nc_pk", [128, 64], F32, stk)
                        chn, q3 = ti // 3, ti % 3
                        for kv in range(2):
                            p.dma("sp", w1t[:], cw1b[kv].rearrange("(m p) f -> p m f", p=128))
                            for g in range(4):
                                kt_ = KC2[:, kv * 4 + g, :]
                                for fx in range(2):
                                    for m in range(16):
                                        if 2 * m < 16:
                                            rhs = kt_[:, 0:TT].rearrange("p (j s) -> p j s", s=16)[:, :, 2 * m]
                                        else:
                                            rhs = kt_[:, 16:16 + TT].rearrange("p (j s) -> p j s", s=16)[:, :, 2 * m - 16]
                                        p.mm(ph[fx][:], w1t[:, m, fx * 128:(fx + 1) * 128], rhs, start=(m == 0), stop=(m == 15))
                                    p.act(hid[:, fx, :], ph[fx][:], AF.Gelu_apprx_tanh, bias=hbias[:, kv * 2 + fx:kv * 2 + fx + 1])
                                if kv == 0:
                                    for fx in range(2):
                                        p.mm(pk[0:64, 0:32], w2c[:, 0, fx, :], hid[:, fx, :], start=(fx == 0), stop=(fx == 1))
                                    p.cp("dve", Kc[0:64, g, chn, q3 * 32:(q3 + 1) * 32], pk[0:64, 0:32])
                                else:
                                    P32 = slice(q3 * 32, (q3 + 1) * 32)
                                    for fx in range(2):
                                        p.mm(pk[P32, 0:64], hid[:, fx, :], w2c[:, 1, fx, :], start=(fx == 0), stop=(fx == 1))
                                    p.cp("dve", Vc[P32, chn, g, 0:64], pk[P32, 0:64])
                        p.barrier()
                    with contextlib.ExitStack() as stk:
                        pss = [ps(f"n2_ps{i}", [128, 4, 128], F32, stk) for i in range(2)]
                        pm = ps("n2_pm", [128, 4, 128], F32, stk)
                        poA = ps("n2_poA", [128, 4, 65], F32, stk)
                        poB = ps("n2_poB", [128, 4, 64], F32, stk)
                        pos_ = ps("n2_pos", [128, 4, 65], F32, stk)
                        pow_ = ps("n2_pow", [128, 4, 65], F32, stk)
                        ptt = ps("n2_pt", [128, 2 * TT], BF16, stk)
                        pe_ = [sb(f"n2_pe{i}", [128, 4, 128], BF16, stk) for i in range(3)]
                        cm = [sb(f"n2_cm{i}", [128, 128], BF16, stk) for i in range(3)]
                        Ft = sb("n2_F", [128, 64], F32, stk)
                        imp = sb("n2_imp", [128, 64], F32, stk)
                        m8 = sb("n2_m8", [128, 8], F32, stk)
                        sel = sb("n2_sel", [128, 64], BF16, stk)
                        selT = sb("n2_selT", [64, 128], BF16, stk)
                        mdg = sb("n2_mdg", [128, 128], BF16, stk)
                        rc = sb("n2_rc", [128, 3, 4], F32, stk)
                        cf = sb("n2_cf", [128, 3, 4], F32, stk)
                        ot = sb("n2_ot", [128, 16, 64], BF16, stk)
                        o1 = sb("n2_o1", [128, 4, 64], F32, stk)
                        o2 = sb("n2_o2", [128, 4, 64], F32, stk)
                        ec = [0]

                        def attn_step(kap, qap, masks, pv_list, first_step, last_step, clamp=False):
                            sc = pss[ec[0] % 2]
                            pt_ = pe_[ec[0] % 3]
                            ec[0] += 1
                            p.mm(sc[:], kap, qap)
                            if clamp:
                                p.ts("dve", sc[:], sc[:], 60.0, None, ALU.min)
                            p.act(pt_[:], sc[:], AF.Exp)
                            for msk in masks:
                                p.tt("dve", pt_[:], pt_[:], bc(msk.unsqueeze(1), [128, 4, 128]), ALU.mult)
                            for (po_, wdt, rows, rhs) in pv_list:
                                for hh in range(4):
                                    p.mm(po_[:, hh, 0:wdt], pt_[0:rows, hh, :], rhs, start=(first_step and hh == 0), stop=last_step, sgc=True)

                        for s in range(NS):
                            qt = ti * NS + s
                            qs = slice(s * 128, (s + 1) * 128)
                            p.dma("sp", Ft[:], c_sel_F[qt])
                            for g in range(4):
                                qap = Qa[:, g * 4:(g + 1) * 4, qs]
                                chmax = ((8 * qt + 7) // 32) // 3
                                for ch in range(chmax + 1):
                                    cmk = cm[(ec[0]) % 3]
                                    p.dma("sp", cmk[:], c_cmp_mask[ch, :, qt * 128:(qt + 1) * 128])
                                    attn_step(Kc[:, g, ch, :], qap, [cmk[:]],
                                              [(poA, 65, 96, Vc[0:96, ch, g, 0:65]), (poB, 64, 96, Vc[0:96, ch, g, 65:129])],
                                              ch == 0, ch == chmax, clamp=True)
                                p.ts("dve", rc[:, 0, :], poA[:, :, 64], 1e-30, None, ALU.max)
                                p.op("dve", "reciprocal", rc[:, 0, :], rc[:, 0, :])
                                p.ts("dve", imp[:], poB[:, 0, :], rc[:, 0, 0:1], None, ALU.mult)
                                for hh in range(1, 4):
                                    p.stt("dve", imp[:], poB[:, hh, :], rc[:, 0, hh:hh + 1], imp[:], ALU.mult, ALU.add)
                                p.tt("dve", imp[:], imp[:], Ft[:], ALU.add)
                                p.op("dve", "max", m8[:], imp[:])
                                p.ts("dve", sel[:], imp[:], m8[:, 7:8], None, ALU.is_ge)
                                p.tr(ptt[0:64, 0:128], sel[:], identb[:])
                                p.cp("dve", selT[:], ptt[0:64, 0:128])
                                for kt in range(qt + 1):
                                    mslot = pm[:, kt % 4, :]
                                    p.mm(mslot, Ebl[:, kt * 128:(kt + 1) * 128], selT[:])
                                    if kt == qt:
                                        p.tt("dve", mdg[:], mslot, mo[:], ALU.mult)
                                        msk = mdg[:]
                                    else:
                                        msk = mslot
                                    attn_step(Ks[:, g, kt * 128:(kt + 1) * 128], qap, [msk],
                                              [(pos_, 65, 128, Vs[:, kt, g, :])], kt == 0, kt == qt, clamp=(kt == qt))
                                kts = [kk_ for kk_ in (qt - 2, qt - 1, qt) if kk_ >= 0]
                                for i_, kt in enumerate(kts):
                                    dl = qt - kt
                                    slot = 2 + s - dl
                                    masks = [mo[:]] if dl == 0 else ([mp[:]] if dl == 2 else [])
                                    attn_step(Kw[:, g, slot * 128:(slot + 1) * 128], qap, masks,
                                              [(pow_, 65, 128, Vw[:, slot, g, :])], i_ == 0, i_ == len(kts) - 1, clamp=(dl == 0))
                                p.op("dve", "reciprocal", rc[:, 1, :], pos_[:, :, 64])
                                p.op("dve", "reciprocal", rc[:, 2, :], pow_[:, :, 64])
                                gv = gates[:, s, g * 12:(g + 1) * 12].rearrange("p (r n) -> p n r", n=3)
                                p.tt("dve", cf[:], rc[:], gv, ALU.mult)
                                p.tt("dve", o1[:], poA[:, :, 0:64], bc(cf[:, 0, :].unsqueeze(2), [128, 4, 64]), ALU.mult)
                                p.tt("dve", o2[:], pos_[:, :, 0:64], bc(cf[:, 1, :].unsqueeze(2), [128, 4, 64]), ALU.mult)
                                p.tt("pool", o1[:], o1[:], o2[:], ALU.add)
                                p.tt("dve", o2[:], pow_[:, :, 0:64], bc(cf[:, 2, :].unsqueeze(2), [128, 4, 64]), ALU.mult)
                                p.tt("pool", ot[:, g * 4:(g + 1) * 4, :], o1[:], o2[:], ALU.add)
                            for kc in range(8):
                                p.tr(ptt[:, kc * 128:(kc + 1) * 128], ot[:, 2 * kc:2 * kc + 2, :].rearrange("p h d -> p (h d)"), identb[:])
                            p.cp("act", oT[:, :, qs], ptt[:, 0:1024].rearrange("p (c t) -> p c t", c=8))
                            p.barrier()
                with contextlib.ExitStack() as stk:
                    pacc = [ps(f"n3_pa{i}", [128, 2, 512], F32, stk) for i in range(2)]
                    wout = sb("n_wout", [128, 8, D], BF16, stk)
                    p.dma("sp", wout[:], nsa_w_outb.rearrange("(kc p) c -> p kc c", p=128))
                    junk = sb("n3_junk", [128, D], BF16, stk)
                    ss1 = sb("n3_ss1", [128, NS], F32, stk)
                    tmp = sb("n3_tmp", [128, D], F32, stk)
                    for s in range(NS):
                        pm_ = pacc[s % 2]
                        for nh in range(2):
                            for kc in range(8):
                                p.mm(pm_[:, nh, :], oT[:, kc, s * 128:(s + 1) * 128], wout[:, kc, nh * 512:(nh + 1) * 512],
                                     start=(kc == 0), stop=(kc == 7))
                        post_norm_residual(pm_[:], xt, 1, s, (junk, ss1, tmp))
                    p.barrier()
                ffn(1, xt, gcarry, first)
                p.dma("sp", out[b, t0:t0 + TT, :].rearrange("(s p) d -> p s d", p=128), xt[:])
                p.barrier()

    def rwkv_tile(first, oT, ST, stk, hg):
        zs = k.zs
        NCH = TT // 64
        pa = [ps(f"r_pa{i}", [128, TT], F32, stk) for i in range(2)]
        pq = ps("r_pq", [128, 4, 64], F32, stk)
        pqt = ps("r_pqt", [128, 4, 64], F32, stk)
        pp_ = ps("r_pp", [128, 2, 4, 64], F32, stk)
        pg1 = ps("r_pg1", [128, 4, 128], F32, stk)
        pg2 = ps("r_pg2", [128, 4, 128], F32, stk)
        py = ps("r_py", [128, 2, 4, 64], F32, stk)
        AR = sb("r_AR", [128, 2, NCH, 2, 64], F32, stk)
        BK = sb("r_BK", [128, 2, NCH, 2, 64], F32, stk)
        TM = sb("r_TM", [128, NS, 3, 256], F32, stk)
        GC = sb("r_GC", [128, 2, NCH], F32, stk)
        bvT = sb("r_bvT", [128, 2, TT], BF16, stk)
        gT = sb("r_gT", [128, 2, TT], BF16, stk)
        if first and hg == 0:
            p.op("pool", "memset", ST[:], 0.0)
        twa = sb("r_twa", [128, TT], BF16, stk)
        sgl = sb("r_sgl", [128, TT], BF16, stk)
        p.act(twa[0:64, :], zs[12][0:64, :], AF.Tanh)
        p.cp("dve", twa[64:128, :], zs[12][64:128, :])
        p.act(sgl[:], zs[13][:], AF.Sigmoid)
        k.maybe_stop("r1")
        t = {n: sb("r_" + n, [128, TT], F32, stk) for n in
             ("e", "L", "a", "kk", "nrm", "kkn", "tmp", "kmod", "ka", "G1", "Gx", "bh", "kh")}
        v3 = lambda a_: a_.rearrange("p (c t) -> p c t", t=64)
        for cl in range(2):
            cc = hg * 2 + cl
            r_s, k_s, v_s = zs[cc], zs[4 + cc], zs[8 + cc]
            csl = slice(cc * 128, (cc + 1) * 128)
            lsl = slice(cl * 128, (cl + 1) * 128)
            p.mm(pa[0][:], w2b[0:64, csl], twa[0:64, :])
            p.act(t["e"][:], pa[0][:], AF.Sigmoid, bias=ppt[:, W0 + cc:W0 + cc + 1])
            p.ts("dve", t["e"][:], t["e"][:], -float(np.exp(-0.5)), None, ALU.mult)
            p.op("dve", "tensor_tensor_scan", t["L"][:], mask01[:], t["e"][:], 0.0, ALU.mult, ALU.add)
            p.mm(pa[1][:], w2b[64:128, csl], twa[64:128, :])
            p.act(t["a"][:], pa[1][:], AF.Sigmoid, bias=ppt[:, A0 + cc:A0 + cc + 1])
            p.mm(pa[0][:], g2b[:, csl], sgl[:])
            p.cp("act", gT[:, cl, :], pa[0][:])
            p.ts("pool", t["kk"][:], k_s[:], ppt[:, KK + cc:KK + cc + 1], None, ALU.mult)
            p.tt("pool", t["tmp"][:], t["kk"][:], t["kk"][:], ALU.mult)
            p.mm(pa[1][:], onesblk[:], t["tmp"][:])
            p.act(t["nrm"][:], pa[1][:], AF.Sqrt)
            p.ts("dve", t["nrm"][:], t["nrm"][:], 1e-12, None, ALU.max)
            p.op("dve", "reciprocal", t["nrm"][:], t["nrm"][:])
            p.tt("dve", t["kkn"][:], t["kk"][:], t["nrm"][:], ALU.mult)
            p.ts("pool", t["tmp"][:], t["a"][:], ppt[:, KA + cc:KA + cc + 1], omka[:, cc:cc + 1], ALU.mult, ALU.add)
            p.tt("pool", t["kmod"][:], k_s[:], t["tmp"][:], ALU.mult)
            p.stt("dve", t["tmp"][:], r_s[:], ppt[:, RK + cc:RK + cc + 1], t["kmod"][:], ALU.mult, ALU.mult)
            p.mm(pa[0][:], onesblk[:], t["tmp"][:])
            p.tt("dve", bvT[:, cl, :], pa[0][:], v_s[:], ALU.mult)
            p.tt("pool", t["ka"][:], t["kkn"][:], t["a"][:], ALU.mult)
            p.act(t["G1"][:], t["L"][:], AF.Exp)
            p.cp("pool", GC[:, cl, :], v3(t["G1"][:])[:, :, 63])
            p.tt("pool", AR[:, cl, :, 1, :], v3(r_s[:]), v3(t["G1"][:]), ALU.mult)
            p.tt("pool", t["tmp"][:], t["L"][:], t["e"][:], ALU.subtract)
            p.act(t["Gx"][:], t["tmp"][:], AF.Exp)
            p.stt("dve", AR[:, cl, :, 0, :], v3(t["kkn"][:]), -1.0, v3(t["Gx"][:]), ALU.mult, ALU.mult)
            p.act(t["Gx"][:], t["L"][:], AF.Exp, scale=-1.0)
            p.tt("dve", BK[:, cl, :, 0, :], v3(t["ka"][:]), v3(t["Gx"][:]), ALU.mult)
            p.tt("pool", BK[:, cl, :, 1, :], v3(t["kmod"][:]), v3(t["Gx"][:]), ALU.mult)
            L3 = v3(t["L"][:])
            p.tt("dve", v3(t["tmp"][:]), bc(L3[:, :, 63:64], [128, NCH, 64]), L3, ALU.subtract)
            p.act(t["Gx"][:], t["tmp"][:], AF.Exp)
            p.tt("dve", t["bh"][:], t["ka"][:], t["Gx"][:], ALU.mult)
            p.tt("pool", t["kh"][:], t["kmod"][:], t["Gx"][:], ALU.mult)
            for qi, src_ in enumerate((t["bh"], t["kh"], v_s)):
                pz = pa[qi % 2]
                for s in range(NS):
                    p.tr(pz[:, s * 128:(s + 1) * 128], src_[:, s * 128:(s + 1) * 128], identf[:])
                p.cp("act" if qi % 2 == 0 else "dve", TM[:, :, qi, lsl], pz[:].rearrange("p (s c) -> p s c", s=NS))
            k.maybe_stop("r2")
        Gs1 = [sb(f"r_Gs1_{i}", [128, 4, 128], F32, stk) for i in range(2)]
        Gs2 = [sb(f"r_Gs2_{i}", [128, 4, 128], F32, stk) for i in range(2)]
        Qb = [sb(f"r_Q{i}", [128, 4, 64], F32, stk) for i in range(2)]
        QTb = [sb(f"r_QT{i}", [128, 4, 64], F32, stk) for i in range(2)]
        Pm = [sb(f"r_P{i}", [128, 4, 64], F32, stk) for i in range(2)]
        RH = sb("r_RH", [128, 4, 64], F32, stk)
        UT = sb("r_UT", [128, 4, 64], F32, stk)
        ysb = sb("r_ysb", [128, 4, 64], F32, stk)
        ysq = sb("r_ysq", [128, 4, 64], F32, stk)
        st1 = sb("r_st1", [128, 4], F32, stk)
        st2 = sb("r_st2", [128, 4], F32, stk)
        st3 = sb("r_st3", [128, 4], F32, stk)
        ob = sb("r_ob", [128, 128], F32, stk)
        HI = [(h // 2, (h % 2) * 64) for h in range(4)]
        for c in range(NCH):
            pc = (c % 2) * 64
            R = slice(pc, pc + 64)
            s = c // 2
            G1s, G2s = Gs1[c % 2], Gs2[c % 2]
            for h in (0, 2, 1, 3):
                cl, pb = HI[h]
                H = slice(pb, pb + 64)
                ar = AR[H, cl, c, :, :].rearrange("p a t -> p (a t)")
                p.mm(pg1[R, h, :], BK[H, cl, c, 0, :], ar)
                p.mm(pg2[R, h, :], BK[H, cl, c, 1, :], ar)
                p.mm(pqt[R, h, :], AR[H, cl, c, 0, :], BK[H, cl, c, 0, :])
            p.tt("dve", G1s[R, :, :], pg1[R, :, :], bc(maskG[R, :].unsqueeze(1), [64, 4, 128]), ALU.mult)
            p.tt("dve", G2s[R, :, :], pg2[R, :, :], bc(maskG[R, :].unsqueeze(1), [64, 4, 128]), ALU.mult)
            k.maybe_stop("r3")
            Q, QT, P_ = Qb[0], QTb[0], Pm[0]
            p.tt("dve", QT[R, :, :], pqt[R, :, :], bc(maskAT[R, :].unsqueeze(1), [64, 4, 64]), ALU.mult)
            p.cp("act", Q[R, :, :], G1s[R, :, 0:64])
            p.tt("pool", P_[R, :, :], G1s[R, :, 0:64], bc(identf[R, pc:pc + 64].unsqueeze(1), [64, 4, 64]), ALU.add)
            cur = 0
            for lv in range(1, 6):
                Qo, QTo = Qb[cur], QTb[cur]
                Qn, QTn = Qb[1 - cur], QTb[1 - cur]
                for h in (0, 2, 1, 3):
                    if lv < 5:
                        p.mm(pq[R, h, :], QTo[R, h, :], Qo[R, h, :])
                    p.mm(pqt[R, h, :], Qo[R, h, :], QTo[R, h, :])
                if lv < 5:
                    p.cp("act", Qn[R, :, :], pq[R, :, :])
                p.cp("dve", QTn[R, :, :], pqt[R, :, :])
                Po, Pn = Pm[cur], Pm[1 - cur]
                for h in (0, 2, 1, 3):
                    p.mm(pp_[R, 0, h, :], QTn[R, h, :], Po[R, h, :])
                p.tt("dve", Pn[R, :, :], pp_[R, 0, :, :], Po[R, :, :], ALU.add)
                cur = 1 - cur
            Tm = Pm[cur]
            k.maybe_stop("r4")
            for h in (0, 2, 1, 3):
                cl, pb = HI[h]
                H = slice(pb, pb + 64)
                cc = hg * 2 + cl
                p.mm(pp_[R, 0, h, :], AR[H, cl, c, 0, :], ST[H, cc, :])
                p.mm(pp_[R, 1, h, :], G2s[R, h, 0:64], TM[R, s, 2, h * 64:(h + 1) * 64])
            k.maybe_stop("r4a")
            p.cp("act", RH[R, :, :], pp_[R, 0, :, :])
            p.tt("dve", RH[R, :, :], RH[R, :, :], pp_[R, 1, :, :], ALU.add)
            k.maybe_stop("r4b")
            for h in (0, 2, 1, 3):
                p.mm(pq[R, h, :], Tm[R, h, :], RH[R, h, :])
            p.cp("dve", UT[R, :, :], pq[R, :, :])
            k.maybe_stop("r4c")
            for h in (0, 2, 1, 3):
                cl, pb = HI[h]
                H = slice(pb, pb + 64)
                cc = hg * 2 + cl
                p.mm(py[R, 0, h, :], AR[H, cl, c, 1, :], ST[H, cc, :])
            for h in (0, 2, 1, 3):
                p.mm(py[R, 1, h, :], G1s[R, h, 64:128], UT[R, h, :], start=True, stop=False)
                p.mm(py[R, 1, h, :], G2s[R, h, 64:128], TM[R, s, 2, h * 64:(h + 1) * 64], start=False, stop=True)
            k.maybe_stop("r4d")
            for h in (0, 2, 1, 3):
                cl, pb = HI[h]
                H = slice(pb, pb + 64)
                p.mm(pa[0][H, cl * 64:(cl + 1) * 64], TM[R, s, 0, h * 64:(h + 1) * 64], UT[R, h, :], start=True, stop=False)
                p.mm(pa[0][H, cl * 64:(cl + 1) * 64], TM[R, s, 1, h * 64:(h + 1) * 64], TM[R, s, 2, h * 64:(h + 1) * 64], start=False, stop=True)
            k.maybe_stop("r4e")
            for cl in range(2):
                cc = hg * 2 + cl
                p.stt("dve", ST[:, cc, :], ST[:, cc, :], GC[:, cl, c:c + 1], pa[0][:, cl * 64:(cl + 1) * 64], ALU.mult, ALU.add)
            k.maybe_stop("r5")
            if c % 2 == 1:
                p.cp("act", ysb[:], py[:, 0, :, :])
                p.tt("dve", ysb[:], ysb[:], py[:, 1, :, :], ALU.add)
                p.op("dve", "tensor_reduce", st1[:], ysb[:], AX.X, ALU.add)
                p.act(ysq[:], ysb[:], AF.Square)
                p.op("dve", "tensor_reduce", st2[:], ysq[:], AX.X, ALU.add)
                p.ts("dve", st1[:], st1[:], 1.0 / 64, None, ALU.mult)
                p.tt("dve", st3[:], st1[:], st1[:], ALU.mult)
                p.stt("dve", st2[:], st2[:], 1.0 / 64, st3[:], ALU.mult, ALU.subtract)
                p.ts("dve", st2[:], st2[:], GN_EPS, None, ALU.add)
                p.act(st2[:], st2[:], AF.Sqrt)
                p.op("dve", "reciprocal", st2[:], st2[:])
                p.tt("dve", ysb[:], ysb[:], bc(st1[:].unsqueeze(2), [128, 4, 64]), ALU.subtract)
                p.tt("dve", ysb[:], ysb[:], bc(st2[:].unsqueeze(2), [128, 4, 64]), ALU.mult)
                pz = pa[1]
                for cl in range(2):
                    p.tr(pz[:, cl * 128:(cl + 1) * 128], ysb[:, 2 * cl:2 * cl + 2, :].rearrange("p h d -> p (h d)"), identf[:])
                for cl in range(2):
                    cc = hg * 2 + cl
                    tsl = slice(s * 128, (s + 1) * 128)
                    p.ts("dve", ob[:], pz[:, cl * 128:(cl + 1) * 128], ppt[:, LG + cc:LG + cc + 1], ppt[:, LB + cc:LB + cc + 1], ALU.mult, ALU.add)
                    p.tt("pool", ob[:], ob[:], bvT[:, cl, tsl], ALU.add)
                    p.tt("pool", oT[:, 4 + cc, tsl], ob[:], gT[:, cl, tsl], ALU.mult)

    k.maybe_stop = maybe_stop
    try:
        maybe_stop("prepass")
        layer0()
        if n_layers > 1:
            layer1()
    except Stop:
        pass
    p.barrier()
    k.ninst = p.ninst
    return nc, k


def _consts():
    c = {}
    c["c_identb"] = np.eye(128, dtype=np.float32).astype(NPBF)
    c["c_identf"] = np.eye(128, dtype=np.float32)
    ob = np.zeros((128, 128), np.float32)
    ob[:64, :64] = 1
    ob[64:, 64:] = 1
    c["c_onesblk"] = ob
    m = np.ones((128, TT), np.float32)
    m[:, ::64] = 0
    c["c_mask01"] = m
    s = (np.arange(128) % 64)[:, None]
    t = np.arange(64)[None, :]
    c["c_maskG"] = np.concatenate([(s < t), (s <= t)], axis=1).astype(np.float32)
    c["c_maskAT"] = (t < s).astype(np.float32)
    slopes = 2.0 ** (-(np.arange(8) + 1.0))
    ql = (np.arange(TT) % 128).astype(np.float32)
    q = np.zeros((2, 8, TT), np.float32)
    q[0] = slopes[:, None]
    q[1] = -slopes[:, None] * ql[None, :]
    c["c_swa_q"] = q.reshape(2, 8 * TT).astype(NPBF)
    c["c_swa_ko"] = np.stack([ql, np.ones(TT, np.float32)]).astype(NPBF)
    c["c_swa_kp"] = np.stack([np.arange(128, dtype=np.float32) - 128, np.ones(128, np.float32)]).astype(NPBF)
    kl = np.arange(128)[:, None]
    qq = np.arange(128)[None, :]
    c["c_swa_mo"] = (kl <= qq).astype(np.float32).astype(NPBF)
    c["c_swa_mp"] = (kl > qq).astype(np.float32).astype(NPBF)
    return c


def _bf_split(v):
    hi = v.astype(NPBF)
    lo = (v - hi.astype(np.float32)).astype(NPBF)
    return hi, lo


def _consts_nsa(T):
    c = {}
    slopes = (2.0 ** (-8.0 * (np.arange(16) + 1.0) / 16)).astype(np.float32)
    s1, s2 = _bf_split(slopes)
    t = np.arange(T, dtype=np.float32)
    q = np.zeros((5, 16, T), np.float32)
    q[0] = s1.astype(np.float32)[:, None]
    q[1] = s2.astype(np.float32)[:, None]
    q[2] = q[0]
    q[3] = q[1]
    q[4] = -slopes[:, None] * t[None, :]
    c["c_nsa_q"] = q.reshape(5, 16 * T).astype(NPBF)

    def krows(pos):
        lo = pos % 128
        hi = pos - lo
        return np.stack([lo, lo, hi, hi, np.ones_like(pos)]).astype(np.float32)

    c["c_nsa_k"] = krows(t).astype(NPBF)
    sp_slot = -np.ones(384, np.int64)
    for ti in range(8):
        for j in range(32):
            sp_slot[(ti // 3) * 128 + 32 * (ti % 3) + j] = 32 * ti + j
    valid_sp = sp_slot >= 1
    posc = np.where(valid_sp, 16.0 * sp_slot + 15.0, 0.0).astype(np.float32)
    kr = krows(posc)
    kr[:, ~valid_sp] = 0
    c["c_cmp_k"] = kr.astype(NPBF)
    tt_ = np.arange(T)[None, :]
    msk = valid_sp[:, None] & (tt_ >= (16 * sp_slot[:, None] + 15))
    c["c_cmp_mask"] = msk.astype(np.float32).reshape(3, 128, T).astype(NPBF)
    mp_ = np.zeros((384, 64), np.float32)
    wts = {-1: 1.0, 0: 2.0, 1: 2.0, 2: 2.0, 3: 1.0}
    for sp in range(384):
        if sp_slot[sp] >= 1:
            cidx = sp_slot[sp] - 1
            for blk in range(64):
                m = cidx - 4 * blk
                if m in wts:
                    mp_[sp, blk] = wts[m]
    c["c_mpool"] = np.ascontiguousarray(mp_.reshape(3, 128, 64).transpose(1, 0, 2).reshape(128, 192)).astype(NPBF)
    F = np.zeros((T // 128, 128, 64), np.float32)
    for qt in range(T // 128):
        cur = (qt * 128 + np.arange(128)) // 64
        blk = np.arange(64)[None, :]
        forced = (blk == 0) | (blk == cur[:, None]) | (blk == cur[:, None] - 1)
        F[qt] = np.where(forced, 1e30, np.where(blk > cur[:, None], -1e30, 0.0))
    c["c_sel_F"] = F
    c["c_E"] = (np.arange(T)[None, :] // 64 == np.arange(64)[:, None]).astype(np.float32).astype(NPBF)
    return c


def _prep_shared(inp, n_layers=2):
    f = lambda a: np.ascontiguousarray(np.asarray(a, dtype=np.float32))
    sh = dict(_consts())
    T = np.asarray(inp["x"]).shape[1]
    gains = np.zeros((8, D), np.float32)
    for l in range(2):
        gains[l * 4 + 0] = inp["mix_pre_g"][l]
        gains[l * 4 + 1] = inp["mix_post_g"][l]
        gains[l * 4 + 2] = inp["ffn_pre_g"][l]
        gains[l * 4 + 3] = inp["ffn_post_g"][l]
    sh["gains"] = gains
    sh["hy_w_in"] = f(inp["hy_w_in"][0])
    sh["hy_w_out"] = f(inp["hy_w_out"][0])
    sh["ffn_w_up"] = f(inp["ffn_w_up"])
    sh["ffn_w_down"] = f(inp["ffn_w_down"])
    fp = np.zeros((2, 128, NFC, 4), np.float32)
    for l in range(2):
        cw = np.asarray(inp["ffn_conv_w"][l]).reshape(3, NFC, 128)
        fp[l, :, :, 0:3] = cw.transpose(2, 1, 0)
        fp[l, :, :, 3] = np.asarray(inp["ffn_conv_b"][l]).reshape(NFC, 128).T
    sh["ffn_pp"] = fp.reshape(2, 128, NFC * 4)
    pp = np.zeros((128, 42), np.float32)
    pp[:, 0:14] = np.asarray(inp["rwkv_mu"][0]).reshape(14, 128).T
    for i, nm in enumerate(["rwkv_w0", "rwkv_a0", "rwkv_k_k", "rwkv_k_a", "rwkv_r_k", "rwkv_ln_g", "rwkv_ln_b"]):
        pp[:, 14 + 4 * i:18 + 4 * i] = np.asarray(inp[nm][0]).reshape(4, 128).T
    sh["pp0"] = pp
    sh["swa_sinks"] = f(inp["swa_sinks"])
    sh["rwkv_w2"] = f(inp["rwkv_w2"][0])
    sh["rwkv_a2"] = f(inp["rwkv_a2"][0])
    sh["rwkv_g2"] = f(inp["rwkv_g2"][0])
    if n_layers > 1:
        sh["nsa_w_in"] = f(inp["nsa_w_in"][0])
        sh["nsa_w_out"] = f(inp["nsa_w_out"][0])
        sh["cmp_w1_k"] = f(inp["nsa_cmp_w1_k"][0]).reshape(2048, 256)
        sh["cmp_w1_v"] = f(inp["nsa_cmp_w1_v"][0]).reshape(2048, 256)
        sh["cmp_w2_k"] = f(inp["nsa_cmp_w2_k"][0])
        sh["cmp_w2_v"] = f(inp["nsa_cmp_w2_v"][0])
        pt = np.zeros((128, 32), np.float32)
        for kv, nm in enumerate(["nsa_cmp_pos_k", "nsa_cmp_pos_v"]):
            pos = np.asarray(inp[nm][0], np.float32)
            pt[:, kv * 16:(kv + 1) * 16] = pos.reshape(16, 2, 64).transpose(1, 2, 0).reshape(128, 16)
        sh["cmp_posT"] = pt
        sh.update(_consts_nsa(T))
    return sh


def kernel(**inputs):
    x = np.asarray(inputs["x"], dtype=np.float32)
    B, T, _ = x.shape
    n = 8
    NB = B // n
    nc, k = build(T=T, NB=NB, n_layers=2)
    sh = _prep_shared(inputs)
    in_maps = []
    for c in range(n):
        m = dict(sh)
        m["x"] = np.ascontiguousarray(x[c * NB:(c + 1) * NB])
        in_maps.append(m)
    res = run_bass_kernel_spmd(nc, in_maps, core_ids=list(range(n)))
    return np.concatenate([r["out"] for r in res.results], axis=0)
```

```python
import contextlib
import numpy as np
import ml_dtypes
import concourse.bass as bass
import concourse.mybir as mybir
from concourse.bass_utils import run_bass_kernel_spmd

F32 = mybir.dt.float32
BF16 = mybir.dt.bfloat16
AF = mybir.ActivationFunctionType
ALU = mybir.AluOpType
AX = mybir.AxisListType
NPBF = ml_dtypes.bfloat16

D = 1024
TT = 512
NS = 4
HD = 64
DFF = 2816
NFC = 22
HY_COLS = 2560
NSA_COLS = 2608
EPS = 1e-6
GN_EPS = 64e-5


class Prog:
    def __init__(self, nc, n_dma_sems=16):
        self.nc = nc
        self.eng = {"pe": nc.tensor, "act": nc.scalar, "dve": nc.vector, "pool": nc.gpsimd, "sp": nc.sync}
        self.psem = {k: nc.alloc_semaphore(name=f"prog_{k}") for k in self.eng}
        self.cnt = {k: 0 for k in self.eng}
        self.waited = {k: {} for k in self.eng}
        self.dsems = {}
        for q in ("sp", "act", "pool"):
            self.dsems[q] = [[nc.alloc_semaphore(name=f"dma_{q}_{i}"), 0] for i in range(n_dma_sems)]
        self.dnext = {q: 0 for q in self.dsems}
        self.bufs = {}
        self.ninst = 0
        self.pe_base = 0
        self.psum_names = set()

    def _st(self, ap):
        nm = ap.tensor.name
        st = self.bufs.get(nm)
        if st is None:
            st = {"w": None, "r": {}}
            self.bufs[nm] = st
        return st

    def _wait(self, e, deps):
        eng = self.eng[e]
        best = {}
        for d in deps:
            if d is None:
                continue
            s, v, key = d
            if key == e and e == "pe":
                continue
            if best.get(key, (None, 0))[1] < v:
                best[key] = (s, v)
        for key, (s, v) in best.items():
            if self.waited[e].get(key, 0) < v:
                eng.wait_ge(s, v)
                self.waited[e][key] = v

    def _deps(self, reads, writes):
        deps = []
        for ap in reads:
            st = self._st(ap)
            deps.append(st["w"])
            if ap.tensor.name in self.psum_names:
                deps.extend(st["r"].values())
        for ap in writes:
            st = self._st(ap)
            deps.append(st["w"])
            deps.extend(st["r"].values())
        return deps

    def _commit(self, ev, reads, writes):
        for ap in reads:
            self._st(ap)["r"][ev[2]] = ev
        for ap in writes:
            st = self._st(ap)
            st["w"] = ev
            st["r"] = {}

    def op(self, e, fn, *args, **kw):
        writes = [args[0]]
        reads = [a for a in args[1:] if isinstance(a, bass.AP)]
        for k, v in kw.items():
            if isinstance(v, bass.AP):
                (writes if k == "accum_out" else reads).append(v)
        self._wait(e, self._deps(reads, writes))
        if e == "pe":
            base = (fn, args[1].start_partition(), args[0].start_partition(), args[1].shape[0], args[1].shape[-1] if len(args[1].shape) == 2 else -1, args[0].tensor.name, str(args[1].dtype))
            if base != self.pe_base and self.cnt["pe"] > 0:
                self.eng["pe"].wait_ge(self.psem["pe"], self.cnt["pe"])
            self.pe_base = base
        ins = getattr(self.eng[e], fn)(*args, **kw)
        self.cnt[e] += 1
        ins.then_inc(self.psem[e], 1)
        self._commit((self.psem[e], self.cnt[e], e), reads, writes)
        self.ninst += 1
        return ins

    def dma(self, q, out, in_, **kw):
        reads, writes = [in_], [out]
        deps = self._deps(reads, writes)
        slot = self.dnext[q]
        self.dnext[q] = (slot + 1) % len(self.dsems[q])
        ent = self.dsems[q][slot]
        key = f"d_{q}_{slot}"
        if ent[1] > 0:
            deps.append((ent[0], ent[1], key))
        self._wait(q, deps)
        ins = self.eng[q].dma_start(out=out, in_=in_, **kw)
        assert ent[1] + 16 <= 240, "DMA semaphore would exceed its usable range; add a barrier()"
        ent[1] += 16
        ins.then_inc(ent[0], 16)
        self._commit((ent[0], ent[1], key), reads, writes)
        self.ninst += 1
        return ins

    def barrier(self):
        deps = []
        for qq, lst in self.dsems.items():
            for i, ent in enumerate(lst):
                if ent[1] > 0:
                    deps.append((ent[0], ent[1], f"d_{qq}_{i}"))
        for k in self.eng:
            if self.cnt[k] > 0:
                deps.append((self.psem[k], self.cnt[k], k))
        for e in self.eng:
            self._wait(e, deps)
        used = any(ent[1] > 0 for lst in self.dsems.values() for ent in lst)
        if used:
            arr = []
            for e in self.eng:
                if e == "sp":
                    continue
                self.eng[e].sem_inc(self.psem[e], 1)
                self.cnt[e] += 1
                arr.append((self.psem[e], self.cnt[e], e))
            self._wait("sp", arr)
            for qq, lst in self.dsems.items():
                for i, ent in enumerate(lst):
                    if ent[1] > 0:
                        self.eng["sp"].sem_clear(ent[0])
                        ent[1] = 0
                    for e in self.eng:
                        self.waited[e].pop(f"d_{qq}_{i}", None)
            self.eng["sp"].sem_inc(self.psem["sp"], 1)
            self.cnt["sp"] += 1
            ev = (self.psem["sp"], self.cnt["sp"], "sp")
            for e in self.eng:
                if e != "sp":
                    self._wait(e, [ev])
        self.bufs = {}

    def mm(self, out, lhsT, rhs, start=True, stop=True, sgc=False):
        if sgc:
            return self.op("pe", "matmul", out, lhsT, rhs, start=start, stop=stop, skip_group_check=True)
        return self.op("pe", "matmul", out, lhsT, rhs, start=start, stop=stop)

    def tr(self, out, in_, ident):
        return self.op("pe", "transpose", out, in_, ident)

    def act(self, out, in_, func=AF.Copy, **kw):
        return self.op("act", "activation", out, in_, func, **kw)

    def tt(self, e, out, in0, in1, op):
        return self.op(e, "tensor_tensor", out, in0, in1, op)

    def ts(self, e, out, in0, s1, s2, op0, op1=None):
        if op1 is None:
            return self.op(e, "tensor_scalar", out, in0, s1, None, op0)
        return self.op(e, "tensor_scalar", out, in0, s1, s2, op0, op1)

    def stt(self, e, out, in0, scalar, in1, op0, op1):
        return self.op("dve", "scalar_tensor_tensor", out, in0, scalar, in1, op0, op1)

    def cp(self, e, out, in_):
        if e == "act":
            return self.act(out, in_, AF.Copy)
        return self.op(e, "tensor_copy", out, in_)


def bc(ap, shape):
    return ap.broadcast_to(list(shape))


class K:
    pass


def build(T=4096, NB=2, n_layers=2, debug=(), stop=None):
    nc = bass.Bass("TRN2", target_bir_lowering=False)
    p = Prog(nc)
    NT = T // TT
    k = K()
    k.nc, k.p, k.T, k.NB, k.NT = nc, p, T, NB, NT
    dbg = {}

    def din(name, shape, dt=F32):
        return nc.dram_tensor(name, list(shape), dt, kind="ExternalInput").ap()

    def dscr(name, shape, dt):
        return nc.dram_tensor(name, list(shape), dt, kind="Internal").ap()

    x = din("x", [NB, T, D])
    out = nc.dram_tensor("out", [NB, T, D], F32, kind="ExternalOutput").ap()
    gains = din("gains", [8, D])
    w_in0 = din("hy_w_in", [D, HY_COLS])
    w_out0 = din("hy_w_out", [D, D])
    w_up = din("ffn_w_up", [2, D, 2 * DFF])
    w_down = din("ffn_w_down", [2, DFF, D])
    ffn_pp = din("ffn_pp", [2, 128, NFC * 4])
    pp0 = din("pp0", [128, 42])
    sinks = din("swa_sinks", [1, 8])
    rw_w2 = din("rwkv_w2", [64, 512])
    rw_a2 = din("rwkv_a2", [64, 512])
    rw_g2 = din("rwkv_g2", [128, 512])
    c_identb = din("c_identb", [128, 128], BF16)
    c_identf = din("c_identf", [128, 128])
    c_onesblk = din("c_onesblk", [128, 128])
    c_mask01 = din("c_mask01", [128, TT])
    c_maskG = din("c_maskG", [128, 128])
    c_maskAT = din("c_maskAT", [128, 64])
    c_swa_q = din("c_swa_q", [2, 8 * TT], BF16)
    c_swa_ko = din("c_swa_ko", [2, TT], BF16)
    c_swa_kp = din("c_swa_kp", [2, 128], BF16)
    c_swa_mo = din("c_swa_mo", [128, 128], BF16)
    c_swa_mp = din("c_swa_mp", [128, 128], BF16)
    if n_layers > 1:
        nsa_w_in = din("nsa_w_in", [D, NSA_COLS])
        nsa_w_out = din("nsa_w_out", [D, D])
        cw1 = [din("cmp_w1_k", [2048, 256]), din("cmp_w1_v", [2048, 256])]
        cw2 = [din("cmp_w2_k", [256, 64]), din("cmp_w2_v", [256, 64])]
        cposT = din("cmp_posT", [128, 32])
        c_nsa_q = din("c_nsa_q", [5, 16 * T], BF16)
        c_nsa_k = din("c_nsa_k", [5, T], BF16)
        c_cmp_k = din("c_cmp_k", [5, 384], BF16)
        c_cmp_mask = din("c_cmp_mask", [3, 128, T], BF16)
        c_mpool = din("c_mpool", [128, 3 * 64], BF16)
        c_sel_F = din("c_sel_F", [T // 128, 128, 64])
        c_E = din("c_E", [64, T], BF16)
        nsa_w_inb = dscr("nsa_w_inb", [D, NSA_COLS], BF16)
        nsa_w_outb = dscr("nsa_w_outb", [D, D], BF16)
        cw1b = [dscr("cmp_w1_kb", [2048, 256], BF16), dscr("cmp_w1_vb", [2048, 256], BF16)]

    w_in0b = dscr("w_in0b", [D, HY_COLS], BF16)
    w_out0b = dscr("w_out0b", [D, D], BF16)
    w_upb = dscr("w_upb", [2, D, 2 * DFF], BF16)
    w_downb = dscr("w_downb", [2, DFF, D], BF16)
    x1 = dscr("x1", [NB, T, D], F32)

    for nm in debug:
        dbg[nm] = None
    k.dbg_out = {}

    def dbg_dump(name, shape, src_ap, dt=F32):
        if name not in debug:
            return
        if name not in k.dbg_out:
            k.dbg_out[name] = nc.dram_tensor("dbg_" + name, list(shape), dt, kind="ExternalOutput").ap()
        return k.dbg_out[name]

    es = contextlib.ExitStack()

    uid = [0]

    def sb(name, shape, dt, stack=None):
        uid[0] += 1
        return (stack or es).enter_context(nc.sbuf_tensor(f"{name}_{uid[0]}", list(shape), dt))

    def ps(name, shape, dt=F32, stack=None):
        uid[0] += 1
        p.psum_names.add(f"{name}_{uid[0]}")
        return (stack or es).enter_context(nc.psum_tensor(f"{name}_{uid[0]}", list(shape), dt))

    with contextlib.ExitStack() as stc:
        cin = [sb(f"cast_in{i}", [128, 2 * DFF], F32, stc) for i in range(2)]
        cout = [sb(f"cast_out{i}", [128, 2 * DFF], BF16, stc) for i in range(2)]
        ccnt = [0]

        def cast_w(dst, src, rows, cols):
            for r0 in range(0, rows, 128):
                i = ccnt[0] % 2
                ccnt[0] += 1
                p.dma("sp", cin[i][:, 0:cols], src[r0:r0 + 128, :])
                h = cols // 2
                p.cp("dve", cout[i][:, 0:h], cin[i][:, 0:h])
                p.cp("pool", cout[i][:, h:cols], cin[i][:, h:cols])
                p.dma("act", dst[r0:r0 + 128, :], cout[i][:, 0:cols])

        cast_w(w_in0b, w_in0, D, HY_COLS)
        cast_w(w_out0b, w_out0, D, D)
        for l in range(n_layers):
            cast_w(w_upb[l], w_up[l], D, 2 * DFF)
            cast_w(w_downb[l], w_down[l], DFF, D)
        p.barrier()
        if n_layers > 1:
            cast_w(nsa_w_inb, nsa_w_in, D, NSA_COLS)
            cast_w(nsa_w_outb, nsa_w_out, D, D)
            for kv in range(2):
                cast_w(cw1b[kv], cw1[kv], 2048, 256)
            p.barrier()
    k.stop = stop

    class Stop(Exception):
        pass

    def maybe_stop(tag, src_ap=None):
        if k.stop == tag:
            if src_ap is not None:
                p.dma("sp", out[0, 0:TT, :].rearrange("(s p) d -> p s d", p=128), src_ap)
            p.barrier()
            raise Stop()

    identb = sb("identb", [128, 128], BF16)
    identf = sb("identf", [128, 128], F32)
    onesblk = sb("onesblk", [128, 128], F32)
    mask01 = sb("mask01", [128, TT], F32)
    maskG = sb("maskG", [128, 128], F32)
    maskAT = sb("maskAT", [128, 64], F32)
    mo = sb("swa_mo", [128, 128], BF16)
    mp = sb("swa_mp", [128, 128], BF16)
    gbc = sb("gbc", [128, 4, D], F32)
    ppt = sb("ppt", [128, 42], F32)
    omu = sb("omu", [128, 14], F32)
    omka = sb("omka", [128, 4], F32)
    fpp = sb("fpp", [128, NFC * 4], F32)
    esink = sb("esink", [128, 8], F32)
    for dst, src in ((identb, c_identb), (identf, c_identf), (onesblk, c_onesblk), (mask01, c_mask01),
                     (maskG, c_maskG), (maskAT, c_maskAT), (mo, c_swa_mo), (mp, c_swa_mp), (ppt, pp0)):
        p.dma("sp", dst[:], src)
    p.dma("sp", esink[:], sinks[0].partition_broadcast(128))
    p.act(esink[:], esink[:], AF.Exp)
    p.ts("dve", omu[:], ppt[:, 0:14], -1.0, 1.0, ALU.mult, ALU.add)
    p.ts("dve", omka[:], ppt[:, 26:30], -1.0, 1.0, ALU.mult, ALU.add)
    MU, W0, A0, KK, KA, RK, LG, LB = 0, 14, 18, 22, 26, 30, 34, 38

    w2b = sb("w2b", [128, 512], BF16)
    g2b = sb("g2b", [128, 512], BF16)
    with contextlib.ExitStack() as st0:
        tmpf = sb("tmpf_l", [128, 512], F32, st0)
        p.dma("sp", tmpf[0:64, :], rw_w2)
        p.dma("sp", tmpf[64:128, :], rw_a2)
        p.cp("dve", w2b[:], tmpf[:])
        tmpg = sb("tmpg_l", [128, 512], F32, st0)
        p.dma("sp", tmpg[:], rw_g2)
        p.cp("dve", g2b[:], tmpg[:])
        p.barrier()

    def load_gains(layer):
        for i in range(4):
            p.dma("sp", gbc[:, i, :], gains[layer * 4 + i].partition_broadcast(128))

    def rmsnorm_to_hT(src, gi, hT, stk, pst):
        junk = sb("nrm_junk", [128, D], BF16, stk)
        ss = sb("nrm_ss", [128, NS], F32, stk)
        hb = sb("nrm_hb", [128, NS, D], BF16, stk)
        for s in range(NS):
            p.act(junk[:], src[:, s, :], AF.Square, accum_out=ss[:, s:s + 1])
        p.ts("dve", ss[:], ss[:], 1.0 / D, EPS, ALU.mult, ALU.add)
        p.act(ss[:], ss[:], AF.Sqrt)
        p.op("dve", "reciprocal", ss[:], ss[:])
        for s in range(NS):
            p.stt("dve" if s % 2 == 0 else "pool", hb[:, s, :], src[:, s, :], ss[:, s:s + 1], gbc[:, gi, :],
                  ALU.mult, ALU.mult)
        for kc in range(8):
            pt = pst[kc % 2]
            for s in range(NS):
                p.tr(pt[:, s * 128:(s + 1) * 128], hb[:, s, kc * 128:(kc + 1) * 128], identb[:])
            p.cp("act" if kc % 2 == 0 else "dve", hT[:, kc, :], pt[:, 0:TT])

    def post_norm_residual(pm, xres, gi, s, stk_tiles):
        junk, ss1, tmp = stk_tiles
        p.act(junk[:], pm, AF.Square, accum_out=ss1[:, s:s + 1])
        p.ts("dve", ss1[:, s:s + 1], ss1[:, s:s + 1], 1.0 / D, EPS, ALU.mult, ALU.add)
        p.act(ss1[:, s:s + 1], ss1[:, s:s + 1], AF.Sqrt)
        p.op("dve", "reciprocal", ss1[:, s:s + 1], ss1[:, s:s + 1])
        p.stt("dve", tmp[:], pm, ss1[:, s:s + 1], gbc[:, gi, :], ALU.mult, ALU.mult)
        p.tt("pool", xres[:, s, :], xres[:, s, :], tmp[:], ALU.add)

    def ffn(layer, xres, gcarry, first_tile):
        with contextlib.ExitStack() as stk:
            aT = sb("f_aT", [128, NFC, TT], BF16, stk)
            with contextlib.ExitStack() as stk2:
                pst = [ps(f"f_pt{i}", [128, 2 * TT], BF16, stk2) for i in range(2)]
                hT = sb("f_hT", [128, 8, TT], BF16, stk2)
                rmsnorm_to_hT(xres, 2, hT, stk2, pst)
                k.maybe_stop("f1")
                wu = [sb(f"f_wu{i}", [128, 8, 256], BF16, stk2) for i in range(3)]
                gb = [sb(f"f_gb{i}", [128, TT + 2], F32, stk2) for i in range(2)]
                cv = [sb(f"f_cv{i}", [128, TT], F32, stk2) for i in range(2)]
                ge = [sb(f"f_ge{i}", [128, TT], F32, stk2) for i in range(2)]
                pg = [ps(f"f_pg{i}", [128, TT], F32, stk2) for i in range(2)]
                pv = [ps(f"f_pv{i}", [128, TT], F32, stk2) for i in range(2)]
                wup = w_upb[layer].rearrange("(kc p) c -> p kc c", p=128)
                for fc in range(NFC):
                    w = wu[fc % 3]
                    p.dma("sp", w[:, :, 0:128], wup[:, :, fc * 128:(fc + 1) * 128])
                    p.dma("sp", w[:, :, 128:256], wup[:, :, DFF + fc * 128:DFF + (fc + 1) * 128])
                    g_, v_, b_, c_, e_ = pg[fc % 2], pv[fc % 2], gb[fc % 2], cv[fc % 2], ge[fc % 2]
                    for kc in range(8):
                        p.mm(g_[:], w[:, kc, 0:128], hT[:, kc, :], start=(kc == 0), stop=(kc == 7))
                    for kc in range(8):
                        p.mm(v_[:], w[:, kc, 128:256], hT[:, kc, :], start=(kc == 0), stop=(kc == 7))
                    if first_tile:
                        p.op("pool", "memset", b_[:, 0:2], 0.0)
                    else:
                        p.cp("pool", b_[:, 0:2], gcarry[:, fc, :])
                    p.cp("dve", b_[:, 2:TT + 2], g_[:])
                    p.cp("pool", gcarry[:, fc, :], b_[:, TT:TT + 2])
                    pc = fpp[:, fc * 4:fc * 4 + 4]
                    p.ts("dve", c_[:], g_[:], pc[:, 2:3], pc[:, 3:4], ALU.mult, ALU.add)
                    p.stt("dve", c_[:], b_[:, 1:TT + 1], pc[:, 1:2], c_[:], ALU.mult, ALU.add)
                    p.stt("pool", c_[:], b_[:, 0:TT], pc[:, 0:1], c_[:], ALU.mult, ALU.add)
                    p.act(e_[:], c_[:], AF.Gelu_apprx_tanh)
                    p.tt("dve", aT[:, fc, :], e_[:], v_[:], ALU.mult)
                    k.maybe_stop("f2")
                p.barrier()
            pacc = [ps(f"f_pa{i}", [128, 2, 512], F32, stk) for i in range(2)]
            k.maybe_stop("f3")
            wd = [sb(f"f_wd{i}", [128, 2, D], BF16, stk) for i in range(3)]
            junk = sb("f_junk", [128, D], BF16, stk)
            ss1 = sb("f_ss1", [128, NS], F32, stk)
            tmp = sb("f_tmp", [128, D], F32, stk)
            wdn = w_downb[layer].rearrange("(fc p) c -> p fc c", p=128)
            wi = 0
            for half in range(2):
                for f0 in range(0, NFC, 2):
                    w = wd[wi % 3]
                    wi += 1
                    p.dma("sp", w[:], wdn[:, f0:f0 + 2, :])
                    for ff in range(2):
                        fc = f0 + ff
                        for si in range(2):
                            s = half * 2 + si
                            for nh in range(2):
                                p.mm(pacc[si][:, nh, :], aT[:, fc, s * 128:(s + 1) * 128], w[:, ff, nh * 512:(nh + 1) * 512],
                                     start=(fc == 0), stop=(fc == NFC - 1))
                for si in range(2):
                    post_norm_residual(pacc[si][:], xres, 3, half * 2 + si, (junk, ss1, tmp))
                    k.maybe_stop("f4")
            p.barrier()

    def layer0():
        with contextlib.ExitStack() as L0:
            layer0_body(L0)
            p.barrier()

    def layer0_body(L0):
        k.zs = [sb(f"zs{j}", [128, TT], F32, L0) for j in range(14)]
        load_gains(0)
        p.dma("sp", fpp[:], ffn_pp[0])
        KaO = sb("KaO", [66, 2, TT], BF16, L0)
        KaP = sb("KaP", [66, 2, TT + 128], BF16, L0)
        Vsw = sb("Vsw", [128, NS + 1, 2, 65], BF16, L0)
        for g in range(2):
            p.dma("sp", KaO[64:66, g, :], c_swa_ko)
            for s in range(NS + 1):
                p.dma("sp", KaP[64:66, g, s * 128:(s + 1) * 128], c_swa_kp)
        p.op("pool", "memset", Vsw[:, :, :, 64:65], 1.0)
        zcarry = sb("zcarry", [128, 14], F32, L0)
        ST = sb("ST", [128, 4, 64], F32, L0)
        gcarry = sb("gcarry", [128, NFC, 2], F32, L0)
        oT = sb("oT", [128, 8, TT], BF16, L0)
        xt = sb("xt", [128, NS, D], F32, L0)
        win = w_in0b.rearrange("(kc p) c -> p kc c", p=128)

        for b in range(NB):
            for ti in range(NT):
                first = (ti == 0)
                t0 = ti * TT
                with contextlib.ExitStack() as stk:
                    pst = [ps(f"p1_pt{i}", [128, 2 * TT], BF16, stk) for i in range(2)]
                    pz = [ps(f"p1_pz{i}", [128, TT], F32, stk) for i in range(2)]
                    Qa = sb("Qa", [66, 8, TT], BF16, stk)
                    p.dma("sp", Qa[64:66, :, :], c_swa_q.rearrange("r (h t) -> r h t", h=8))
                    p.dma("sp", xt[:], x[b, t0:t0 + TT, :].rearrange("(s p) d -> p s d", p=128))
                    hT = sb("p1_hT", [128, 8, TT], BF16, stk)
                    rmsnorm_to_hT(xt, 0, hT, stk, pst)
                    maybe_stop("p1a", xt[:])
                    wq = [sb(f"p1_w{i}", [128, 8, 128], BF16, stk) for i in range(4)]
                    wcnt = [0]

                    def wload(c0, n):
                        w = wq[wcnt[0] % 4]
                        wcnt[0] += 1
                        p.dma("sp", w[:, :, 0:n], win[:, :, c0:c0 + n])
                        return w

                    def proj_fm(w, c0, n, dst_ps):
                        for kc in range(8):
                            p.mm(dst_ps, w[:, kc, c0:c0 + n], hT[:, kc, :], start=(kc == 0), stop=(kc == 7))

                    for hp in range(4):
                        w = wload(hp * 128, 128)
                        for hh in range(2):
                            h = hp * 2 + hh
                            z = pz[h % 2]
                            proj_fm(w, hh * 64, 64, z[0:64, :])
                            p.act(Qa[0:64, h, :], z[0:64, :], AF.Copy, scale=0.125)
                    maybe_stop("p1b1", xt[:])
                    if not first:
                        for g in range(2):
                            p.cp("pool", KaP[0:64, g, 0:128], KaP[0:64, g, TT:TT + 128])
                        p.cp("pool", Vsw[:, 0, :, 0:64], Vsw[:, NS, :, 0:64])
                    w = wload(512, 128)
                    for g in range(2):
                        z = pz[g % 2]
                        proj_fm(w, g * 64, 64, z[0:64, :])
                        p.cp("dve", KaO[0:64, g, :], z[0:64, :])
                        p.cp("dve", KaP[0:64, g, 128:128 + TT], z[0:64, :])
                    maybe_stop("p1b2", xt[:])
                    w = wload(640, 128)
                    for s in range(NS):
                        z = pz[s % 2]
                        for kc in range(8):
                            p.mm(z[:, 0:128], hT[:, kc, s * 128:(s + 1) * 128], w[:, kc, :], start=(kc == 0), stop=(kc == 7))
                        p.cp("dve", Vsw[:, s + 1, :, 0:64], z[:, 0:128].rearrange("p (g d) -> p g d", g=2))
                    maybe_stop("p1b", xt[:])
                    zs = k.zs
                    zb = [sb(f"p1_zb{i}", [128, TT + 1], F32, stk) for i in range(2)]
                    ztmp = [sb(f"p1_zt{i}", [128, TT], F32, stk) for i in range(2)]
                    for j in range(14):
                        w = wload(768 + j * 128, 128)
                        z = pz[j % 2]
                        proj_fm(w, 0, 128, z[:])
                        b_ = zb[j % 2]
                        tm = ztmp[j % 2]
                        p.act(b_[:, 1:TT + 1], z[:], AF.Copy)
                        if first:
                            p.op("pool", "memset", b_[:, 0:1], 0.0)
                        else:
                            p.cp("pool", b_[:, 0:1], zcarry[:, j:j + 1])
                        p.cp("pool", zcarry[:, j:j + 1], b_[:, TT:TT + 1])
                        p.ts("dve", tm[:], b_[:, 0:TT], ppt[:, MU + j:MU + j + 1], None, ALU.mult)
                        p.stt("dve", zs[j][:], b_[:, 1:TT + 1], omu[:, j:j + 1], tm[:], ALU.mult, ALU.add)
                    maybe_stop("p1c", xt[:])
                    pss = [ps(f"p1_ps{i}", [128, 4, 128], F32, stk) for i in range(2)]
                    po = [ps(f"p1_po{i}", [128, 4, 65], F32, stk) for i in range(2)]
                    pe_ = [sb(f"p1_pe{i}", [128, 4, 128], BF16, stk) for i in range(3)]
                    oa = sb("p1_oa", [128, 8, 64], BF16, stk)
                    den = sb("p1_den", [128, 8], F32, stk)
                    ecnt = 0
                    for s in range(NS):
                        has_prev = not (first and s == 0)
                        for g in range(2):
                            pog = po[g]
                            steps = ([(KaP[:, g, s * 128:(s + 1) * 128], Vsw[:, s, g, :], mp)] if has_prev else []) + \
                                    [(KaO[:, g, s * 128:(s + 1) * 128], Vsw[:, s + 1, g, :], mo)]
                            for si, (kap, vap, msk) in enumerate(steps):
                                sc = pss[ecnt % 2]
                                pt_ = pe_[ecnt % 3]
                                ecnt += 1
                                p.mm(sc[:], kap, Qa[:, g * 4:(g + 1) * 4, s * 128:(s + 1) * 128])
                                p.act(pt_[:], sc[:], AF.Exp)
                                p.tt("dve", pt_[:], pt_[:], bc(msk[:].unsqueeze(1), [128, 4, 128]), ALU.mult)
                                for hh in range(4):
                                    p.mm(pog[:, hh, :], pt_[:, hh, :], vap, start=(si == 0 and hh == 0), stop=(si == len(steps) - 1), sgc=True)
                            p.tt("dve", den[:, g * 4:(g + 1) * 4], pog[:, :, 64], esink[:, g * 4:(g + 1) * 4], ALU.add)
                            p.op("dve", "reciprocal", den[:, g * 4:(g + 1) * 4], den[:, g * 4:(g + 1) * 4])
                            p.tt("dve", oa[:, g * 4:(g + 1) * 4, :], pog[:, :, 0:64],
                                 bc(den[:, g * 4:(g + 1) * 4].unsqueeze(2), [128, 4, 64]), ALU.mult)
                        pt = pst[s % 2]
                        for kc in range(4):
                            p.tr(pt[:, kc * 128:(kc + 1) * 128], oa[:, 2 * kc:2 * kc + 2, :].rearrange("p h d -> p (h d)"), identb[:])
                        p.cp("act", oT[:, 0:4, s * 128:(s + 1) * 128], pt[:, 0:512].rearrange("p (c t) -> p c t", c=4))
                    p.barrier()
                maybe_stop("p1", xt[:])
                for hg in range(2):
                    with contextlib.ExitStack() as stk:
                        rwkv_tile(first, oT, ST, stk, hg)
                        p.barrier()
                maybe_stop("p2", xt[:])
                with contextlib.ExitStack() as stk:
                    pacc = [ps(f"p3_pa{i}", [128, 2, 512], F32, stk) for i in range(2)]
                    wout = sb("wout", [128, 8, D], BF16, stk)
                    p.dma("sp", wout[:], w_out0b.rearrange("(kc p) c -> p kc c", p=128))
                    junk = sb("p3_junk", [128, D], BF16, stk)
                    ss1 = sb("p3_ss1", [128, NS], F32, stk)
                    tmp = sb("p3_tmp", [128, D], F32, stk)
                    for s in range(NS):
                        pm = pacc[s % 2]
                        for nh in range(2):
                            for kc in range(8):
                                p.mm(pm[:, nh, :], oT[:, kc, s * 128:(s + 1) * 128], wout[:, kc, nh * 512:(nh + 1) * 512],
                                     start=(kc == 0), stop=(kc == 7))
                        post_norm_residual(pm[:], xt, 1, s, (junk, ss1, tmp))
                    p.barrier()
                maybe_stop("p3", xt[:])
                ffn(0, xt, gcarry, first)
                dst = x1 if n_layers > 1 else out
                p.dma("sp", dst[b, t0:t0 + TT, :].rearrange("(s p) d -> p s d", p=128), xt[:])
                p.barrier()


    def layer1():
        with contextlib.ExitStack() as L1:
            layer1_body(L1)
            p.barrier()

    def layer1_body(L1):
        load_gains(1)
        p.dma("sp", fpp[:], ffn_pp[1])
        NKT = T // 128
        xt = sb("n_xt", [128, NS, D], F32, L1)
        oT = sb("n_oT", [128, 8, TT], BF16, L1)
        gcarry = sb("n_gcarry", [128, NFC, 2], F32, L1)
        Ks = sb("n_Ks", [69, 4, T], BF16, L1)
        Vs = sb("n_Vs", [128, NKT, 4, 65], BF16, L1)
        Kw = sb("n_Kw", [69, 4, 6 * 128], BF16, L1)
        Vw = sb("n_Vw", [128, 6, 4, 65], BF16, L1)
        Kc = sb("n_Kc", [69, 4, 3, 128], BF16, L1)
        Vc = sb("n_Vc", [128, 3, 4, 129], BF16, L1)
        KC2 = sb("n_KC2", [128, 8, 16 + TT], BF16, L1)
        Ebl = sb("n_E", [64, T], BF16, L1)
        hbias = sb("n_hbias", [128, 4], F32, L1)
        w2c = sb("n_w2c", [128, 2, 2, 64], BF16, L1)
        p.dma("sp", Ebl[:], c_E)
        p.op("pool", "memset", Vs[:, :, :, 64:65], 1.0)
        p.op("pool", "memset", Vw[:, :, :, 64:65], 1.0)
        p.op("pool", "memset", Kc[:], 0.0)
        p.op("pool", "memset", Vc[:], 0.0)
        for g in range(4):
            p.dma("sp", Ks[64:69, g, :], c_nsa_k)
            p.dma("sp", Kc[64:69, g, :, :], c_cmp_k.rearrange("r (c s) -> r c s", c=3))
            for ch in range(3):
                p.dma("sp", Vc[:, ch, g, 65:129], c_mpool[:, ch * 64:(ch + 1) * 64])
        p.op("pool", "memset", Vc[:, :, :, 64:65], 1.0)
        win = nsa_w_inb.rearrange("(kc p) c -> p kc c", p=128)
        with contextlib.ExitStack() as stk:
            w1t = sb("n_w1t", [128, 16, 256], BF16, stk)
            posf = sb("n_posf", [128, 32], F32, stk)
            posb = sb("n_posb", [128, 32], BF16, stk)
            w2f = sb("n_w2f", [128, 2, 2, 64], F32, stk)
            pb_ = ps("n_pb", [128, 8], F32, stk)
            p.dma("sp", posf[:], cposT)
            p.cp("dve", posb[:], posf[:])
            for kv in range(2):
                p.dma("sp", w2f[:, kv, :, :], cw2[kv].rearrange("(c p) d -> p c d", p=128))
            p.cp("dve", w2c[:], w2f[:])
            for kv in range(2):
                p.dma("sp", w1t[:], cw1b[kv].rearrange("(m p) f -> p m f", p=128))
                for fx in range(2):
                    for m in range(16):
                        p.mm(pb_[:, kv * 2 + fx:kv * 2 + fx + 1], w1t[:, m, fx * 128:(fx + 1) * 128],
                             posb[:, kv * 16 + m:kv * 16 + m + 1], start=(m == 0), stop=(m == 15))
            p.cp("dve", hbias[:], pb_[:, 0:4])
            p.barrier()

        for b in range(NB):
            for ti in range(NT):
                first = (ti == 0)
                t0 = ti * TT
                with contextlib.ExitStack() as stkQ:
                    Qa = sb("n_Qa", [69, 16, TT], BF16, stkQ)
                    gates = sb("n_gates", [128, NS, 48], F32, stkQ)
                    with contextlib.ExitStack() as stk:
                        pst = [ps(f"n1_pt{i}", [128, 2 * TT], BF16, stk) for i in range(2)]
                        pz = [ps(f"n1_pz{i}", [128, TT], F32, stk) for i in range(2)]
                        p.dma("sp", Qa[64:69, :, :], c_nsa_q.rearrange("r (h t) -> r h t", h=16)[:, :, t0:t0 + TT])
                        p.dma("sp", xt[:], x1[b, t0:t0 + TT, :].rearrange("(s p) d -> p s d", p=128))
                        hT = sb("n1_hT", [128, 8, TT], BF16, stk)
                        rmsnorm_to_hT(xt, 0, hT, stk, pst)
                        wq = [sb(f"n1_w{i}", [128, 8, 128], BF16, stk) for i in range(4)]
                        wcnt = [0]

                        def wload(cols):
                            w = wq[wcnt[0] % 4]
                            wcnt[0] += 1
                            o = 0
                            for c0, n in cols:
                                p.dma("sp", w[:, :, o:o + n], win[:, :, c0:c0 + n])
                                o += n
                            return w

                        def proj_fm(w, c0, n, dst_ps):
                            for kc in range(8):
                                p.mm(dst_ps, w[:, kc, c0:c0 + n], hT[:, kc, :], start=(kc == 0), stop=(kc == 7))

                        zi = 0
                        for hp in range(8):
                            w = wload([(hp * 128, 128)])
                            for hh in range(2):
                                z = pz[zi % 2]; zi += 1
                                proj_fm(w, hh * 64, 64, z[0:64, :])
                                p.act(Qa[0:64, hp * 2 + hh, :], z[0:64, :], AF.Copy, scale=0.125)
                        for kv in range(2):
                            for g in range(4):
                                c0 = 1024 + kv * 256 + g * 64
                                w = wload([(c0, 64), (c0, 64)])
                                z = pz[zi % 2]; zi += 1
                                proj_fm(w, 0, 128, z[:])
                                kt_ = KC2[:, kv * 4 + g, :]
                                if first:
                                    p.op("pool", "memset", kt_[:, 0:16], 0.0)
                                else:
                                    p.cp("pool", kt_[0:64, 0:16], kt_[0:64, TT:TT + 16])
                                    p.cp("pool", kt_[64:128, 0:15], kt_[64:128, TT:TT + 15])
                                p.cp("dve", kt_[0:64, 16:16 + TT], z[0:64, :])
                                p.cp("dve", kt_[64:128, 15:15 + TT], z[64:128, :])
                        for which, dstK, cbase in (("s", Ks, 1536), ("w", Kw, 2048)):
                            if which == "w":
                                if not first:
                                    p.cp("pool", Kw[:, :, 0:256], Kw[:, :, 512:768])
                                    p.cp("pool", Vw[:, 0:2, :, 0:64], Vw[:, 4:6, :, 0:64])
                                for g in range(4):
                                    p.dma("sp", Kw[64:69, g, 256:768], c_nsa_k[:, t0:t0 + TT])
                                    if not first:
                                        p.dma("sp", Kw[64:69, g, 0:256], c_nsa_k[:, t0 - 256:t0])
                            for gp in range(2):
                                w = wload([(cbase + gp * 128, 128)])
                                for gg in range(2):
                                    g = gp * 2 + gg
                                    z = pz[zi % 2]; zi += 1
                                    proj_fm(w, gg * 64, 64, z[0:64, :])
                                    if which == "s":
                                        p.cp("dve", Ks[0:64, g, t0:t0 + TT], z[0:64, :])
                                    else:
                                        p.cp("dve", Kw[0:64, g, 256:768], z[0:64, :])
                        for which, cbase in (("s", 1792), ("w", 2304)):
                            wa = wload([(cbase, 128)])
                            wb_ = wload([(cbase + 128, 128)])
                            for s in range(NS):
                                z = pz[zi % 2]; zi += 1
                                for half, w in enumerate((wa, wb_)):
                                    for kc in range(8):
                                        p.mm(z[:, half * 128:(half + 1) * 128], hT[:, kc, s * 128:(s + 1) * 128], w[:, kc, :],
                                             start=(kc == 0), stop=(kc == 7))
                                src_ = z[:, 0:256].rearrange("p (g d) -> p g d", g=4)
                                if which == "s":
                                    p.cp("dve", Vs[:, ti * NS + s, :, 0:64], src_)
                                else:
                                    p.cp("dve", Vw[:, 2 + s, :, 0:64], src_)
                        w = wload([(2560, 48)])
                        for s in range(NS):
                            z = pz[zi % 2]; zi += 1
                            for kc in range(8):
                                p.mm(z[:, 0:48], hT[:, kc, s * 128:(s + 1) * 128], w[:, kc, 0:48], start=(kc == 0), stop=(kc == 7))
                            p.act(gates[:, s, :], z[:, 0:48], AF.Sigmoid)
                        p.barrier()
                    with contextlib.ExitStack() as stk:
                        w1t = sb("nc_w1t", [128, 16, 256], BF16, stk)
                        hid = sb("nc_hid", [128, 2, 32], BF16, stk)
                        ph = [ps(f"nc_ph{i}", [128, 32], F32, stk) for i in range(2)]
                        pk = ps("nc_pk", [128, 64], F32, stk)
                        chn, q3 = ti // 3, ti % 3
                        for kv in range(2):
                            p.dma("sp", w1t[:], cw1b[kv].rearrange("(m p) f -> p m f", p=128))
                            for g in range(4):
                                kt_ = KC2[:, kv * 4 + g, :]
                                for fx in range(2):
                                    for m in range(16):
                                        if 2 * m < 16:
                                            rhs = kt_[:, 0:TT].rearrange("p (j s) -> p j s", s=16)[:, :, 2 * m]
                                        else:
                                            rhs = kt_[:, 16:16 + TT].rearrange("p (j s) -> p j s", s=16)[:, :, 2 * m - 16]
                                        p.mm(ph[fx][:], w1t[:, m, fx * 128:(fx + 1) * 128], rhs, start=(m == 0), stop=(m == 15))
                                    p.act(hid[:, fx, :], ph[fx][:], AF.Gelu_apprx_tanh, bias=hbias[:, kv * 2 + fx:kv * 2 + fx + 1])
                                if kv == 0:
                                    for fx in range(2):
                                        p.mm(pk[0:64, 0:32], w2c[:, 0, fx, :], hid[:, fx, :], start=(fx == 0), stop=(fx == 1))
                                    p.cp("dve", Kc[0:64, g, chn, q3 * 32:(q3 + 1) * 32], pk[0:64, 0:32])
                                else:
                                    P32 = slice(q3 * 32, (q3 + 1) * 32)
                                    for fx in range(2):
                                        p.mm(pk[P32, 0:64], hid[:, fx, :], w2c[:, 1, fx, :], start=(fx == 0), stop=(fx == 1))
                                    p.cp("dve", Vc[P32, chn, g, 0:64], pk[P32, 0:64])
                        p.barrier()
                    with contextlib.ExitStack() as stk:
                        pss = [ps(f"n2_ps{i}", [128, 4, 128], F32, stk) for i in range(2)]
                        pm = ps("n2_pm", [128, 4, 128], F32, stk)
                        poA = ps("n2_poA", [128, 4, 65], F32, stk)
                        poB = ps("n2_poB", [128, 4, 64], F32, stk)
                        pos_ = ps("n2_pos", [128, 4, 65], F32, stk)
                        pow_ = ps("n2_pow", [128, 4, 65], F32, stk)
                        ptt = ps("n2_pt", [128, 2 * TT], BF16, stk)
                        pe_ = [sb(f"n2_pe{i}", [128, 4, 128], BF16, stk) for i in range(3)]
                        cm = [sb(f"n2_cm{i}", [128, 128], BF16, stk) for i in range(3)]
                        Ft = sb("n2_F", [128, 64], F32, stk)
                        imp = sb("n2_imp", [128, 64], F32, stk)
                        m8 = sb("n2_m8", [128, 8], F32, stk)
                        sel = sb("n2_sel", [128, 64], BF16, stk)
                        selT = sb("n2_selT", [64, 128], BF16, stk)
                        mdg = sb("n2_mdg", [128, 128], BF16, stk)
                        rc = sb("n2_rc", [128, 3, 4], F32, stk)
                        cf = sb("n2_cf", [128, 3, 4], F32, stk)
                        ot = sb("n2_ot", [128, 16, 64], BF16, stk)
                        o1 = sb("n2_o1", [128, 4, 64], F32, stk)
                        o2 = sb("n2_o2", [128, 4, 64], F32, stk)
                        ec = [0]

                        def attn_step(kap, qap, masks, pv_list, first_step, last_step, clamp=False):
                            sc = pss[ec[0] % 2]
                            pt_ = pe_[ec[0] % 3]
                            ec[0] += 1
                            p.mm(sc[:], kap, qap)
                            if clamp:
                                p.ts("dve", sc[:], sc[:], 60.0, None, ALU.min)
                            p.act(pt_[:], sc[:], AF.Exp)
                            for msk in masks:
                                p.tt("dve", pt_[:], pt_[:], bc(msk.unsqueeze(1), [128, 4, 128]), ALU.mult)
                            for (po_, wdt, rows, rhs) in pv_list:
                                for hh in range(4):
                                    p.mm(po_[:, hh, 0:wdt], pt_[0:rows, hh, :], rhs, start=(first_step and hh == 0), stop=last_step, sgc=True)

                        for s in range(NS):
                            qt = ti * NS + s
                            qs = slice(s * 128, (s + 1) * 128)
                            p.dma("sp", Ft[:], c_sel_F[qt])
                            for g in range(4):
                                qap = Qa[:, g * 4:(g + 1) * 4, qs]
                                chmax = ((8 * qt + 7) // 32) // 3
                                for ch in range(chmax + 1):
                                    cmk = cm[(ec[0]) % 3]
                                    p.dma("sp", cmk[:], c_cmp_mask[ch, :, qt * 128:(qt + 1) * 128])
                                    attn_step(Kc[:, g, ch, :], qap, [cmk[:]],
                                              [(poA, 65, 96, Vc[0:96, ch, g, 0:65]), (poB, 64, 96, Vc[0:96, ch, g, 65:129])],
                                              ch == 0, ch == chmax, clamp=True)
                                p.ts("dve", rc[:, 0, :], poA[:, :, 64], 1e-30, None, ALU.max)
                                p.op("dve", "reciprocal", rc[:, 0, :], rc[:, 0, :])
                                p.ts("dve", imp[:], poB[:, 0, :], rc[:, 0, 0:1], None, ALU.mult)
                                for hh in range(1, 4):
                                    p.stt("dve", imp[:], poB[:, hh, :], rc[:, 0, hh:hh + 1], imp[:], ALU.mult, ALU.add)
                                p.tt("dve", imp[:], imp[:], Ft[:], ALU.add)
                                p.op("dve", "max", m8[:], imp[:])
                                p.ts("dve", sel[:], imp[:], m8[:, 7:8], None, ALU.is_ge)
                                p.tr(ptt[0:64, 0:128], sel[:], identb[:])
                                p.cp("dve", selT[:], ptt[0:64, 0:128])
                                for kt in range(qt + 1):
                                    mslot = pm[:, kt % 4, :]
                                    p.mm(mslot, Ebl[:, kt * 128:(kt + 1) * 128], selT[:])
                                    if kt == qt:
                                        p.tt("dve", mdg[:], mslot, mo[:], ALU.mult)
                                        msk = mdg[:]
                                    else:
                                        msk = mslot
                                    attn_step(Ks[:, g, kt * 128:(kt + 1) * 128], qap, [msk],
                                              [(pos_, 65, 128, Vs[:, kt, g, :])], kt == 0, kt == qt, clamp=(kt == qt))
                                kts = [kk_ for kk_ in (qt - 2, qt - 1, qt) if kk_ >= 0]
                                for i_, kt in enumerate(kts):
                                    dl = qt - kt
                                    slot = 2 + s - dl
                                    masks = [mo[:]] if dl == 0 else ([mp[:]] if dl == 2 else [])
                                    attn_step(Kw[:, g, slot * 128:(slot + 1) * 128], qap, masks,
                                              [(pow_, 65, 128, Vw[:, slot, g, :])], i_ == 0, i_ == len(kts) - 1, clamp=(dl == 0))
                                p.op("dve", "reciprocal", rc[:, 1, :], pos_[:, :, 64])
                                p.op("dve", "reciprocal", rc[:, 2, :], pow_[:, :, 64])
                                gv = gates[:, s, g * 12:(g + 1) * 12].rearrange("p (r n) -> p n r", n=3)
                                p.tt("dve", cf[:], rc[:], gv, ALU.mult)
                                p.tt("dve", o1[:], poA[:, :, 0:64], bc(cf[:, 0, :].unsqueeze(2), [128, 4, 64]), ALU.mult)
                                p.tt("dve", o2[:], pos_[:, :, 0:64], bc(cf[:, 1, :].unsqueeze(2), [128, 4, 64]), ALU.mult)
                                p.tt("pool", o1[:], o1[:], o2[:], ALU.add)
                                p.tt("dve", o2[:], pow_[:, :, 0:64], bc(cf[:, 2, :].unsqueeze(2), [128, 4, 64]), ALU.mult)
                                p.tt("pool", ot[:, g * 4:(g + 1) * 4, :], o1[:], o2[:], ALU.add)
                            for kc in range(8):
                                p.tr(ptt[:, kc * 128:(kc + 1) * 128], ot[:, 2 * kc:2 * kc + 2, :].rearrange("p h d -> p (h d)"), identb[:])
                            p.cp("act", oT[:, :, qs], ptt[:, 0:1024].rearrange("p (c t) -> p c t", c=8))
                            p.barrier()
                with contextlib.ExitStack() as stk:
                    pacc = [ps(f"n3_pa{i}", [128, 2, 512], F32, stk) for i in range(2)]
                    wout = sb("n_wout", [128, 8, D], BF16, stk)
                    p.dma("sp", wout[:], nsa_w_outb.rearrange("(kc p) c -> p kc c", p=128))
                    junk = sb("n3_junk", [128, D], BF16, stk)
                    ss1 = sb("n3_ss1", [128, NS], F32, stk)
                    tmp = sb("n3_tmp", [128, D], F32, stk)
                    for s in range(NS):
                        pm_ = pacc[s % 2]
                        for nh in range(2):
                            for kc in range(8):
                                p.mm(pm_[:, nh, :], oT[:, kc, s * 128:(s + 1) * 128], wout[:, kc, nh * 512:(nh + 1) * 512],
                                     start=(kc == 0), stop=(kc == 7))
                        post_norm_residual(pm_[:], xt, 1, s, (junk, ss1, tmp))
                    p.barrier()
                ffn(1, xt, gcarry, first)
                p.dma("sp", out[b, t0:t0 + TT, :].rearrange("(s p) d -> p s d", p=128), xt[:])
                p.barrier()

    def rwkv_tile(first, oT, ST, stk, hg):
        zs = k.zs
        NCH = TT // 64
        pa = [ps(f"r_pa{i}", [128, TT], F32, stk) for i in range(2)]
        pq = ps("r_pq", [128, 4, 64], F32, stk)
        pqt = ps("r_pqt", [128, 4, 64], F32, stk)
        pp_ = ps("r_pp", [128, 2, 4, 64], F32, stk)
        pg1 = ps("r_pg1", [128, 4, 128], F32, stk)
        pg2 = ps("r_pg2", [128, 4, 128], F32, stk)
        py = ps("r_py", [128, 2, 4, 64], F32, stk)
        AR = sb("r_AR", [128, 2, NCH, 2, 64], BF16, stk)
        BK = sb("r_BK", [128, 2, NCH, 2, 64], BF16, stk)
        TM = sb("r_TM", [128, NS, 3, 256], BF16, stk)
        STb = sb("r_STb", [128, 4, 64], BF16, stk)
        GC = sb("r_GC", [128, 2, NCH], F32, stk)
        bvT = sb("r_bvT", [128, 2, TT], BF16, stk)
        gT = sb("r_gT", [128, 2, TT], BF16, stk)
        if first and hg == 0:
            p.op("pool", "memset", ST[:], 0.0)
        twa = sb("r_twa", [128, TT], BF16, stk)
        sgl = sb("r_sgl", [128, TT], BF16, stk)
        p.act(twa[0:64, :], zs[12][0:64, :], AF.Tanh)
        p.cp("dve", twa[64:128, :], zs[12][64:128, :])
        p.act(sgl[:], zs[13][:], AF.Sigmoid)
        k.maybe_stop("r1")
        t = {n: sb("r_" + n, [128, TT], F32, stk) for n in
             ("e", "L", "a", "kk", "nrm", "kkn", "tmp", "kmod", "ka", "G1", "Gx", "bh", "kh")}
        v3 = lambda a_: a_.rearrange("p (c t) -> p c t", t=64)
        for cl in range(2):
            cc = hg * 2 + cl
            r_s, k_s, v_s = zs[cc], zs[4 + cc], zs[8 + cc]
            csl = slice(cc * 128, (cc + 1) * 128)
            lsl = slice(cl * 128, (cl + 1) * 128)
            p.mm(pa[0][:], w2b[0:64, csl], twa[0:64, :])
            p.act(t["e"][:], pa[0][:], AF.Sigmoid, bias=ppt[:, W0 + cc:W0 + cc + 1])
            p.ts("dve", t["e"][:], t["e"][:], -float(np.exp(-0.5)), None, ALU.mult)
            p.op("dve", "tensor_tensor_scan", t["L"][:], mask01[:], t["e"][:], 0.0, ALU.mult, ALU.add)
            p.mm(pa[1][:], w2b[64:128, csl], twa[64:128, :])
            p.act(t["a"][:], pa[1][:], AF.Sigmoid, bias=ppt[:, A0 + cc:A0 + cc + 1])
            p.mm(pa[0][:], g2b[:, csl], sgl[:])
            p.cp("act", gT[:, cl, :], pa[0][:])
            p.ts("pool", t["kk"][:], k_s[:], ppt[:, KK + cc:KK + cc + 1], None, ALU.mult)
            p.tt("pool", t["tmp"][:], t["kk"][:], t["kk"][:], ALU.mult)
            p.mm(pa[1][:], onesblk[:], t["tmp"][:])
            p.act(t["nrm"][:], pa[1][:], AF.Sqrt)
            p.ts("dve", t["nrm"][:], t["nrm"][:], 1e-12, None, ALU.max)
            p.op("dve", "reciprocal", t["nrm"][:], t["nrm"][:])
            p.tt("dve", t["kkn"][:], t["kk"][:], t["nrm"][:], ALU.mult)
            p.ts("pool", t["tmp"][:], t["a"][:], ppt[:, KA + cc:KA + cc + 1], omka[:, cc:cc + 1], ALU.mult, ALU.add)
            p.tt("pool", t["kmod"][:], k_s[:], t["tmp"][:], ALU.mult)
            p.stt("dve", t["tmp"][:], r_s[:], ppt[:, RK + cc:RK + cc + 1], t["kmod"][:], ALU.mult, ALU.mult)
            p.mm(pa[0][:], onesblk[:], t["tmp"][:])
            p.tt("dve", bvT[:, cl, :], pa[0][:], v_s[:], ALU.mult)
            p.tt("pool", t["ka"][:], t["kkn"][:], t["a"][:], ALU.mult)
            p.act(t["G1"][:], t["L"][:], AF.Exp)
            p.cp("pool", GC[:, cl, :], v3(t["G1"][:])[:, :, 63])
            p.tt("pool", AR[:, cl, :, 1, :], v3(r_s[:]), v3(t["G1"][:]), ALU.mult)
            p.tt("pool", t["tmp"][:], t["L"][:], t["e"][:], ALU.subtract)
            p.act(t["Gx"][:], t["tmp"][:], AF.Exp)
            p.stt("dve", AR[:, cl, :, 0, :], v3(t["kkn"][:]), -1.0, v3(t["Gx"][:]), ALU.mult, ALU.mult)
            p.act(t["Gx"][:], t["L"][:], AF.Exp, scale=-1.0)
            p.tt("dve", BK[:, cl, :, 0, :], v3(t["ka"][:]), v3(t["Gx"][:]), ALU.mult)
            p.tt("pool", BK[:, cl, :, 1, :], v3(t["kmod"][:]), v3(t["Gx"][:]), ALU.mult)
            L3 = v3(t["L"][:])
            p.tt("dve", v3(t["tmp"][:]), bc(L3[:, :, 63:64], [128, NCH, 64]), L3, ALU.subtract)
            p.act(t["Gx"][:], t["tmp"][:], AF.Exp)
            p.tt("dve", t["bh"][:], t["ka"][:], t["Gx"][:], ALU.mult)
            p.tt("pool", t["kh"][:], t["kmod"][:], t["Gx"][:], ALU.mult)
            for qi, src_ in enumerate((t["bh"], t["kh"], v_s)):
                pz = pa[qi % 2]
                for s in range(NS):
                    p.tr(pz[:, s * 128:(s + 1) * 128], src_[:, s * 128:(s + 1) * 128], identf[:])
                p.cp("act" if qi % 2 == 0 else "dve", TM[:, :, qi, lsl], pz[:].rearrange("p (s c) -> p s c", s=NS))
            k.maybe_stop("r2")
        Gs1 = [sb(f"r_Gs1_{i}", [128, 4, 128], BF16, stk) for i in range(2)]
        Gs2 = [sb(f"r_Gs2_{i}", [128, 4, 128], BF16, stk) for i in range(2)]
        Qb = [sb(f"r_Q{i}", [128, 4, 64], BF16, stk) for i in range(2)]
        QTb = [sb(f"r_QT{i}", [128, 4, 64], BF16, stk) for i in range(2)]
        Pm = [sb(f"r_P{i}", [128, 4, 64], F32, stk) for i in range(2)]
        Pb = [sb(f"r_Pb{i}", [128, 4, 64], BF16, stk) for i in range(2)]
        RH = sb("r_RH", [128, 4, 64], BF16, stk)
        UT = sb("r_UT", [128, 4, 64], BF16, stk)
        ysb = sb("r_ysb", [128, 4, 64], F32, stk)
        ysq = sb("r_ysq", [128, 4, 64], F32, stk)
        st1 = sb("r_st1", [128, 4], F32, stk)
        st2 = sb("r_st2", [128, 4], F32, stk)
        st3 = sb("r_st3", [128, 4], F32, stk)
        ob = sb("r_ob", [128, 128], F32, stk)
        HI = [(h // 2, (h % 2) * 64) for h in range(4)]
        p.cp("pool", STb[:], ST[:])
        for c in range(NCH):
            pc = (c % 2) * 64
            R = slice(pc, pc + 64)
            s = c // 2
            G1s, G2s = Gs1[c % 2], Gs2[c % 2]
            for h in (0, 2, 1, 3):
                cl, pb = HI[h]
                H = slice(pb, pb + 64)
                ar = AR[H, cl, c, :, :].rearrange("p a t -> p (a t)")
                p.mm(pg1[R, h, :], BK[H, cl, c, 0, :], ar)
                p.mm(pg2[R, h, :], BK[H, cl, c, 1, :], ar)
                p.mm(pqt[R, h, :], AR[H, cl, c, 0, :], BK[H, cl, c, 0, :])
            p.tt("dve", G1s[R, :, :], pg1[R, :, :], bc(maskG[R, :].unsqueeze(1), [64, 4, 128]), ALU.mult)
            p.tt("dve", G2s[R, :, :], pg2[R, :, :], bc(maskG[R, :].unsqueeze(1), [64, 4, 128]), ALU.mult)
            k.maybe_stop("r3")
            Q, QT, P_ = Qb[0], QTb[0], Pm[0]
            p.tt("dve", QT[R, :, :], pqt[R, :, :], bc(maskAT[R, :].unsqueeze(1), [64, 4, 64]), ALU.mult)
            p.cp("act", Q[R, :, :], G1s[R, :, 0:64])
            p.tt("pool", P_[R, :, :], G1s[R, :, 0:64], bc(identf[R, pc:pc + 64].unsqueeze(1), [64, 4, 64]), ALU.add)
            p.tt("pool", Pb[0][R, :, :], G1s[R, :, 0:64], bc(identf[R, pc:pc + 64].unsqueeze(1), [64, 4, 64]), ALU.add)
            cur = 0
            for lv in range(1, 6):
                Qo, QTo = Qb[cur], QTb[cur]
                Qn, QTn = Qb[1 - cur], QTb[1 - cur]
                for h in (0, 2, 1, 3):
                    if lv < 5:
                        p.mm(pq[R, h, :], QTo[R, h, :], Qo[R, h, :])
                    p.mm(pqt[R, h, :], Qo[R, h, :], QTo[R, h, :])
                if lv < 5:
                    p.cp("act", Qn[R, :, :], pq[R, :, :])
                p.cp("dve", QTn[R, :, :], pqt[R, :, :])
                Po, Pn = Pm[cur], Pm[1 - cur]
                for h in (0, 2, 1, 3):
                    p.mm(pp_[R, 0, h, :], QTn[R, h, :], Pb[cur][R, h, :])
                p.tt("dve", Pn[R, :, :], pp_[R, 0, :, :], Po[R, :, :], ALU.add)
                p.tt("dve", Pb[1 - cur][R, :, :], pp_[R, 0, :, :], Po[R, :, :], ALU.add)
                cur = 1 - cur
            Tm = Pb[cur]
            k.maybe_stop("r4")
            for h in (0, 2, 1, 3):
                cl, pb = HI[h]
                H = slice(pb, pb + 64)
                cc = hg * 2 + cl
                p.mm(pp_[R, 0, h, :], AR[H, cl, c, 0, :], STb[H, cc, :])
                p.mm(pp_[R, 1, h, :], G2s[R, h, 0:64], TM[R, s, 2, h * 64:(h + 1) * 64])
            k.maybe_stop("r4a")
            p.cp("act", RH[R, :, :], pp_[R, 0, :, :])
            p.tt("dve", RH[R, :, :], RH[R, :, :], pp_[R, 1, :, :], ALU.add)
            k.maybe_stop("r4b")
            for h in (0, 2, 1, 3):
                p.mm(pq[R, h, :], Tm[R, h, :], RH[R, h, :])
            p.cp("dve", UT[R, :, :], pq[R, :, :])
            k.maybe_stop("r4c")
            for h in (0, 2, 1, 3):
                cl, pb = HI[h]
                H = slice(pb, pb + 64)
                cc = hg * 2 + cl
                p.mm(py[R, 0, h, :], AR[H, cl, c, 1, :], STb[H, cc, :])
            for h in (0, 2, 1, 3):
                p.mm(py[R, 1, h, :], G1s[R, h, 64:128], UT[R, h, :], start=True, stop=False)
                p.mm(py[R, 1, h, :], G2s[R, h, 64:128], TM[R, s, 2, h * 64:(h + 1) * 64], start=False, stop=True)
            k.maybe_stop("r4d")
            for h in (0, 2, 1, 3):
                cl, pb = HI[h]
                H = slice(pb, pb + 64)
                p.mm(pa[0][H, cl * 64:(cl + 1) * 64], TM[R, s, 0, h * 64:(h + 1) * 64], UT[R, h, :], start=True, stop=False)
                p.mm(pa[0][H, cl * 64:(cl + 1) * 64], TM[R, s, 1, h * 64:(h + 1) * 64], TM[R, s, 2, h * 64:(h + 1) * 64], start=False, stop=True)
            k.maybe_stop("r4e")
            for cl in range(2):
                cc = hg * 2 + cl
                p.stt("dve", ST[:, cc, :], ST[:, cc, :], GC[:, cl, c:c + 1], pa[0][:, cl * 64:(cl + 1) * 64], ALU.mult, ALU.add)
                p.cp("pool", STb[:, cc, :], ST[:, cc, :])
            k.maybe_stop("r5")
            if c % 2 == 1:
                p.cp("act", ysb[:], py[:, 0, :, :])
                p.tt("dve", ysb[:], ysb[:], py[:, 1, :, :], ALU.add)
                p.op("dve", "tensor_reduce", st1[:], ysb[:], AX.X, ALU.add)
                p.act(ysq[:], ysb[:], AF.Square)
                p.op("dve", "tensor_reduce", st2[:], ysq[:], AX.X, ALU.add)
                p.ts("dve", st1[:], st1[:], 1.0 / 64, None, ALU.mult)
                p.tt("dve", st3[:], st1[:], st1[:], ALU.mult)
                p.stt("dve", st2[:], st2[:], 1.0 / 64, st3[:], ALU.mult, ALU.subtract)
                p.ts("dve", st2[:], st2[:], GN_EPS, None, ALU.add)
                p.act(st2[:], st2[:], AF.Sqrt)
                p.op("dve", "reciprocal", st2[:], st2[:])
                p.tt("dve", ysb[:], ysb[:], bc(st1[:].unsqueeze(2), [128, 4, 64]), ALU.subtract)
                p.tt("dve", ysb[:], ysb[:], bc(st2[:].unsqueeze(2), [128, 4, 64]), ALU.mult)
                pz = pa[1]
                for cl in range(2):
                    p.tr(pz[:, cl * 128:(cl + 1) * 128], ysb[:, 2 * cl:2 * cl + 2, :].rearrange("p h d -> p (h d)"), identf[:])
                for cl in range(2):
                    cc = hg * 2 + cl
                    tsl = slice(s * 128, (s + 1) * 128)
                    p.ts("dve", ob[:], pz[:, cl * 128:(cl + 1) * 128], ppt[:, LG + cc:LG + cc + 1], ppt[:, LB + cc:LB + cc + 1], ALU.mult, ALU.add)
                    p.tt("pool", ob[:], ob[:], bvT[:, cl, tsl], ALU.add)
                    p.tt("pool", oT[:, 4 + cc, tsl], ob[:], gT[:, cl, tsl], ALU.mult)

    k.maybe_stop = maybe_stop
    try:
        maybe_stop("prepass")
        layer0()
        if n_layers > 1:
            layer1()
    except Stop:
        pass
    p.barrier()
    k.ninst = p.ninst
    return nc, k


def _consts():
    c = {}
    c["c_identb"] = np.eye(128, dtype=np.float32).astype(NPBF)
    c["c_identf"] = np.eye(128, dtype=np.float32)
    ob = np.zeros((128, 128), np.float32)
    ob[:64, :64] = 1
    ob[64:, 64:] = 1
    c["c_onesblk"] = ob
    m = np.ones((128, TT), np.float32)
    m[:, ::64] = 0
    c["c_mask01"] = m
    s = (np.arange(128) % 64)[:, None]
    t = np.arange(64)[None, :]
    c["c_maskG"] = np.concatenate([(s < t), (s <= t)], axis=1).astype(np.float32)
    c["c_maskAT"] = (t < s).astype(np.float32)
    slopes = 2.0 ** (-(np.arange(8) + 1.0))
    ql = (np.arange(TT) % 128).astype(np.float32)
    q = np.zeros((2, 8, TT), np.float32)
    q[0] = slopes[:, None]
    q[1] = -slopes[:, None] * ql[None, :]
    c["c_swa_q"] = q.reshape(2, 8 * TT).astype(NPBF)
    c["c_swa_ko"] = np.stack([ql, np.ones(TT, np.float32)]).astype(NPBF)
    c["c_swa_kp"] = np.stack([np.arange(128, dtype=np.float32) - 128, np.ones(128, np.float32)]).astype(NPBF)
    kl = np.arange(128)[:, None]
    qq = np.arange(128)[None, :]
    c["c_swa_mo"] = (kl <= qq).astype(np.float32).astype(NPBF)
    c["c_swa_mp"] = (kl > qq).astype(np.float32).astype(NPBF)
    return c


def _bf_split(v):
    hi = v.astype(NPBF)
    lo = (v - hi.astype(np.float32)).astype(NPBF)
    return hi, lo


def _consts_nsa(T):
    c = {}
    slopes = (2.0 ** (-8.0 * (np.arange(16) + 1.0) / 16)).astype(np.float32)
    s1, s2 = _bf_split(slopes)
    t = np.arange(T, dtype=np.float32)
    q = np.zeros((5, 16, T), np.float32)
    q[0] = s1.astype(np.float32)[:, None]
    q[1] = s2.astype(np.float32)[:, None]
    q[2] = q[0]
    q[3] = q[1]
    q[4] = -slopes[:, None] * t[None, :]
    c["c_nsa_q"] = q.reshape(5, 16 * T).astype(NPBF)

    def krows(pos):
        lo = pos % 128
        hi = pos - lo
        return np.stack([lo, lo, hi, hi, np.ones_like(pos)]).astype(np.float32)

    c["c_nsa_k"] = krows(t).astype(NPBF)
    sp_slot = -np.ones(384, np.int64)
    for ti in range(8):
        for j in range(32):
            sp_slot[(ti // 3) * 128 + 32 * (ti % 3) + j] = 32 * ti + j
    valid_sp = sp_slot >= 1
    posc = np.where(valid_sp, 16.0 * sp_slot + 15.0, 0.0).astype(np.float32)
    kr = krows(posc)
    kr[:, ~valid_sp] = 0
    c["c_cmp_k"] = kr.astype(NPBF)
    tt_ = np.arange(T)[None, :]
    msk = valid_sp[:, None] & (tt_ >= (16 * sp_slot[:, None] + 15))
    c["c_cmp_mask"] = msk.astype(np.float32).reshape(3, 128, T).astype(NPBF)
    mp_ = np.zeros((384, 64), np.float32)
    wts = {-1: 1.0, 0: 2.0, 1: 2.0, 2: 2.0, 3: 1.0}
    for sp in range(384):
        if sp_slot[sp] >= 1:
            cidx = sp_slot[sp] - 1
            for blk in range(64):
                m = cidx - 4 * blk
                if m in wts:
                    mp_[sp, blk] = wts[m]
    c["c_mpool"] = np.ascontiguousarray(mp_.reshape(3, 128, 64).transpose(1, 0, 2).reshape(128, 192)).astype(NPBF)
    F = np.zeros((T // 128, 128, 64), np.float32)
    for qt in range(T // 128):
        cur = (qt * 128 + np.arange(128)) // 64
        blk = np.arange(64)[None, :]
        forced = (blk == 0) | (blk == cur[:, None]) | (blk == cur[:, None] - 1)
        F[qt] = np.where(forced, 1e30, np.where(blk > cur[:, None], -1e30, 0.0))
    c["c_sel_F"] = F
    c["c_E"] = (np.arange(T)[None, :] // 64 == np.arange(64)[:, None]).astype(np.float32).astype(NPBF)
    return c


def _prep_shared(inp, n_layers=2):
    f = lambda a: np.ascontiguousarray(np.asarray(a, dtype=np.float32))
    sh = dict(_consts())
    T = np.asarray(inp["x"]).shape[1]
    gains = np.zeros((8, D), np.float32)
    for l in range(2):
        gains[l * 4 + 0] = inp["mix_pre_g"][l]
        gains[l * 4 + 1] = inp["mix_post_g"][l]
        gains[l * 4 + 2] = inp["ffn_pre_g"][l]
        gains[l * 4 + 3] = inp["ffn_post_g"][l]
    sh["gains"] = gains
    sh["hy_w_in"] = f(inp["hy_w_in"][0])
    sh["hy_w_out"] = f(inp["hy_w_out"][0])
    sh["ffn_w_up"] = f(inp["ffn_w_up"])
    sh["ffn_w_down"] = f(inp["ffn_w_down"])
    fp = np.zeros((2, 128, NFC, 4), np.float32)
    for l in range(2):
        cw = np.asarray(inp["ffn_conv_w"][l]).reshape(3, NFC, 128)
        fp[l, :, :, 0:3] = cw.transpose(2, 1, 0)
        fp[l, :, :, 3] = np.asarray(inp["ffn_conv_b"][l]).reshape(NFC, 128).T
    sh["ffn_pp"] = fp.reshape(2, 128, NFC * 4)
    pp = np.zeros((128, 42), np.float32)
    pp[:, 0:14] = np.asarray(inp["rwkv_mu"][0]).reshape(14, 128).T
    for i, nm in enumerate(["rwkv_w0", "rwkv_a0", "rwkv_k_k", "rwkv_k_a", "rwkv_r_k", "rwkv_ln_g", "rwkv_ln_b"]):
        pp[:, 14 + 4 * i:18 + 4 * i] = np.asarray(inp[nm][0]).reshape(4, 128).T
    sh["pp0"] = pp
    sh["swa_sinks"] = f(inp["swa_sinks"])
    sh["rwkv_w2"] = f(inp["rwkv_w2"][0])
    sh["rwkv_a2"] = f(inp["rwkv_a2"][0])
    sh["rwkv_g2"] = f(inp["rwkv_g2"][0])
    if n_layers > 1:
        sh["nsa_w_in"] = f(inp["nsa_w_in"][0])
        sh["nsa_w_out"] = f(inp["nsa_w_out"][0])
        sh["cmp_w1_k"] = f(inp["nsa_cmp_w1_k"][0]).reshape(2048, 256)
        sh["cmp_w1_v"] = f(inp["nsa_cmp_w1_v"][0]).reshape(2048, 256)
        sh["cmp_w2_k"] = f(inp["nsa_cmp_w2_k"][0])
        sh["cmp_w2_v"] = f(inp["nsa_cmp_w2_v"][0])
        pt = np.zeros((128, 32), np.float32)
        for kv, nm in enumerate(["nsa_cmp_pos_k", "nsa_cmp_pos_v"]):
            pos = np.asarray(inp[nm][0], np.float32)
            pt[:, kv * 16:(kv + 1) * 16] = pos.reshape(16, 2, 64).transpose(1, 2, 0).reshape(128, 16)
        sh["cmp_posT"] = pt
        sh.update(_consts_nsa(T))
    return sh


def kernel(**inputs):
    x = np.asarray(inputs["x"], dtype=np.float32)
    B, T, _ = x.shape
    n = 8
    NB = B // n
    nc, k = build(T=T, NB=NB, n_layers=2)
    sh = _prep_shared(inputs)
    in_maps = []
    for c in range(n):
        m = dict(sh)
        m["x"] = np.ascontiguousarray(x[c * NB:(c + 1) * NB])
        in_maps.append(m)
    res = run_bass_kernel_spmd(nc, in_maps, core_ids=list(range(n)))
    return np.concatenate([r["out"] for r in res.results], axis=0)
```

```python
import contextlib
import numpy as np
import ml_dtypes
import concourse.bass as bass
import concourse.mybir as mybir
from concourse.bass_utils import run_bass_kernel_spmd

F32 = mybir.dt.float32
BF16 = mybir.dt.bfloat16
AF = mybir.ActivationFunctionType
ALU = mybir.AluOpType
AX = mybir.AxisListType
NPBF = ml_dtypes.bfloat16

D = 1024
TT = 512
NS = 4
HD = 64
DFF = 2816
NFC = 22
HY_COLS = 2560
NSA_COLS = 2608
EPS = 1e-6
GN_EPS = 64e-5


class Prog:
    def __init__(self, nc, n_dma_sems=16):
        self.nc = nc
        self.eng = {"pe": nc.tensor, "act": nc.scalar, "dve": nc.vector, "pool": nc.gpsimd, "sp": nc.sync}
        self.psem = {k: nc.alloc_semaphore(name=f"prog_{k}") for k in self.eng}
        self.cnt = {k: 0 for k in self.eng}
        self.waited = {k: {} for k in self.eng}
        self.dsems = {}
        for q in ("sp", "act", "pool"):
            self.dsems[q] = [[nc.alloc_semaphore(name=f"dma_{q}_{i}"), 0] for i in range(n_dma_sems)]
        self.dnext = {q: 0 for q in self.dsems}
        self.bufs = {}
        self.ninst = 0
        self.pe_base = 0
        self.psum_names = set()

    def _st(self, ap):
        nm = ap.tensor.name
        st = self.bufs.get(nm)
        if st is None:
            st = {"w": None, "r": {}}
            self.bufs[nm] = st
        return st

    def _wait(self, e, deps):
        eng = self.eng[e]
        best = {}
        for d in deps:
            if d is None:
                continue
            s, v, key = d
            if key == e and e == "pe":
                continue
            if best.get(key, (None, 0))[1] < v:
                best[key] = (s, v)
        for key, (s, v) in best.items():
            if self.waited[e].get(key, 0) < v:
                eng.wait_ge(s, v)
                self.waited[e][key] = v

    def _deps(self, reads, writes):
        deps = []
        for ap in reads:
            st = self._st(ap)
            deps.append(st["w"])
            if ap.tensor.name in self.psum_names:
                deps.extend(st["r"].values())
        for ap in writes:
            st = self._st(ap)
            deps.append(st["w"])
            deps.extend(st["r"].values())
        return deps

    def _commit(self, ev, reads, writes):
        for ap in reads:
            self._st(ap)["r"][ev[2]] = ev
        for ap in writes:
            st = self._st(ap)
            st["w"] = ev
            st["r"] = {}

    def op(self, e, fn, *args, **kw):
        writes = [args[0]]
        reads = [a for a in args[1:] if isinstance(a, bass.AP)]
        for k, v in kw.items():
            if isinstance(v, bass.AP):
                (writes if k == "accum_out" else reads).append(v)
        self._wait(e, self._deps(reads, writes))
        if e == "pe":
            base = (fn, args[1].start_partition(), args[0].start_partition(), args[1].shape[0], args[1].shape[-1] if len(args[1].shape) == 2 else -1, args[0].tensor.name, str(args[1].dtype))
            if base != self.pe_base and self.cnt["pe"] > 0:
                self.eng["pe"].wait_ge(self.psem["pe"], self.cnt["pe"])
            self.pe_base = base
        ins = getattr(self.eng[e], fn)(*args, **kw)
        self.cnt[e] += 1
        ins.then_inc(self.psem[e], 1)
        self._commit((self.psem[e], self.cnt[e], e), reads, writes)
        self.ninst += 1
        return ins

    def dma(self, q, out, in_, **kw):
        reads, writes = [in_], [out]
        deps = self._deps(reads, writes)
        slot = self.dnext[q]
        self.dnext[q] = (slot + 1) % len(self.dsems[q])
        ent = self.dsems[q][slot]
        key = f"d_{q}_{slot}"
        if ent[1] > 0:
            deps.append((ent[0], ent[1], key))
        self._wait(q, deps)
        ins = self.eng[q].dma_start(out=out, in_=in_, **kw)
        assert ent[1] + 16 <= 240, "DMA semaphore would exceed its usable range; add a barrier()"
        ent[1] += 16
        ins.then_inc(ent[0], 16)
        self._commit((ent[0], ent[1], key), reads, writes)
        self.ninst += 1
        return ins

    def barrier(self):
        deps = []
        for qq, lst in self.dsems.items():
            for i, ent in enumerate(lst):
                if ent[1] > 0:
                    deps.append((ent[0], ent[1], f"d_{qq}_{i}"))
        for k in self.eng:
            if self.cnt[k] > 0:
                deps.append((self.psem[k], self.cnt[k], k))
        for e in self.eng:
            self._wait(e, deps)
        used = any(ent[1] > 0 for lst in self.dsems.values() for ent in lst)
        if used:
            arr = []
            for e in self.eng:
                if e == "sp":
                    continue
                self.eng[e].sem_inc(self.psem[e], 1)
                self.cnt[e] += 1
                arr.append((self.psem[e], self.cnt[e], e))
            self._wait("sp", arr)
            for qq, lst in self.dsems.items():
                for i, ent in enumerate(lst):
                    if ent[1] > 0:
                        self.eng["sp"].sem_clear(ent[0])
                        ent[1] = 0
                    for e in self.eng:
                        self.waited[e].pop(f"d_{qq}_{i}", None)
            self.eng["sp"].sem_inc(self.psem["sp"], 1)
            self.cnt["sp"] += 1
            ev = (self.psem["sp"], self.cnt["sp"], "sp")
            for e in self.eng:
                if e != "sp":
                    self._wait(e, [ev])
        self.bufs = {}

    def mm(self, out, lhsT, rhs, start=True, stop=True, sgc=False):
        if sgc:
            return self.op("pe", "matmul", out, lhsT, rhs, start=start, stop=stop, skip_group_check=True)
        return self.op("pe", "matmul", out, lhsT, rhs, start=start, stop=stop)

    def tr(self, out, in_, ident):
        return self.op("pe", "transpose", out, in_, ident)

    def act(self, out, in_, func=AF.Copy, **kw):
        return self.op("act", "activation", out, in_, func, **kw)

    def tt(self, e, out, in0, in1, op):
        return self.op(e, "tensor_tensor", out, in0, in1, op)

    def ts(self, e, out, in0, s1, s2, op0, op1=None):
        if op1 is None:
            return self.op(e, "tensor_scalar", out, in0, s1, None, op0)
        return self.op(e, "tensor_scalar", out, in0, s1, s2, op0, op1)

    def stt(self, e, out, in0, scalar, in1, op0, op1):
        return self.op("dve", "scalar_tensor_tensor", out, in0, scalar, in1, op0, op1)

    def cp(self, e, out, in_):
        if e == "act":
            return self.act(out, in_, AF.Copy)
        return self.op(e, "tensor_copy", out, in_)


def bc(ap, shape):
    return ap.broadcast_to(list(shape))


class K:
    pass


def build(T=4096, NB=2, n_layers=2, debug=(), stop=None):
    nc = bass.Bass("TRN2", target_bir_lowering=False)
    p = Prog(nc)
    NT = T // TT
    k = K()
    k.nc, k.p, k.T, k.NB, k.NT = nc, p, T, NB, NT
    dbg = {}

    def din(name, shape, dt=F32):
        return nc.dram_tensor(name, list(shape), dt, kind="ExternalInput").ap()

    def dscr(name, shape, dt):
        return nc.dram_tensor(name, list(shape), dt, kind="Internal").ap()

    x = din("x", [NB, T, D])
    out = nc.dram_tensor("out", [NB, T, D], F32, kind="ExternalOutput").ap()
    gains = din("gains", [8, D])
    w_in0 = din("hy_w_in", [D, HY_COLS])
    w_out0 = din("hy_w_out", [D, D])
    w_up = din("ffn_w_up", [2, D, 2 * DFF])
    w_down = din("ffn_w_down", [2, DFF, D])
    ffn_pp = din("ffn_pp", [2, 128, NFC * 4])
    pp0 = din("pp0", [128, 42])
    sinks = din("swa_sinks", [1, 8])
    rw_w2 = din("rwkv_w2", [64, 512])
    rw_a2 = din("rwkv_a2", [64, 512])
    rw_g2 = din("rwkv_g2", [128, 512])
    c_identb = din("c_identb", [128, 128], BF16)
    c_identf = din("c_identf", [128, 128])
    c_onesblk = din("c_onesblk", [128, 128])
    c_mask01 = din("c_mask01", [128, TT])
    c_maskG = din("c_maskG", [128, 128])
    c_maskAT = din("c_maskAT", [128, 64])
    c_swa_q = din("c_swa_q", [2, 8 * TT], BF16)
    c_swa_ko = din("c_swa_ko", [2, TT], BF16)
    c_swa_kp = din("c_swa_kp", [2, 128], BF16)
    c_swa_mo = din("c_swa_mo", [128, 128], BF16)
    c_swa_mp = din("c_swa_mp", [128, 128], BF16)
    if n_layers > 1:
        nsa_w_in = din("nsa_w_in", [D, NSA_COLS])
        nsa_w_out = din("nsa_w_out", [D, D])
        cw1 = [din("cmp_w1_k", [2048, 256]), din("cmp_w1_v", [2048, 256])]
        cw2 = [din("cmp_w2_k", [256, 64]), din("cmp_w2_v", [256, 64])]
        cposT = din("cmp_posT", [128, 32])
        c_nsa_q = din("c_nsa_q", [5, 16 * T], BF16)
        c_nsa_k = din("c_nsa_k", [5, T], BF16)
        c_cmp_k = din("c_cmp_k", [5, 384], BF16)
        c_cmp_mask = din("c_cmp_mask", [3, 128, T], BF16)
        c_mpool = din("c_mpool", [128, 3 * 64], BF16)
        c_sel_F = din("c_sel_F", [T // 128, 128, 64])
        c_E = din("c_E", [64, T], BF16)
        nsa_w_inb = dscr("nsa_w_inb", [D, NSA_COLS], BF16)
        nsa_w_outb = dscr("nsa_w_outb", [D, D], BF16)
        cw1b = [dscr("cmp_w1_kb", [2048, 256], BF16), dscr("cmp_w1_vb", [2048, 256], BF16)]

    w_in0b = dscr("w_in0b", [D, HY_COLS], BF16)
    w_out0b = dscr("w_out0b", [D, D], BF16)
    w_upb = dscr("w_upb", [2, D, 2 * DFF], BF16)
    w_downb = dscr("w_downb", [2, DFF, D], BF16)
    x1 = dscr("x1", [NB, T, D], F32)

    for nm in debug:
        dbg[nm] = None
    k.dbg_out = {}

    def dbg_dump(name, shape, src_ap, dt=F32):
        if name not in debug:
            return
        if name not in k.dbg_out:
            k.dbg_out[name] = nc.dram_tensor("dbg_" + name, list(shape), dt, kind="ExternalOutput").ap()
        return k.dbg_out[name]

    es = contextlib.ExitStack()

    uid = [0]

    def sb(name, shape, dt, stack=None):
        uid[0] += 1
        return (stack or es).enter_context(nc.sbuf_tensor(f"{name}_{uid[0]}", list(shape), dt))

    def ps(name, shape, dt=F32, stack=None):
        uid[0] += 1
        p.psum_names.add(f"{name}_{uid[0]}")
        return (stack or es).enter_context(nc.psum_tensor(f"{name}_{uid[0]}", list(shape), dt))

    with contextlib.ExitStack() as stc:
        cin = [sb(f"cast_in{i}", [128, 2 * DFF], F32, stc) for i in range(2)]
        cout = [sb(f"cast_out{i}", [128, 2 * DFF], BF16, stc) for i in range(2)]
        ccnt = [0]

        def cast_w(dst, src, rows, cols):
            for r0 in range(0, rows, 128):
                i = ccnt[0] % 2
                ccnt[0] += 1
                p.dma("sp", cin[i][:, 0:cols], src[r0:r0 + 128, :])
                h = cols // 2
                p.cp("dve", cout[i][:, 0:h], cin[i][:, 0:h])
                p.cp("pool", cout[i][:, h:cols], cin[i][:, h:cols])
                p.dma("act", dst[r0:r0 + 128, :], cout[i][:, 0:cols])

        cast_w(w_in0b, w_in0, D, HY_COLS)
        cast_w(w_out0b, w_out0, D, D)
        for l in range(n_layers):
            cast_w(w_upb[l], w_up[l], D, 2 * DFF)
            cast_w(w_downb[l], w_down[l], DFF, D)
        p.barrier()
        if n_layers > 1:
            cast_w(nsa_w_inb, nsa_w_in, D, NSA_COLS)
            cast_w(nsa_w_outb, nsa_w_out, D, D)
            for kv in range(2):
                cast_w(cw1b[kv], cw1[kv], 2048, 256)
            p.barrier()
    k.stop = stop

    class Stop(Exception):
        pass

    def maybe_stop(tag, src_ap=None):
        if k.stop == tag:
            if src_ap is not None:
                p.dma("sp", out[0, 0:TT, :].rearrange("(s p) d -> p s d", p=128), src_ap)
            p.barrier()
            raise Stop()

    identb = sb("identb", [128, 128], BF16)
    identf = sb("identf", [128, 128], F32)
    onesblk = sb("onesblk", [128, 128], F32)
    mask01 = sb("mask01", [128, TT], F32)
    maskG = sb("maskG", [128, 128], F32)
    maskAT = sb("maskAT", [128, 64], F32)
    mo = sb("swa_mo", [128, 128], BF16)
    mp = sb("swa_mp", [128, 128], BF16)
    gbc = sb("gbc", [128, 4, D], F32)
    ppt = sb("ppt", [128, 42], F32)
    omu = sb("omu", [128, 14], F32)
    omka = sb("omka", [128, 4], F32)
    fpp = sb("fpp", [128, NFC * 4], F32)
    esink = sb("esink", [128, 8], F32)
    for dst, src in ((identb, c_identb), (identf, c_identf), (onesblk, c_onesblk), (mask01, c_mask01),
                     (maskG, c_maskG), (maskAT, c_maskAT), (mo, c_swa_mo), (mp, c_swa_mp), (ppt, pp0)):
        p.dma("sp", dst[:], src)
    p.dma("sp", esink[:], sinks[0].partition_broadcast(128))
    p.act(esink[:], esink[:], AF.Exp)
    p.ts("dve", omu[:], ppt[:, 0:14], -1.0, 1.0, ALU.mult, ALU.add)
    p.ts("dve", omka[:], ppt[:, 26:30], -1.0, 1.0, ALU.mult, ALU.add)
    MU, W0, A0, KK, KA, RK, LG, LB = 0, 14, 18, 22, 26, 30, 34, 38

    w2b = sb("w2b", [128, 512], BF16)
    g2b = sb("g2b", [128, 512], BF16)
    with contextlib.ExitStack() as st0:
        tmpf = sb("tmpf_l", [128, 512], F32, st0)
        p.dma("sp", tmpf[0:64, :], rw_w2)
        p.dma("sp", tmpf[64:128, :], rw_a2)
        p.cp("dve", w2b[:], tmpf[:])
        tmpg = sb("tmpg_l", [128, 512], F32, st0)
        p.dma("sp", tmpg[:], rw_g2)
        p.cp("dve", g2b[:], tmpg[:])
        p.barrier()

    def load_gains(layer):
        for i in range(4):
            p.dma("sp", gbc[:, i, :], gains[layer * 4 + i].partition_broadcast(128))

    def rmsnorm_to_hT(src, gi, hT, stk, pst):
        junk = sb("nrm_junk", [128, D], BF16, stk)
        ss = sb("nrm_ss", [128, NS], F32, stk)
        hb = sb("nrm_hb", [128, NS, D], BF16, stk)
        for s in range(NS):
            p.act(junk[:], src[:, s, :], AF.Square, accum_out=ss[:, s:s + 1])
        p.ts("dve", ss[:], ss[:], 1.0 / D, EPS, ALU.mult, ALU.add)
        p.act(ss[:], ss[:], AF.Sqrt)
        p.op("dve", "reciprocal", ss[:], ss[:])
        for s in range(NS):
            p.stt("dve" if s % 2 == 0 else "pool", hb[:, s, :], src[:, s, :], ss[:, s:s + 1], gbc[:, gi, :],
                  ALU.mult, ALU.mult)
        for kc in range(8):
            pt = pst[kc % 2]
            for s in range(NS):
                p.tr(pt[:, s * 128:(s + 1) * 128], hb[:, s, kc * 128:(kc + 1) * 128], identb[:])
            p.cp("act" if kc % 2 == 0 else "dve", hT[:, kc, :], pt[:, 0:TT])

    def post_norm_residual(pm, xres, gi, s, stk_tiles):
        junk, ss1, tmp = stk_tiles
        p.act(junk[:], pm, AF.Square, accum_out=ss1[:, s:s + 1])
        p.ts("dve", ss1[:, s:s + 1], ss1[:, s:s + 1], 1.0 / D, EPS, ALU.mult, ALU.add)
        p.act(ss1[:, s:s + 1], ss1[:, s:s + 1], AF.Sqrt)
        p.op("dve", "reciprocal", ss1[:, s:s + 1], ss1[:, s:s + 1])
        p.stt("dve", tmp[:], pm, ss1[:, s:s + 1], gbc[:, gi, :], ALU.mult, ALU.mult)
        p.tt("pool", xres[:, s, :], xres[:, s, :], tmp[:], ALU.add)

    def ffn(layer, xres, gcarry, first_tile):
        with contextlib.ExitStack() as stk:
            aT = sb("f_aT", [128, NFC, TT], BF16, stk)
            with contextlib.ExitStack() as stk2:
                pst = [ps(f"f_pt{i}", [128, 2 * TT], BF16, stk2) for i in range(2)]
                hT = sb("f_hT", [128, 8, TT], BF16, stk2)
                rmsnorm_to_hT(xres, 2, hT, stk2, pst)
                k.maybe_stop("f1")
                wu = [sb(f"f_wu{i}", [128, 8, 256], BF16, stk2) for i in range(3)]
                gb = [sb(f"f_gb{i}", [128, TT + 2], F32, stk2) for i in range(2)]
                cv = [sb(f"f_cv{i}", [128, TT], F32, stk2) for i in range(2)]
                ge = [sb(f"f_ge{i}", [128, TT], F32, stk2) for i in range(2)]
                pg = [ps(f"f_pg{i}", [128, TT], F32, stk2) for i in range(2)]
                pv = [ps(f"f_pv{i}", [128, TT], F32, stk2) for i in range(2)]
                wup = w_upb[layer].rearrange("(kc p) c -> p kc c", p=128)
                for fc in range(NFC):
                    w = wu[fc % 3]
                    p.dma("sp", w[:, :, 0:128], wup[:, :, fc * 128:(fc + 1) * 128])
                    p.dma("sp", w[:, :, 128:256], wup[:, :, DFF + fc * 128:DFF + (fc + 1) * 128])
                    g_, v_, b_, c_, e_ = pg[fc % 2], pv[fc % 2], gb[fc % 2], cv[fc % 2], ge[fc % 2]
                    for kc in range(8):
                        p.mm(g_[:], w[:, kc, 0:128], hT[:, kc, :], start=(kc == 0), stop=(kc == 7))
                    for kc in range(8):
                        p.mm(v_[:], w[:, kc, 128:256], hT[:, kc, :], start=(kc == 0), stop=(kc == 7))
                    if first_tile:
                        p.op("pool", "memset", b_[:, 0:2], 0.0)
                    else:
                        p.cp("pool", b_[:, 0:2], gcarry[:, fc, :])
                    p.cp("dve", b_[:, 2:TT + 2], g_[:])
                    p.cp("pool", gcarry[:, fc, :], b_[:, TT:TT + 2])
                    pc = fpp[:, fc * 4:fc * 4 + 4]
                    p.ts("dve", c_[:], g_[:], pc[:, 2:3], pc[:, 3:4], ALU.mult, ALU.add)
                    p.stt("dve", c_[:], b_[:, 1:TT + 1], pc[:, 1:2], c_[:], ALU.mult, ALU.add)
                    p.stt("pool", c_[:], b_[:, 0:TT], pc[:, 0:1], c_[:], ALU.mult, ALU.add)
                    p.act(e_[:], c_[:], AF.Gelu_apprx_tanh)
                    p.tt("dve", aT[:, fc, :], e_[:], v_[:], ALU.mult)
                    k.maybe_stop("f2")
                p.barrier()
            pacc = [ps(f"f_pa{i}", [128, 2, 512], F32, stk) for i in range(2)]
            k.maybe_stop("f3")
            wd = [sb(f"f_wd{i}", [128, 2, D], BF16, stk) for i in range(3)]
            junk = sb("f_junk", [128, D], BF16, stk)
            ss1 = sb("f_ss1", [128, NS], F32, stk)
            tmp = sb("f_tmp", [128, D], F32, stk)
            wdn = w_downb[layer].rearrange("(fc p) c -> p fc c", p=128)
            wi = 0
            for half in range(2):
                for f0 in range(0, NFC, 2):
                    w = wd[wi % 3]
                    wi += 1
                    p.dma("sp", w[:], wdn[:, f0:f0 + 2, :])
                    for ff in range(2):
                        fc = f0 + ff
                        for si in range(2):
                            s = half * 2 + si
                            for nh in range(2):
                                p.mm(pacc[si][:, nh, :], aT[:, fc, s * 128:(s + 1) * 128], w[:, ff, nh * 512:(nh + 1) * 512],
                                     start=(fc == 0), stop=(fc == NFC - 1))
                for si in range(2):
                    post_norm_residual(pacc[si][:], xres, 3, half * 2 + si, (junk, ss1, tmp))
                    k.maybe_stop("f4")
            p.barrier()

    def layer0():
        with contextlib.ExitStack() as L0:
            layer0_body(L0)
            p.barrier()

    def layer0_body(L0):
        k.zs = [sb(f"zs{j}", [128, TT], F32, L0) for j in range(14)]
        load_gains(0)
        p.dma("sp", fpp[:], ffn_pp[0])
        KaO = sb("KaO", [66, 2, TT], BF16, L0)
        KaP = sb("KaP", [66, 2, TT + 128], BF16, L0)
        Vsw = sb("Vsw", [128, NS + 1, 2, 65], BF16, L0)
        for g in range(2):
            p.dma("sp", KaO[64:66, g, :], c_swa_ko)
            for s in range(NS + 1):
                p.dma("sp", KaP[64:66, g, s * 128:(s + 1) * 128], c_swa_kp)
        p.op("pool", "memset", Vsw[:, :, :, 64:65], 1.0)
        zcarry = sb("zcarry", [128, 14], F32, L0)
        ST = sb("ST", [128, 4, 64], F32, L0)
        gcarry = sb("gcarry", [128, NFC, 2], F32, L0)
        oT = sb("oT", [128, 8, TT], BF16, L0)
        xt = sb("xt", [128, NS, D], F32, L0)
        win = w_in0b.rearrange("(kc p) c -> p kc c", p=128)

        for b in range(NB):
            for ti in range(NT):
                first = (ti == 0)
                t0 = ti * TT
                with contextlib.ExitStack() as stk:
                    pst = [ps(f"p1_pt{i}", [128, 2 * TT], BF16, stk) for i in range(2)]
                    pz = [ps(f"p1_pz{i}", [128, TT], F32, stk) for i in range(2)]
                    Qa = sb("Qa", [66, 8, TT], BF16, stk)
                    p.dma("sp", Qa[64:66, :, :], c_swa_q.rearrange("r (h t) -> r h t", h=8))
                    p.dma("sp", xt[:], x[b, t0:t0 + TT, :].rearrange("(s p) d -> p s d", p=128))
                    hT = sb("p1_hT", [128, 8, TT], BF16, stk)
                    rmsnorm_to_hT(xt, 0, hT, stk, pst)
                    maybe_stop("p1a", xt[:])
                    wq = [sb(f"p1_w{i}", [128, 8, 128], BF16, stk) for i in range(4)]
                    wcnt = [0]

                    def wload(c0, n):
                        w = wq[wcnt[0] % 4]
                        wcnt[0] += 1
                        p.dma("sp", w[:, :, 0:n], win[:, :, c0:c0 + n])
                        return w

                    def proj_fm(w, c0, n, dst_ps):
                        for kc in range(8):
                            p.mm(dst_ps, w[:, kc, c0:c0 + n], hT[:, kc, :], start=(kc == 0), stop=(kc == 7))

                    for hp in range(4):
                        w = wload(hp * 128, 128)
                        for hh in range(2):
                            h = hp * 2 + hh
                            z = pz[h % 2]
                            proj_fm(w, hh * 64, 64, z[0:64, :])
                            p.act(Qa[0:64, h, :], z[0:64, :], AF.Copy, scale=0.125)
                    maybe_stop("p1b1", xt[:])
                    if not first:
                        for g in range(2):
                            p.cp("pool", KaP[0:64, g, 0:128], KaP[0:64, g, TT:TT + 128])
                        p.cp("pool", Vsw[:, 0, :, 0:64], Vsw[:, NS, :, 0:64])
                    w = wload(512, 128)
                    for g in range(2):
                        z = pz[g % 2]
                        proj_fm(w, g * 64, 64, z[0:64, :])
                        p.cp("dve", KaO[0:64, g, :], z[0:64, :])
                        p.cp("dve", KaP[0:64, g, 128:128 + TT], z[0:64, :])
                    maybe_stop("p1b2", xt[:])
                    w = wload(640, 128)
                    for s in range(NS):
                        z = pz[s % 2]
                        for kc in range(8):
                            p.mm(z[:, 0:128], hT[:, kc, s * 128:(s + 1) * 128], w[:, kc, :], start=(kc == 0), stop=(kc == 7))
                        p.cp("dve", Vsw[:, s + 1, :, 0:64], z[:, 0:128].rearrange("p (g d) -> p g d", g=2))
                    maybe_stop("p1b", xt[:])
                    zs = k.zs
                    zb = [sb(f"p1_zb{i}", [128, TT + 1], F32, stk) for i in range(2)]
                    ztmp = [sb(f"p1_zt{i}", [128, TT], F32, stk) for i in range(2)]
                    for j in range(14):
                        w = wload(768 + j * 128, 128)
                        z = pz[j % 2]
                        proj_fm(w, 0, 128, z[:])
                        b_ = zb[j % 2]
                        tm = ztmp[j % 2]
                        p.act(b_[:, 1:TT + 1], z[:], AF.Copy)
                        if first:
                            p.op("pool", "memset", b_[:, 0:1], 0.0)
                        else:
                            p.cp("pool", b_[:, 0:1], zcarry[:, j:j + 1])
                        p.cp("pool", zcarry[:, j:j + 1], b_[:, TT:TT + 1])
                        p.ts("dve", tm[:], b_[:, 0:TT], ppt[:, MU + j:MU + j + 1], None, ALU.mult)
                        p.stt("dve", zs[j][:], b_[:, 1:TT + 1], omu[:, j:j + 1], tm[:], ALU.mult, ALU.add)
                    maybe_stop("p1c", xt[:])
                    pss = [ps(f"p1_ps{i}", [128, 4, 128], F32, stk) for i in range(2)]
                    po = [ps(f"p1_po{i}", [128, 4, 65], F32, stk) for i in range(2)]
                    pe_ = [sb(f"p1_pe{i}", [128, 4, 128], BF16, stk) for i in range(3)]
                    oa = sb("p1_oa", [128, 8, 64], BF16, stk)
                    den = sb("p1_den", [128, 8], F32, stk)
                    ecnt = 0
                    for s in range(NS):
                        has_prev = not (first and s == 0)
                        for g in range(2):
                            pog = po[g]
                            steps = ([(KaP[:, g, s * 128:(s + 1) * 128], Vsw[:, s, g, :], mp)] if has_prev else []) + \
                                    [(KaO[:, g, s * 128:(s + 1) * 128], Vsw[:, s + 1, g, :], mo)]
                            for si, (kap, vap, msk) in enumerate(steps):
                                sc = pss[ecnt % 2]
                                pt_ = pe_[ecnt % 3]
                                ecnt += 1
                                p.mm(sc[:], kap, Qa[:, g * 4:(g + 1) * 4, s * 128:(s + 1) * 128])
                                p.act(pt_[:], sc[:], AF.Exp)
                                p.tt("dve", pt_[:], pt_[:], bc(msk[:].unsqueeze(1), [128, 4, 128]), ALU.mult)
                                for hh in range(4):
                                    p.mm(pog[:, hh, :], pt_[:, hh, :], vap, start=(si == 0 and hh == 0), stop=(si == len(steps) - 1), sgc=True)
                            p.tt("dve", den[:, g * 4:(g + 1) * 4], pog[:, :, 64], esink[:, g * 4:(g + 1) * 4], ALU.add)
                            p.op("dve", "reciprocal", den[:, g * 4:(g + 1) * 4], den[:, g * 4:(g + 1) * 4])
                            p.tt("dve", oa[:, g * 4:(g + 1) * 4, :], pog[:, :, 0:64],
                                 bc(den[:, g * 4:(g + 1) * 4].unsqueeze(2), [128, 4, 64]), ALU.mult)
                        pt = pst[s % 2]
                        for kc in range(4):
                            p.tr(pt[:, kc * 128:(kc + 1) * 128], oa[:, 2 * kc:2 * kc + 2, :].rearrange("p h d -> p (h d)"), identb[:])
                        p.cp("act", oT[:, 0:4, s * 128:(s + 1) * 128], pt[:, 0:512].rearrange("p (c t) -> p c t", c=4))
                    p.barrier()
                maybe_stop("p1", xt[:])
                for hg in range(2):
                    with contextlib.ExitStack() as stk:
                        rwkv_tile(first, oT, ST, stk, hg)
                        p.barrier()
                maybe_stop("p2", xt[:])
                with contextlib.ExitStack() as stk:
                    pacc = [ps(f"p3_pa{i}", [128, 2, 512], F32, stk) for i in range(2)]
                    wout = sb("wout", [128, 8, D], BF16, stk)
                    p.dma("sp", wout[:], w_out0b.rearrange("(kc p) c -> p kc c", p=128))
                    junk = sb("p3_junk", [128, D], BF16, stk)
                    ss1 = sb("p3_ss1", [128, NS], F32, stk)
                    tmp = sb("p3_tmp", [128, D], F32, stk)
                    for s in range(NS):
                        pm = pacc[s % 2]
                        for nh in range(2):
                            for kc in range(8):
                                p.mm(pm[:, nh, :], oT[:, kc, s * 128:(s + 1) * 128], wout[:, kc, nh * 512:(nh + 1) * 512],
                                     start=(kc == 0), stop=(kc == 7))
                        post_norm_residual(pm[:], xt, 1, s, (junk, ss1, tmp))
                    p.barrier()
                maybe_stop("p3", xt[:])
                ffn(0, xt, gcarry, first)
                dst = x1 if n_layers > 1 else out
                p.dma("sp", dst[b, t0:t0 + TT, :].rearrange("(s p) d -> p s d", p=128), xt[:])
                p.barrier()


    def layer1():
        with contextlib.ExitStack() as L1:
            layer1_body(L1)
            p.barrier()

    def layer1_body(L1):
        load_gains(1)
        p.dma("sp", fpp[:], ffn_pp[1])
        NKT = T // 128
        xt = sb("n_xt", [128, NS, D], F32, L1)
        oT = sb("n_oT", [128, 8, TT], BF16, L1)
        gcarry = sb("n_gcarry", [128, NFC, 2], F32, L1)
        Ks = sb("n_Ks", [69, 4, T], BF16, L1)
        Vs = sb("n_Vs", [128, NKT, 4, 65], BF16, L1)
        Kw = sb("n_Kw", [69, 4, 6 * 128], BF16, L1)
        Vw = sb("n_Vw", [128, 6, 4, 65], BF16, L1)
        Kc = sb("n_Kc", [69, 4, 3, 128], BF16, L1)
        Vc = sb("n_Vc", [128, 3, 4, 129], BF16, L1)
        KC2 = sb("n_KC2", [128, 8, 16 + TT], BF16, L1)
        Ebl = sb("n_E", [64, T], BF16, L1)
        hbias = sb("n_hbias", [128, 4], F32, L1)
        w2c = sb("n_w2c", [128, 2, 2, 64], BF16, L1)
        p.dma("sp", Ebl[:], c_E)
        p.op("pool", "memset", Vs[:, :, :, 64:65], 1.0)
        p.op("pool", "memset", Vw[:, :, :, 64:65], 1.0)
        p.op("pool", "memset", Kc[:], 0.0)
        p.op("pool", "memset", Vc[:], 0.0)
        for g in range(4):
            p.dma("sp", Ks[64:69, g, :], c_nsa_k)
            p.dma("sp", Kc[64:69, g, :, :], c_cmp_k.rearrange("r (c s) -> r c s", c=3))
            for ch in range(3):
                p.dma("sp", Vc[:, ch, g, 65:129], c_mpool[:, ch * 64:(ch + 1) * 64])
        p.op("pool", "memset", Vc[:, :, :, 64:65], 1.0)
        win = nsa_w_inb.rearrange("(kc p) c -> p kc c", p=128)
        with contextlib.ExitStack() as stk:
            w1t = sb("n_w1t", [128, 16, 256], BF16, stk)
            posf = sb("n_posf", [128, 32], F32, stk)
            posb = sb("n_posb", [128, 32], BF16, stk)
            w2f = sb("n_w2f", [128, 2, 2, 64], F32, stk)
            pb_ = ps("n_pb", [128, 8], F32, stk)
            p.dma("sp", posf[:], cposT)
            p.cp("dve", posb[:], posf[:])
            for kv in range(2):
                p.dma("sp", w2f[:, kv, :, :], cw2[kv].rearrange("(c p) d -> p c d", p=128))
            p.cp("dve", w2c[:], w2f[:])
            for kv in range(2):
                p.dma("sp", w1t[:], cw1b[kv].rearrange("(m p) f -> p m f", p=128))
                for fx in range(2):
                    for m in range(16):
                        p.mm(pb_[:, kv * 2 + fx:kv * 2 + fx + 1], w1t[:, m, fx * 128:(fx + 1) * 128],
                             posb[:, kv * 16 + m:kv * 16 + m + 1], start=(m == 0), stop=(m == 15))
            p.cp("dve", hbias[:], pb_[:, 0:4])
            p.barrier()

        for b in range(NB):
            for ti in range(NT):
                first = (ti == 0)
                t0 = ti * TT
                with contextlib.ExitStack() as stkQ:
                    Qa = sb("n_Qa", [69, 16, TT], BF16, stkQ)
                    gates = sb("n_gates", [128, NS, 48], F32, stkQ)
                    with contextlib.ExitStack() as stk:
                        pst = [ps(f"n1_pt{i}", [128, 2 * TT], BF16, stk) for i in range(2)]
                        pz = [ps(f"n1_pz{i}", [128, TT], F32, stk) for i in range(2)]
                        p.dma("sp", Qa[64:69, :, :], c_nsa_q.rearrange("r (h t) -> r h t", h=16)[:, :, t0:t0 + TT])
                        p.dma("sp", xt[:], x1[b, t0:t0 + TT, :].rearrange("(s p) d -> p s d", p=128))
                        hT = sb("n1_hT", [128, 8, TT], BF16, stk)
                        rmsnorm_to_hT(xt, 0, hT, stk, pst)
                        wq = [sb(f"n1_w{i}", [128, 8, 128], BF16, stk) for i in range(4)]
                        wcnt = [0]

                        def wload(cols):
                            w = wq[wcnt[0] % 4]
                            wcnt[0] += 1
                            o = 0
                            for c0, n in cols:
                                p.dma("sp", w[:, :, o:o + n], win[:, :, c0:c0 + n])
                                o += n
                            return w

                        def proj_fm(w, c0, n, dst_ps):
                            for kc in range(8):
                                p.mm(dst_ps, w[:, kc, c0:c0 + n], hT[:, kc, :], start=(kc == 0), stop=(kc == 7))

                        zi = 0
                        for hp in range(8):
                            w = wload([(hp * 128, 128)])
                            for hh in range(2):
                                z = pz[zi % 2]; zi += 1
                                proj_fm(w, hh * 64, 64, z[0:64, :])
                                p.act(Qa[0:64, hp * 2 + hh, :], z[0:64, :], AF.Copy, scale=0.125)
                        for kv in range(2):
                            for g in range(4):
                                c0 = 1024 + kv * 256 + g * 64
                                w = wload([(c0, 64), (c0, 64)])
                                z = pz[zi % 2]; zi += 1
                                proj_fm(w, 0, 128, z[:])
                                kt_ = KC2[:, kv * 4 + g, :]
                                if first:
                                    p.op("pool", "memset", kt_[:, 0:16], 0.0)
                                else:
                                    p.cp("pool", kt_[0:64, 0:16], kt_[0:64, TT:TT + 16])
                                    p.cp("pool", kt_[64:128, 0:15], kt_[64:128, TT:TT + 15])
                                p.cp("dve", kt_[0:64, 16:16 + TT], z[0:64, :])
                                p.cp("dve", kt_[64:128, 15:15 + TT], z[64:128, :])
                        for which, dstK, cbase in (("s", Ks, 1536), ("w", Kw, 2048)):
                            if which == "w":
                                if not first:
                                    p.cp("pool", Kw[:, :, 0:256], Kw[:, :, 512:768])
                                    p.cp("pool", Vw[:, 0:2, :, 0:64], Vw[:, 4:6, :, 0:64])
                                for g in range(4):
                                    p.dma("sp", Kw[64:69, g, 256:768], c_nsa_k[:, t0:t0 + TT])
                                    if not first:
                                        p.dma("sp", Kw[64:69, g, 0:256], c_nsa_k[:, t0 - 256:t0])
                            for gp in range(2):
                                w = wload([(cbase + gp * 128, 128)])
                                for gg in range(2):
                                    g = gp * 2 + gg
                                    z = pz[zi % 2]; zi += 1
                                    proj_fm(w, gg * 64, 64, z[0:64, :])
                                    if which == "s":
                                        p.cp("dve", Ks[0:64, g, t0:t0 + TT], z[0:64, :])
                                    else:
                                        p.cp("dve", Kw[0:64, g, 256:768], z[0:64, :])
                        for which, cbase in (("s", 1792), ("w", 2304)):
                            wa = wload([(cbase, 128)])
                            wb_ = wload([(cbase + 128, 128)])
                            for s in range(NS):
                                z = pz[zi % 2]; zi += 1
                                for half, w in enumerate((wa, wb_)):
                                    for kc in range(8):
                                        p.mm(z[:, half * 128:(half + 1) * 128], hT[:, kc, s * 128:(s + 1) * 128], w[:, kc, :],
                                             start=(kc == 0), stop=(kc == 7))
                                src_ = z[:, 0:256].rearrange("p (g d) -> p g d", g=4)
                                if which == "s":
                                    p.cp("dve", Vs[:, ti * NS + s, :, 0:64], src_)
                                else:
                                    p.cp("dve", Vw[:, 2 + s, :, 0:64], src_)
                        w = wload([(2560, 48)])
                        for s in range(NS):
                            z = pz[zi % 2]; zi += 1
                            for kc in range(8):
                                p.mm(z[:, 0:48], hT[:, kc, s * 128:(s + 1) * 128], w[:, kc, 0:48], start=(kc == 0), stop=(kc == 7))
                            p.act(gates[:, s, :], z[:, 0:48], AF.Sigmoid)
                        p.barrier()
                    with contextlib.ExitStack() as stk:
                        w1t = sb("nc_w1t", [128, 16, 256], BF16, stk)
                        hid = sb("nc_hid", [128, 2, 32], BF16, stk)
                        ph = [ps(f"nc_ph{i}", [128, 32], F32, stk) for i in range(2)]
                        pk = ps("nc_pk", [128, 64], F32, stk)
                        chn, q3 = ti // 3, ti % 3
                        for kv in range(2):
                            p.dma("sp", w1t[:], cw1b[kv].rearrange("(m p) f -> p m f", p=128))
                            for g in range(4):
                                kt_ = KC2[:, kv * 4 + g, :]
                                for fx in range(2):
                                    for m in range(16):
                                        if 2 * m < 16:
                                            rhs = kt_[:, 0:TT].rearrange("p (j s) -> p j s", s=16)[:, :, 2 * m]
                                        else:
                                            rhs = kt_[:, 16:16 + TT].rearrange("p (j s) -> p j s", s=16)[:, :, 2 * m - 16]
                                        p.mm(ph[fx][:], w1t[:, m, fx * 128:(fx + 1) * 128], rhs, start=(m == 0), stop=(m == 15))
                                    p.act(hid[:, fx, :], ph[fx][:], AF.Gelu_apprx_tanh, bias=hbias[:, kv * 2 + fx:kv * 2 + fx + 1])
                                if kv == 0:
                                    for fx in range(2):
                                        p.mm(pk[0:64, 0:32], w2c[:, 0, fx, :], hid[:, fx, :], start=(fx == 0), stop=(fx == 1))
                                    p.cp("dve", Kc[0:64, g, chn, q3 * 32:(q3 + 1) * 32], pk[0:64, 0:32])
                                else:
                                    P32 = slice(q3 * 32, (q3 + 1) * 32)
                                    for fx in range(2):
                                        p.mm(pk[P32, 0:64], hid[:, fx, :], w2c[:, 1, fx, :], start=(fx == 0), stop=(fx == 1))
                                    p.cp("dve", Vc[P32, chn, g, 0:64], pk[P32, 0:64])
                        p.barrier()
                    with contextlib.ExitStack() as stk:
                        pss = [ps(f"n2_ps{i}", [128, 4, 128], F32, stk) for i in range(2)]
                        pm = ps("n2_pm", [128, 4, 128], F32, stk)
                        poA = ps("n2_poA", [128, 4, 65], F32, stk)
                        poB = ps("n2_poB", [128, 4, 64], F32, stk)
                        pos_ = ps("n2_pos", [128, 4, 65], F32, stk)
                        pow_ = ps("n2_pow", [128, 4, 65], F32, stk)
                        ptt = ps("n2_pt", [128, 2 * TT], BF16, stk)
                        pe_ = [sb(f"n2_pe{i}", [128, 4, 128], BF16, stk) for i in range(3)]
                        cm = [sb(f"n2_cm{i}", [128, 128], BF16, stk) for i in range(3)]
                        Ft = sb("n2_F", [128, 64], F32, stk)
                        imp = sb("n2_imp", [128, 64], F32, stk)
                        m8 = sb("n2_m8", [128, 8], F32, stk)
                        sel = sb("n2_sel", [128, 64], BF16, stk)
                        selT = sb("n2_selT", [64, 128], BF16, stk)
                        mdg = sb("n2_mdg", [128, 128], BF16, stk)
                        rc = sb("n2_rc", [128, 3, 4], F32, stk)
                        cf = sb("n2_cf", [128, 3, 4], F32, stk)
                        ot = sb("n2_ot", [128, 16, 64], BF16, stk)
                        o1 = sb("n2_o1", [128, 4, 64], F32, stk)
                        o2 = sb("n2_o2", [128, 4, 64], F32, stk)
                        ec = [0]

                        def score(st):
                            sc = pss[ec[0] % 2]
                            pt_ = pe_[ec[0] % 3]
                            ec[0] += 1
                            if st.get("pre"):
                                st["pre"]()
                            p.mm(sc[:], st["kap"], st["qap"])
                            if st["clamp"]:
                                p.ts("dve", sc[:], sc[:], 60.0, None, ALU.min)
                            return sc, pt_

                        def finish(st, bufs):
                            sc, pt_ = bufs
                            p.act(pt_[:], sc[:], AF.Exp)
                            for msk in st["masks"]:
                                p.tt("dve", pt_[:], pt_[:], bc(msk.unsqueeze(1), [128, 4, 128]), ALU.mult)
                            for (po_, wdt, rows, rhs) in st["pv"]:
                                for hh in range(4):
                                    p.mm(po_[:, hh, 0:wdt], pt_[0:rows, hh, :], rhs, start=(st["first"] and hh == 0), stop=st["last"], sgc=True)

                        def run_steps(seq):
                            pending = None
                            for st in seq:
                                if callable(st):
                                    if pending:
                                        finish(*pending)
                                        pending = None
                                    st()
                                    continue
                                bufs = score(st)
                                if pending:
                                    finish(*pending)
                                pending = (st, bufs)
                            if pending:
                                finish(*pending)

                        for s in range(NS):
                            qt = ti * NS + s
                            qs = slice(s * 128, (s + 1) * 128)
                            p.dma("sp", Ft[:], c_sel_F[qt])
                            for g in range(4):
                                qap = Qa[:, g * 4:(g + 1) * 4, qs]
                                seq = []
                                chmax = ((8 * qt + 7) // 32) // 3
                                for ch in range(chmax + 1):
                                    cmk = cm[ch % 3]

                                    def pre_cmp(cmk=cmk, ch=ch):
                                        p.dma("sp", cmk[:], c_cmp_mask[ch, :, qt * 128:(qt + 1) * 128])
                                    seq.append(dict(kap=Kc[:, g, ch, :], qap=qap, clamp=True, masks=[cmk[:]], pre=pre_cmp,
                                                    pv=[(poA, 65, 96, Vc[0:96, ch, g, 0:65]), (poB, 64, 96, Vc[0:96, ch, g, 65:129])],
                                                    first=(ch == 0), last=(ch == chmax)))

                                def selection():
                                    p.ts("dve", rc[:, 0, :], poA[:, :, 64], 1e-30, None, ALU.max)
                                    p.op("dve", "reciprocal", rc[:, 0, :], rc[:, 0, :])
                                    p.ts("dve", imp[:], poB[:, 0, :], rc[:, 0, 0:1], None, ALU.mult)
                                    for hh in range(1, 4):
                                        p.stt("dve", imp[:], poB[:, hh, :], rc[:, 0, hh:hh + 1], imp[:], ALU.mult, ALU.add)
                                    p.tt("dve", imp[:], imp[:], Ft[:], ALU.add)
                                    p.op("dve", "max", m8[:], imp[:])
                                    p.ts("dve", sel[:], imp[:], m8[:, 7:8], None, ALU.is_ge)
                                    p.tr(ptt[0:64, 0:128], sel[:], identb[:])
                                    p.cp("dve", selT[:], ptt[0:64, 0:128])
                                seq.append(selection)
                                kts = [kk_ for kk_ in (qt - 2, qt - 1, qt) if kk_ >= 0]
                                for i_, kt in enumerate(kts):
                                    dl = qt - kt
                                    slot = 2 + s - dl
                                    masks = [mo[:]] if dl == 0 else ([mp[:]] if dl == 2 else [])
                                    seq.append(dict(kap=Kw[:, g, slot * 128:(slot + 1) * 128], qap=qap, clamp=(dl == 0), masks=masks,
                                                    pv=[(pow_, 65, 128, Vw[:, slot, g, :])], first=(i_ == 0), last=(i_ == len(kts) - 1)))
                                for kt in range(qt + 1):
                                    mslot = pm[:, kt % 4, :]

                                    def pre_slc(kt=kt, mslot=mslot):
                                        p.mm(mslot, Ebl[:, kt * 128:(kt + 1) * 128], selT[:])
                                        if kt == qt:
                                            p.tt("dve", mdg[:], mslot, mo[:], ALU.mult)
                                    seq.append(dict(kap=Ks[:, g, kt * 128:(kt + 1) * 128], qap=qap, clamp=(kt == qt),
                                                    masks=[mdg[:] if kt == qt else mslot], pre=pre_slc,
                                                    pv=[(pos_, 65, 128, Vs[:, kt, g, :])], first=(kt == 0), last=(kt == qt)))
                                run_steps(seq)
                                p.op("dve", "reciprocal", rc[:, 1, :], pos_[:, :, 64])
                                p.op("dve", "reciprocal", rc[:, 2, :], pow_[:, :, 64])
                                gv = gates[:, s, g * 12:(g + 1) * 12].rearrange("p (r n) -> p n r", n=3)
                                p.tt("dve", cf[:], rc[:], gv, ALU.mult)
                                p.tt("dve", o1[:], poA[:, :, 0:64], bc(cf[:, 0, :].unsqueeze(2), [128, 4, 64]), ALU.mult)
                                p.tt("dve", o2[:], pos_[:, :, 0:64], bc(cf[:, 1, :].unsqueeze(2), [128, 4, 64]), ALU.mult)
                                p.tt("pool", o1[:], o1[:], o2[:], ALU.add)
                                p.tt("dve", o2[:], pow_[:, :, 0:64], bc(cf[:, 2, :].unsqueeze(2), [128, 4, 64]), ALU.mult)
                                p.tt("pool", ot[:, g * 4:(g + 1) * 4, :], o1[:], o2[:], ALU.add)
                            for kc in range(8):
                                p.tr(ptt[:, kc * 128:(kc + 1) * 128], ot[:, 2 * kc:2 * kc + 2, :].rearrange("p h d -> p (h d)"), identb[:])
                            p.cp("act", oT[:, :, qs], ptt[:, 0:1024].rearrange("p (c t) -> p c t", c=8))
                        p.barrier()
                with contextlib.ExitStack() as stk:
                    pacc = [ps(f"n3_pa{i}", [128, 2, 512], F32, stk) for i in range(2)]
                    wout = sb("n_wout", [128, 8, D], BF16, stk)
                    p.dma("sp", wout[:], nsa_w_outb.rearrange("(kc p) c -> p kc c", p=128))
                    junk = sb("n3_junk", [128, D], BF16, stk)
                    ss1 = sb("n3_ss1", [128, NS], F32, stk)
                    tmp = sb("n3_tmp", [128, D], F32, stk)
                    for s in range(NS):
                        pm_ = pacc[s % 2]
                        for nh in range(2):
                            for kc in range(8):
                                p.mm(pm_[:, nh, :], oT[:, kc, s * 128:(s + 1) * 128], wout[:, kc, nh * 512:(nh + 1) * 512],
                                     start=(kc == 0), stop=(kc == 7))
                        post_norm_residual(pm_[:], xt, 1, s, (junk, ss1, tmp))
                    p.barrier()
                ffn(1, xt, gcarry, first)
                p.dma("sp", out[b, t0:t0 + TT, :].rearrange("(s p) d -> p s d", p=128), xt[:])
                p.barrier()

    def rwkv_tile(first, oT, ST, stk, hg):
        zs = k.zs
        NCH = TT // 64
        pa = [ps(f"r_pa{i}", [128, TT], F32, stk) for i in range(2)]
        pq = ps("r_pq", [128, 4, 64], F32, stk)
        pqt = ps("r_pqt", [128, 4, 64], F32, stk)
        pp_ = ps("r_pp", [128, 2, 4, 64], F32, stk)
        pg1 = ps("r_pg1", [128, 4, 128], F32, stk)
        pg2 = ps("r_pg2", [128, 4, 128], F32, stk)
        py = ps("r_py", [128, 2, 4, 64], F32, stk)
        AR = sb("r_AR", [128, 2, NCH, 2, 64], BF16, stk)
        BK = sb("r_BK", [128, 2, NCH, 2, 64], BF16, stk)
        TM = sb("r_TM", [128, NS, 3, 256], BF16, stk)
        STb = sb("r_STb", [128, 4, 64], BF16, stk)
        GC = sb("r_GC", [128, 2, NCH], F32, stk)
        bvT = sb("r_bvT", [128, 2, TT], BF16, stk)
        gT = sb("r_gT", [128, 2, TT], BF16, stk)
        if first and hg == 0:
            p.op("pool", "memset", ST[:], 0.0)
        twa = sb("r_twa", [128, TT], BF16, stk)
        sgl = sb("r_sgl", [128, TT], BF16, stk)
        p.act(twa[0:64, :], zs[12][0:64, :], AF.Tanh)
        p.cp("dve", twa[64:128, :], zs[12][64:128, :])
        p.act(sgl[:], zs[13][:], AF.Sigmoid)
        k.maybe_stop("r1")
        t = {n: sb("r_" + n, [128, TT], F32, stk) for n in
             ("e", "L", "a", "kk", "nrm", "kkn", "tmp", "kmod", "ka", "G1", "Gx", "bh", "kh")}
        v3 = lambda a_: a_.rearrange("p (c t) -> p c t", t=64)
        for cl in range(2):
            cc = hg * 2 + cl
            r_s, k_s, v_s = zs[cc], zs[4 + cc], zs[8 + cc]
            csl = slice(cc * 128, (cc + 1) * 128)
            lsl = slice(cl * 128, (cl + 1) * 128)
            p.mm(pa[0][:], w2b[0:64, csl], twa[0:64, :])
            p.act(t["e"][:], pa[0][:], AF.Sigmoid, bias=ppt[:, W0 + cc:W0 + cc + 1])
            p.ts("dve", t["e"][:], t["e"][:], -float(np.exp(-0.5)), None, ALU.mult)
            p.op("dve", "tensor_tensor_scan", t["L"][:], mask01[:], t["e"][:], 0.0, ALU.mult, ALU.add)
            p.mm(pa[1][:], w2b[64:128, csl], twa[64:128, :])
            p.act(t["a"][:], pa[1][:], AF.Sigmoid, bias=ppt[:, A0 + cc:A0 + cc + 1])
            p.mm(pa[0][:], g2b[:, csl], sgl[:])
            p.cp("act", gT[:, cl, :], pa[0][:])
            p.ts("pool", t["kk"][:], k_s[:], ppt[:, KK + cc:KK + cc + 1], None, ALU.mult)
            p.tt("pool", t["tmp"][:], t["kk"][:], t["kk"][:], ALU.mult)
            p.mm(pa[1][:], onesblk[:], t["tmp"][:])
            p.act(t["nrm"][:], pa[1][:], AF.Sqrt)
            p.ts("dve", t["nrm"][:], t["nrm"][:], 1e-12, None, ALU.max)
            p.op("dve", "reciprocal", t["nrm"][:], t["nrm"][:])
            p.tt("dve", t["kkn"][:], t["kk"][:], t["nrm"][:], ALU.mult)
            p.ts("pool", t["tmp"][:], t["a"][:], ppt[:, KA + cc:KA + cc + 1], omka[:, cc:cc + 1], ALU.mult, ALU.add)
            p.tt("pool", t["kmod"][:], k_s[:], t["tmp"][:], ALU.mult)
            p.stt("dve", t["tmp"][:], r_s[:], ppt[:, RK + cc:RK + cc + 1], t["kmod"][:], ALU.mult, ALU.mult)
            p.mm(pa[0][:], onesblk[:], t["tmp"][:])
            p.tt("dve", bvT[:, cl, :], pa[0][:], v_s[:], ALU.mult)
            p.tt("pool", t["ka"][:], t["kkn"][:], t["a"][:], ALU.mult)
            p.act(t["G1"][:], t["L"][:], AF.Exp)
            p.cp("pool", GC[:, cl, :], v3(t["G1"][:])[:, :, 63])
            p.tt("pool", AR[:, cl, :, 1, :], v3(r_s[:]), v3(t["G1"][:]), ALU.mult)
            p.tt("pool", t["tmp"][:], t["L"][:], t["e"][:], ALU.subtract)
            p.act(t["Gx"][:], t["tmp"][:], AF.Exp)
            p.stt("dve", AR[:, cl, :, 0, :], v3(t["kkn"][:]), -1.0, v3(t["Gx"][:]), ALU.mult, ALU.mult)
            p.act(t["Gx"][:], t["L"][:], AF.Exp, scale=-1.0)
            p.tt("dve", BK[:, cl, :, 0, :], v3(t["ka"][:]), v3(t["Gx"][:]), ALU.mult)
            p.tt("pool", BK[:, cl, :, 1, :], v3(t["kmod"][:]), v3(t["Gx"][:]), ALU.mult)
            L3 = v3(t["L"][:])
            p.tt("dve", v3(t["tmp"][:]), bc(L3[:, :, 63:64], [128, NCH, 64]), L3, ALU.subtract)
            p.act(t["Gx"][:], t["tmp"][:], AF.Exp)
            p.tt("dve", t["bh"][:], t["ka"][:], t["Gx"][:], ALU.mult)
            p.tt("pool", t["kh"][:], t["kmod"][:], t["Gx"][:], ALU.mult)
            for qi, src_ in enumerate((t["bh"], t["kh"], v_s)):
                pz = pa[qi % 2]
                for s in range(NS):
                    p.tr(pz[:, s * 128:(s + 1) * 128], src_[:, s * 128:(s + 1) * 128], identf[:])
                p.cp("act" if qi % 2 == 0 else "dve", TM[:, :, qi, lsl], pz[:].rearrange("p (s c) -> p s c", s=NS))
            k.maybe_stop("r2")
        Gs1 = [sb(f"r_Gs1_{i}", [128, 4, 128], BF16, stk) for i in range(2)]
        Gs2 = [sb(f"r_Gs2_{i}", [128, 4, 128], BF16, stk) for i in range(2)]
        Qb = [sb(f"r_Q{i}", [128, 4, 64], BF16, stk) for i in range(2)]
        QTb = [sb(f"r_QT{i}", [128, 4, 64], BF16, stk) for i in range(2)]
        Pm = [sb(f"r_P{i}", [128, 4, 64], F32, stk) for i in range(2)]
        Pb = [sb(f"r_Pb{i}", [128, 4, 64], BF16, stk) for i in range(2)]
        RH = sb("r_RH", [128, 4, 64], BF16, stk)
        UT = sb("r_UT", [128, 4, 64], BF16, stk)
        ysb = sb("r_ysb", [128, 4, 64], F32, stk)
        ysq = sb("r_ysq", [128, 4, 64], F32, stk)
        st1 = sb("r_st1", [128, 4], F32, stk)
        st2 = sb("r_st2", [128, 4], F32, stk)
        st3 = sb("r_st3", [128, 4], F32, stk)
        ob = sb("r_ob", [128, 128], F32, stk)
        HI = [(h // 2, (h % 2) * 64) for h in range(4)]
        p.cp("pool", STb[:], ST[:])
        Tb = [sb(f"r_Tb{i}", [128, 4, 64], BF16, stk) for i in range(2)]
        prh = pa[1][:].rearrange("p (a h d) -> p a h d", a=2, h=4)
        pu = pa[0][:, 128:384].rearrange("p (h d) -> p h d", h=4)

        def prepT(c):
            pc = (c % 2) * 64
            R = slice(pc, pc + 64)
            G1s, G2s = Gs1[c % 2], Gs2[c % 2]
            for h in (0, 2, 1, 3):
                cl, pb = HI[h]
                H = slice(pb, pb + 64)
                ar = AR[H, cl, c, :, :].rearrange("p a t -> p (a t)")
                p.mm(pg1[R, h, :], BK[H, cl, c, 0, :], ar)
                p.mm(pg2[R, h, :], BK[H, cl, c, 1, :], ar)
                p.mm(pqt[R, h, :], AR[H, cl, c, 0, :], BK[H, cl, c, 0, :])
            p.tt("dve", G1s[R, :, :], pg1[R, :, :], bc(maskG[R, :].unsqueeze(1), [64, 4, 128]), ALU.mult)
            p.tt("dve", G2s[R, :, :], pg2[R, :, :], bc(maskG[R, :].unsqueeze(1), [64, 4, 128]), ALU.mult)
            Q, QT, P_ = Qb[0], QTb[0], Pm[0]
            p.tt("dve", QT[R, :, :], pqt[R, :, :], bc(maskAT[R, :].unsqueeze(1), [64, 4, 64]), ALU.mult)
            p.cp("act", Q[R, :, :], G1s[R, :, 0:64])
            p.tt("pool", P_[R, :, :], G1s[R, :, 0:64], bc(identf[R, pc:pc + 64].unsqueeze(1), [64, 4, 64]), ALU.add)
            p.tt("pool", Pb[0][R, :, :], G1s[R, :, 0:64], bc(identf[R, pc:pc + 64].unsqueeze(1), [64, 4, 64]), ALU.add)
            yield
            cur = 0
            for lv in range(1, 6):
                Qo, QTo = Qb[cur], QTb[cur]
                Qn, QTn = Qb[1 - cur], QTb[1 - cur]
                for h in (0, 2, 1, 3):
                    if lv < 5:
                        p.mm(pq[R, h, :], QTo[R, h, :], Qo[R, h, :])
                    p.mm(pqt[R, h, :], Qo[R, h, :], QTo[R, h, :])
                if lv < 5:
                    p.cp("act", Qn[R, :, :], pq[R, :, :])
                p.cp("dve", QTn[R, :, :], pqt[R, :, :])
                yield
                Po, Pn = Pm[cur], Pm[1 - cur]
                for h in (0, 2, 1, 3):
                    p.mm(pp_[R, 0, h, :], QTn[R, h, :], Pb[cur][R, h, :])
                if lv < 5:
                    p.tt("dve", Pn[R, :, :], pp_[R, 0, :, :], Po[R, :, :], ALU.add)
                    p.tt("dve", Pb[1 - cur][R, :, :], pp_[R, 0, :, :], Po[R, :, :], ALU.add)
                else:
                    p.tt("dve", Tb[c % 2][R, :, :], pp_[R, 0, :, :], Po[R, :, :], ALU.add)
                cur = 1 - cur
                yield

        def chain(c):
            pc = (c % 2) * 64
            R = slice(pc, pc + 64)
            s = c // 2
            G1s, G2s = Gs1[c % 2], Gs2[c % 2]
            Tm = Tb[c % 2]
            for h in (0, 2, 1, 3):
                cl, pb = HI[h]
                H = slice(pb, pb + 64)
                cc = hg * 2 + cl
                p.mm(prh[R, 0, h, :], AR[H, cl, c, 0, :], STb[H, cc, :])
                p.mm(prh[R, 1, h, :], G2s[R, h, 0:64], TM[R, s, 2, h * 64:(h + 1) * 64])
            p.cp("act", RH[R, :, :], prh[R, 0, :, :])
            p.tt("dve", RH[R, :, :], RH[R, :, :], prh[R, 1, :, :], ALU.add)
            yield
            for h in (0, 2, 1, 3):
                p.mm(pu[R, h, :], Tm[R, h, :], RH[R, h, :])
            p.cp("dve", UT[R, :, :], pu[R, :, :])
            yield
            for h in (0, 2, 1, 3):
                cl, pb = HI[h]
                H = slice(pb, pb + 64)
                cc = hg * 2 + cl
                p.mm(py[R, 0, h, :], AR[H, cl, c, 1, :], STb[H, cc, :])
            for h in (0, 2, 1, 3):
                p.mm(py[R, 1, h, :], G1s[R, h, 64:128], UT[R, h, :], start=True, stop=False)
                p.mm(py[R, 1, h, :], G2s[R, h, 64:128], TM[R, s, 2, h * 64:(h + 1) * 64], start=False, stop=True)
            for h in (0, 2, 1, 3):
                cl, pb = HI[h]
                H = slice(pb, pb + 64)
                p.mm(pa[0][H, cl * 64:(cl + 1) * 64], TM[R, s, 0, h * 64:(h + 1) * 64], UT[R, h, :], start=True, stop=False)
                p.mm(pa[0][H, cl * 64:(cl + 1) * 64], TM[R, s, 1, h * 64:(h + 1) * 64], TM[R, s, 2, h * 64:(h + 1) * 64], start=False, stop=True)
            for cl in range(2):
                cc = hg * 2 + cl
                p.stt("dve", ST[:, cc, :], ST[:, cc, :], GC[:, cl, c:c + 1], pa[0][:, cl * 64:(cl + 1) * 64], ALU.mult, ALU.add)
                p.cp("pool", STb[:, cc, :], ST[:, cc, :])
            yield
            if c % 2 == 1:
                p.cp("act", ysb[:], py[:, 0, :, :])
                p.tt("dve", ysb[:], ysb[:], py[:, 1, :, :], ALU.add)
                p.op("dve", "tensor_reduce", st1[:], ysb[:], AX.X, ALU.add)
                p.act(ysq[:], ysb[:], AF.Square)
                p.op("dve", "tensor_reduce", st2[:], ysq[:], AX.X, ALU.add)
                p.ts("dve", st1[:], st1[:], 1.0 / 64, None, ALU.mult)
                p.tt("dve", st3[:], st1[:], st1[:], ALU.mult)
                p.stt("dve", st2[:], st2[:], 1.0 / 64, st3[:], ALU.mult, ALU.subtract)
                p.ts("dve", st2[:], st2[:], GN_EPS, None, ALU.add)
                p.act(st2[:], st2[:], AF.Sqrt)
                p.op("dve", "reciprocal", st2[:], st2[:])
                p.tt("dve", ysb[:], ysb[:], bc(st1[:].unsqueeze(2), [128, 4, 64]), ALU.subtract)
                p.tt("dve", ysb[:], ysb[:], bc(st2[:].unsqueeze(2), [128, 4, 64]), ALU.mult)
                yield
                pz = pa[1]
                for cl in range(2):
                    p.tr(pz[:, cl * 128:(cl + 1) * 128], ysb[:, 2 * cl:2 * cl + 2, :].rearrange("p h d -> p (h d)"), identf[:])
                for cl in range(2):
                    cc = hg * 2 + cl
                    tsl = slice(s * 128, (s + 1) * 128)
                    p.ts("dve", ob[:], pz[:, cl * 128:(cl + 1) * 128], ppt[:, LG + cc:LG + cc + 1], ppt[:, LB + cc:LB + cc + 1], ALU.mult, ALU.add)
                    p.tt("pool", ob[:], ob[:], bvT[:, cl, tsl], ALU.add)
                    p.tt("pool", oT[:, 4 + cc, tsl], ob[:], gT[:, cl, tsl], ALU.mult)
                yield

        for _ in prepT(0):
            pass
        for c in range(NCH):
            gA = chain(c)
            gB = prepT(c + 1) if c + 1 < NCH else iter(())
            doneA = doneB = False
            while not (doneA and doneB):
                if not doneB:
                    try:
                        next(gB)
                    except StopIteration:
                        doneB = True
                if not doneA:
                    try:
                        next(gA)
                    except StopIteration:
                        doneA = True

    k.maybe_stop = maybe_stop
    try:
        maybe_stop("prepass")
        layer0()
        if n_layers > 1:
            layer1()
    except Stop:
        pass
    p.barrier()
    k.ninst = p.ninst
    return nc, k


def _consts():
    c = {}
    c["c_identb"] = np.eye(128, dtype=np.float32).astype(NPBF)
    c["c_identf"] = np.eye(128, dtype=np.float32)
    ob = np.zeros((128, 128), np.float32)
    ob[:64, :64] = 1
    ob[64:, 64:] = 1
    c["c_onesblk"] = ob
    m = np.ones((128, TT), np.float32)
    m[:, ::64] = 0
    c["c_mask01"] = m
    s = (np.arange(128) % 64)[:, None]
    t = np.arange(64)[None, :]
    c["c_maskG"] = np.concatenate([(s < t), (s <= t)], axis=1).astype(np.float32)
    c["c_maskAT"] = (t < s).astype(np.float32)
    slopes = 2.0 ** (-(np.arange(8) + 1.0))
    ql = (np.arange(TT) % 128).astype(np.float32)
    q = np.zeros((2, 8, TT), np.float32)
    q[0] = slopes[:, None]
    q[1] = -slopes[:, None] * ql[None, :]
    c["c_swa_q"] = q.reshape(2, 8 * TT).astype(NPBF)
    c["c_swa_ko"] = np.stack([ql, np.ones(TT, np.float32)]).astype(NPBF)
    c["c_swa_kp"] = np.stack([np.arange(128, dtype=np.float32) - 128, np.ones(128, np.float32)]).astype(NPBF)
    kl = np.arange(128)[:, None]
    qq = np.arange(128)[None, :]
    c["c_swa_mo"] = (kl <= qq).astype(np.float32).astype(NPBF)
    c["c_swa_mp"] = (kl > qq).astype(np.float32).astype(NPBF)
    return c


def _bf_split(v):
    hi = v.astype(NPBF)
    lo = (v - hi.astype(np.float32)).astype(NPBF)
    return hi, lo


def _consts_nsa(T):
    c = {}
    slopes = (2.0 ** (-8.0 * (np.arange(16) + 1.0) / 16)).astype(np.float32)
    s1, s2 = _bf_split(slopes)
    t = np.arange(T, dtype=np.float32)
    q = np.zeros((5, 16, T), np.float32)
    q[0] = s1.astype(np.float32)[:, None]
    q[1] = s2.astype(np.float32)[:, None]
    q[2] = q[0]
    q[3] = q[1]
    q[4] = -slopes[:, None] * t[None, :]
    c["c_nsa_q"] = q.reshape(5, 16 * T).astype(NPBF)

    def krows(pos):
        lo = pos % 128
        hi = pos - lo
        return np.stack([lo, lo, hi, hi, np.ones_like(pos)]).astype(np.float32)

    c["c_nsa_k"] = krows(t).astype(NPBF)
    sp_slot = -np.ones(384, np.int64)
    for ti in range(8):
        for j in range(32):
            sp_slot[(ti // 3) * 128 + 32 * (ti % 3) + j] = 32 * ti + j
    valid_sp = sp_slot >= 1
    posc = np.where(valid_sp, 16.0 * sp_slot + 15.0, 0.0).astype(np.float32)
    kr = krows(posc)
    kr[:, ~valid_sp] = 0
    c["c_cmp_k"] = kr.astype(NPBF)
    tt_ = np.arange(T)[None, :]
    msk = valid_sp[:, None] & (tt_ >= (16 * sp_slot[:, None] + 15))
    c["c_cmp_mask"] = msk.astype(np.float32).reshape(3, 128, T).astype(NPBF)
    mp_ = np.zeros((384, 64), np.float32)
    wts = {-1: 1.0, 0: 2.0, 1: 2.0, 2: 2.0, 3: 1.0}
    for sp in range(384):
        if sp_slot[sp] >= 1:
            cidx = sp_slot[sp] - 1
            for blk in range(64):
                m = cidx - 4 * blk
                if m in wts:
                    mp_[sp, blk] = wts[m]
    c["c_mpool"] = np.ascontiguousarray(mp_.reshape(3, 128, 64).transpose(1, 0, 2).reshape(128, 192)).astype(NPBF)
    F = np.zeros((T // 128, 128, 64), np.float32)
    for qt in range(T // 128):
        cur = (qt * 128 + np.arange(128)) // 64
        blk = np.arange(64)[None, :]
        forced = (blk == 0) | (blk == cur[:, None]) | (blk == cur[:, None] - 1)
        F[qt] = np.where(forced, 1e30, np.where(blk > cur[:, None], -1e30, 0.0))
    c["c_sel_F"] = F
    c["c_E"] = (np.arange(T)[None, :] // 64 == np.arange(64)[:, None]).astype(np.float32).astype(NPBF)
    return c


def _prep_shared(inp, n_layers=2):
    f = lambda a: np.ascontiguousarray(np.asarray(a, dtype=np.float32))
    sh = dict(_consts())
    T = np.asarray(inp["x"]).shape[1]
    gains = np.zeros((8, D), np.float32)
    for l in range(2):
        gains[l * 4 + 0] = inp["mix_pre_g"][l]
        gains[l * 4 + 1] = inp["mix_post_g"][l]
        gains[l * 4 + 2] = inp["ffn_pre_g"][l]
        gains[l * 4 + 3] = inp["ffn_post_g"][l]
    sh["gains"] = gains
    sh["hy_w_in"] = f(inp["hy_w_in"][0])
    sh["hy_w_out"] = f(inp["hy_w_out"][0])
    sh["ffn_w_up"] = f(inp["ffn_w_up"])
    sh["ffn_w_down"] = f(inp["ffn_w_down"])
    fp = np.zeros((2, 128, NFC, 4), np.float32)
    for l in range(2):
        cw = np.asarray(inp["ffn_conv_w"][l]).reshape(3, NFC, 128)
        fp[l, :, :, 0:3] = cw.transpose(2, 1, 0)
        fp[l, :, :, 3] = np.asarray(inp["ffn_conv_b"][l]).reshape(NFC, 128).T
    sh["ffn_pp"] = fp.reshape(2, 128, NFC * 4)
    pp = np.zeros((128, 42), np.float32)
    pp[:, 0:14] = np.asarray(inp["rwkv_mu"][0]).reshape(14, 128).T
    for i, nm in enumerate(["rwkv_w0", "rwkv_a0", "rwkv_k_k", "rwkv_k_a", "rwkv_r_k", "rwkv_ln_g", "rwkv_ln_b"]):
        pp[:, 14 + 4 * i:18 + 4 * i] = np.asarray(inp[nm][0]).reshape(4, 128).T
    sh["pp0"] = pp
    sh["swa_sinks"] = f(inp["swa_sinks"])
    sh["rwkv_w2"] = f(inp["rwkv_w2"][0])
    sh["rwkv_a2"] = f(inp["rwkv_a2"][0])
    sh["rwkv_g2"] = f(inp["rwkv_g2"][0])
    if n_layers > 1:
        sh["nsa_w_in"] = f(inp["nsa_w_in"][0])
        sh["nsa_w_out"] = f(inp["nsa_w_out"][0])
        sh["cmp_w1_k"] = f(inp["nsa_cmp_w1_k"][0]).reshape(2048, 256)
        sh["cmp_w1_v"] = f(inp["nsa_cmp_w1_v"][0]).reshape(2048, 256)
        sh["cmp_w2_k"] = f(inp["nsa_cmp_w2_k"][0])
        sh["cmp_w2_v"] = f(inp["nsa_cmp_w2_v"][0])
        pt = np.zeros((128, 32), np.float32)
        for kv, nm in enumerate(["nsa_cmp_pos_k", "nsa_cmp_pos_v"]):
            pos = np.asarray(inp[nm][0], np.float32)
            pt[:, kv * 16:(kv + 1) * 16] = pos.reshape(16, 2, 64).transpose(1, 2, 0).reshape(128, 16)
        sh["cmp_posT"] = pt
        sh.update(_consts_nsa(T))
    return sh


def kernel(**inputs):
    x = np.asarray(inputs["x"], dtype=np.float32)
    B, T, _ = x.shape
    n = 8
    NB = B // n
    nc, k = build(T=T, NB=NB, n_layers=2)
    sh = _prep_shared(inputs)
    in_maps = []
    for c in range(n):
        m = dict(sh)
        m["x"] = np.ascontiguousarray(x[c * NB:(c + 1) * NB])
        in_maps.append(m)
    res = run_bass_kernel_spmd(nc, in_maps, core_ids=list(range(n)))
    return np.concatenate([r["out"] for r in res.results], axis=0)
```

```python
import contextlib
import numpy as np
import ml_dtypes
import concourse.bass as bass
import concourse.mybir as mybir
from concourse.bass_utils import run_bass_kernel_spmd

F32 = mybir.dt.float32
BF16 = mybir.dt.bfloat16
AF = mybir.ActivationFunctionType
ALU = mybir.AluOpType
AX = mybir.AxisListType
NPBF = ml_dtypes.bfloat16

D = 1024
TT = 512
NS = 4
HD = 64
DFF = 2816
NFC = 22
HY_COLS = 2560
NSA_COLS = 2608
EPS = 1e-6
GN_EPS = 64e-5


class Prog:
    def __init__(self, nc, n_dma_sems=16):
        self.nc = nc
        self.eng = {"pe": nc.tensor, "act": nc.scalar, "dve": nc.vector, "pool": nc.gpsimd, "sp": nc.sync}
        self.psem = {k: nc.alloc_semaphore(name=f"prog_{k}") for k in self.eng}
        self.cnt = {k: 0 for k in self.eng}
        self.waited = {k: {} for k in self.eng}
        self.dsems = {}
        for q in ("sp", "act", "pool"):
            self.dsems[q] = [[nc.alloc_semaphore(name=f"dma_{q}_{i}"), 0] for i in range(n_dma_sems)]
        self.dnext = {q: 0 for q in self.dsems}
        self.bufs = {}
        self.ninst = 0
        self.pe_base = 0
        self.psum_names = set()

    def _st(self, ap):
        nm = ap.tensor.name
        st = self.bufs.get(nm)
        if st is None:
            st = {"w": None, "r": {}}
            self.bufs[nm] = st
        return st

    def _wait(self, e, deps):
        eng = self.eng[e]
        best = {}
        for d in deps:
            if d is None:
                continue
            s, v, key = d
            if key == e and e == "pe":
                continue
            if best.get(key, (None, 0))[1] < v:
                best[key] = (s, v)
        for key, (s, v) in best.items():
            if self.waited[e].get(key, 0) < v:
                eng.wait_ge(s, v)
                self.waited[e][key] = v

    def _deps(self, reads, writes):
        deps = []
        for ap in reads:
            st = self._st(ap)
            deps.append(st["w"])
            if ap.tensor.name in self.psum_names:
                deps.extend(st["r"].values())
        for ap in writes:
            st = self._st(ap)
            deps.append(st["w"])
            deps.extend(st["r"].values())
        return deps

    def _commit(self, ev, reads, writes):
        for ap in reads:
            self._st(ap)["r"][ev[2]] = ev
        for ap in writes:
            st = self._st(ap)
            st["w"] = ev
            st["r"] = {}

    def op(self, e, fn, *args, **kw):
        writes = [args[0]]
        reads = [a for a in args[1:] if isinstance(a, bass.AP)]
        for k, v in kw.items():
            if isinstance(v, bass.AP):
                (writes if k == "accum_out" else reads).append(v)
        self._wait(e, self._deps(reads, writes))
        if e == "pe":
            base = (fn, args[1].start_partition(), args[0].start_partition(), args[1].shape[0], args[1].shape[-1] if len(args[1].shape) == 2 else -1, str(args[1].dtype))
            if base != self.pe_base and self.cnt["pe"] > 0:
                self.eng["pe"].wait_ge(self.psem["pe"], self.cnt["pe"])
            self.pe_base = base
        ins = getattr(self.eng[e], fn)(*args, **kw)
        self.cnt[e] += 1
        ins.then_inc(self.psem[e], 1)
        self._commit((self.psem[e], self.cnt[e], e), reads, writes)
        self.ninst += 1
        return ins

    def dma(self, q, out, in_, **kw):
        reads, writes = [in_], [out]
        deps = self._deps(reads, writes)
        slot = self.dnext[q]
        self.dnext[q] = (slot + 1) % len(self.dsems[q])
        ent = self.dsems[q][slot]
        key = f"d_{q}_{slot}"
        if ent[1] > 0:
            deps.append((ent[0], ent[1], key))
        self._wait(q, deps)
        ins = self.eng[q].dma_start(out=out, in_=in_, **kw)
        assert ent[1] + 16 <= 240, "DMA semaphore would exceed its usable range; add a barrier()"
        ent[1] += 16
        ins.then_inc(ent[0], 16)
        self._commit((ent[0], ent[1], key), reads, writes)
        self.ninst += 1
        return ins

    def barrier(self):
        deps = []
        for qq, lst in self.dsems.items():
            for i, ent in enumerate(lst):
                if ent[1] > 0:
                    deps.append((ent[0], ent[1], f"d_{qq}_{i}"))
        for k in self.eng:
            if self.cnt[k] > 0:
                deps.append((self.psem[k], self.cnt[k], k))
        for e in self.eng:
            self._wait(e, deps)
        used = any(ent[1] > 0 for lst in self.dsems.values() for ent in lst)
        if used:
            arr = []
            for e in self.eng:
                if e == "sp":
                    continue
                self.eng[e].sem_inc(self.psem[e], 1)
                self.cnt[e] += 1
                arr.append((self.psem[e], self.cnt[e], e))
            self._wait("sp", arr)
            for qq, lst in self.dsems.items():
                for i, ent in enumerate(lst):
                    if ent[1] > 0:
                        self.eng["sp"].sem_clear(ent[0])
                        ent[1] = 0
                    for e in self.eng:
                        self.waited[e].pop(f"d_{qq}_{i}", None)
            self.eng["sp"].sem_inc(self.psem["sp"], 1)
            self.cnt["sp"] += 1
            ev = (self.psem["sp"], self.cnt["sp"], "sp")
            for e in self.eng:
                if e != "sp":
                    self._wait(e, [ev])
        self.bufs = {}

    def mm(self, out, lhsT, rhs, start=True, stop=True, sgc=False):
        if sgc:
            return self.op("pe", "matmul", out, lhsT, rhs, start=start, stop=stop, skip_group_check=True)
        return self.op("pe", "matmul", out, lhsT, rhs, start=start, stop=stop)

    def tr(self, out, in_, ident):
        return self.op("pe", "transpose", out, in_, ident)

    def act(self, out, in_, func=AF.Copy, **kw):
        return self.op("act", "activation", out, in_, func, **kw)

    def tt(self, e, out, in0, in1, op):
        return self.op(e, "tensor_tensor", out, in0, in1, op)

    def ts(self, e, out, in0, s1, s2, op0, op1=None):
        if op1 is None:
            return self.op(e, "tensor_scalar", out, in0, s1, None, op0)
        return self.op(e, "tensor_scalar", out, in0, s1, s2, op0, op1)

    def stt(self, e, out, in0, scalar, in1, op0, op1):
        return self.op("dve", "scalar_tensor_tensor", out, in0, scalar, in1, op0, op1)

    def cp(self, e, out, in_):
        if e == "act":
            return self.act(out, in_, AF.Copy)
        return self.op(e, "tensor_copy", out, in_)


def bc(ap, shape):
    return ap.broadcast_to(list(shape))


class K:
    pass


def build(T=4096, NB=2, n_layers=2, debug=(), stop=None):
    nc = bass.Bass("TRN2", target_bir_lowering=False)
    p = Prog(nc)
    NT = T // TT
    k = K()
    k.nc, k.p, k.T, k.NB, k.NT = nc, p, T, NB, NT
    dbg = {}

    def din(name, shape, dt=F32):
        return nc.dram_tensor(name, list(shape), dt, kind="ExternalInput").ap()

    def dscr(name, shape, dt):
        return nc.dram_tensor(name, list(shape), dt, kind="Internal").ap()

    x = din("x", [NB, T, D])
    out = nc.dram_tensor("out", [NB, T, D], F32, kind="ExternalOutput").ap()
    gains = din("gains", [8, D])
    w_in0 = din("hy_w_in", [D, HY_COLS])
    w_out0 = din("hy_w_out", [D, D])
    w_up = din("ffn_w_up", [2, D, 2 * DFF])
    w_down = din("ffn_w_down", [2, DFF, D])
    ffn_pp = din("ffn_pp", [2, 128, NFC * 4])
    pp0 = din("pp0", [128, 42])
    sinks = din("swa_sinks", [1, 8])
    rw_w2 = din("rwkv_w2", [64, 512])
    rw_a2 = din("rwkv_a2", [64, 512])
    rw_g2 = din("rwkv_g2", [128, 512])
    c_identb = din("c_identb", [128, 128], BF16)
    c_identf = din("c_identf", [128, 128])
    c_onesblk = din("c_onesblk", [128, 128])
    c_mask01 = din("c_mask01", [128, TT])
    c_maskG = din("c_maskG", [128, 128])
    c_maskAT = din("c_maskAT", [128, 64])
    c_swa_q = din("c_swa_q", [2, 8 * TT], BF16)
    c_swa_ko = din("c_swa_ko", [2, TT], BF16)
    c_swa_kp = din("c_swa_kp", [2, 128], BF16)
    c_swa_mo = din("c_swa_mo", [128, 128], BF16)
    c_swa_mp = din("c_swa_mp", [128, 128], BF16)
    if n_layers > 1:
        nsa_w_in = din("nsa_w_in", [D, NSA_COLS])
        nsa_w_out = din("nsa_w_out", [D, D])
        cw1 = [din("cmp_w1_k", [2048, 256]), din("cmp_w1_v", [2048, 256])]
        cw2 = [din("cmp_w2_k", [256, 64]), din("cmp_w2_v", [256, 64])]
        cposT = din("cmp_posT", [128, 32])
        c_nsa_q = din("c_nsa_q", [5, 16 * T], BF16)
        c_nsa_k = din("c_nsa_k", [5, T], BF16)
        c_cmp_k = din("c_cmp_k", [5, 384], BF16)
        c_cmp_mask = din("c_cmp_mask", [3, 128, T], BF16)
        c_mpool = din("c_mpool", [128, 3 * 64], BF16)
        c_sel_F = din("c_sel_F", [T // 128, 128, 64])
        c_E = din("c_E", [64, T], BF16)
        nsa_w_inb = dscr("nsa_w_inb", [D, NSA_COLS], BF16)
        nsa_w_outb = dscr("nsa_w_outb", [D, D], BF16)
        cw1b = [dscr("cmp_w1_kb", [2048, 256], BF16), dscr("cmp_w1_vb", [2048, 256], BF16)]

    w_in0b = dscr("w_in0b", [D, HY_COLS], BF16)
    w_out0b = dscr("w_out0b", [D, D], BF16)
    w_upb = dscr("w_upb", [2, D, 2 * DFF], BF16)
    w_downb = dscr("w_downb", [2, DFF, D], BF16)
    x1 = dscr("x1", [NB, T, D], F32)

    for nm in debug:
        dbg[nm] = None
    k.dbg_out = {}

    def dbg_dump(name, shape, src_ap, dt=F32):
        if name not in debug:
            return
        if name not in k.dbg_out:
            k.dbg_out[name] = nc.dram_tensor("dbg_" + name, list(shape), dt, kind="ExternalOutput").ap()
        return k.dbg_out[name]

    es = contextlib.ExitStack()

    uid = [0]

    def sb(name, shape, dt, stack=None):
        uid[0] += 1
        return (stack or es).enter_context(nc.sbuf_tensor(f"{name}_{uid[0]}", list(shape), dt))

    def ps(name, shape, dt=F32, stack=None):
        uid[0] += 1
        p.psum_names.add(f"{name}_{uid[0]}")
        return (stack or es).enter_context(nc.psum_tensor(f"{name}_{uid[0]}", list(shape), dt))

    with contextlib.ExitStack() as stc:
        cin = [sb(f"cast_in{i}", [128, 2 * DFF], F32, stc) for i in range(2)]
        cout = [sb(f"cast_out{i}", [128, 2 * DFF], BF16, stc) for i in range(2)]
        ccnt = [0]

        def cast_w(dst, src, rows, cols):
            for r0 in range(0, rows, 128):
                i = ccnt[0] % 2
                ccnt[0] += 1
                p.dma("sp", cin[i][:, 0:cols], src[r0:r0 + 128, :])
                h = cols // 2
                p.cp("dve", cout[i][:, 0:h], cin[i][:, 0:h])
                p.cp("pool", cout[i][:, h:cols], cin[i][:, h:cols])
                p.dma("act", dst[r0:r0 + 128, :], cout[i][:, 0:cols])

        cast_w(w_in0b, w_in0, D, HY_COLS)
        cast_w(w_out0b, w_out0, D, D)
        for l in range(n_layers):
            cast_w(w_upb[l], w_up[l], D, 2 * DFF)
            cast_w(w_downb[l], w_down[l], DFF, D)
        p.barrier()
        if n_layers > 1:
            cast_w(nsa_w_inb, nsa_w_in, D, NSA_COLS)
            cast_w(nsa_w_outb, nsa_w_out, D, D)
            for kv in range(2):
                cast_w(cw1b[kv], cw1[kv], 2048, 256)
            p.barrier()
    k.stop = stop

    class Stop(Exception):
        pass

    def maybe_stop(tag, src_ap=None):
        if k.stop == tag:
            if src_ap is not None:
                p.dma("sp", out[0, 0:TT, :].rearrange("(s p) d -> p s d", p=128), src_ap)
            p.barrier()
            raise Stop()

    identb = sb("identb", [128, 128], BF16)
    identf = sb("identf", [128, 128], F32)
    onesblk = sb("onesblk", [128, 128], F32)
    mask01 = sb("mask01", [128, TT], F32)
    maskG = sb("maskG", [128, 128], F32)
    maskAT = sb("maskAT", [128, 64], F32)
    mo = sb("swa_mo", [128, 128], BF16)
    mp = sb("swa_mp", [128, 128], BF16)
    gbc = sb("gbc", [128, 4, D], F32)
    ppt = sb("ppt", [128, 42], F32)
    omu = sb("omu", [128, 14], F32)
    omka = sb("omka", [128, 4], F32)
    fpp = sb("fpp", [128, NFC * 4], F32)
    esink = sb("esink", [128, 8], F32)
    for dst, src in ((identb, c_identb), (identf, c_identf), (onesblk, c_onesblk), (mask01, c_mask01),
                     (maskG, c_maskG), (maskAT, c_maskAT), (mo, c_swa_mo), (mp, c_swa_mp), (ppt, pp0)):
        p.dma("sp", dst[:], src)
    p.dma("sp", esink[:], sinks[0].partition_broadcast(128))
    p.act(esink[:], esink[:], AF.Exp)
    p.ts("dve", omu[:], ppt[:, 0:14], -1.0, 1.0, ALU.mult, ALU.add)
    p.ts("dve", omka[:], ppt[:, 26:30], -1.0, 1.0, ALU.mult, ALU.add)
    MU, W0, A0, KK, KA, RK, LG, LB = 0, 14, 18, 22, 26, 30, 34, 38

    w2b = sb("w2b", [128, 512], BF16)
    g2b = sb("g2b", [128, 512], BF16)
    with contextlib.ExitStack() as st0:
        tmpf = sb("tmpf_l", [128, 512], F32, st0)
        p.dma("sp", tmpf[0:64, :], rw_w2)
        p.dma("sp", tmpf[64:128, :], rw_a2)
        p.cp("dve", w2b[:], tmpf[:])
        tmpg = sb("tmpg_l", [128, 512], F32, st0)
        p.dma("sp", tmpg[:], rw_g2)
        p.cp("dve", g2b[:], tmpg[:])
        p.barrier()

    def load_gains(layer):
        for i in range(4):
            p.dma("sp", gbc[:, i, :], gains[layer * 4 + i].partition_broadcast(128))

    def rmsnorm_to_hT(src, gi, hT, stk, pst):
        junk = sb("nrm_junk", [128, D], BF16, stk)
        ss = sb("nrm_ss", [128, NS], F32, stk)
        hb = sb("nrm_hb", [128, NS, D], BF16, stk)
        for s in range(NS):
            p.act(junk[:], src[:, s, :], AF.Square, accum_out=ss[:, s:s + 1])
        p.ts("dve", ss[:], ss[:], 1.0 / D, EPS, ALU.mult, ALU.add)
        p.act(ss[:], ss[:], AF.Sqrt)
        p.op("dve", "reciprocal", ss[:], ss[:])
        for s in range(NS):
            p.stt("dve" if s % 2 == 0 else "pool", hb[:, s, :], src[:, s, :], ss[:, s:s + 1], gbc[:, gi, :],
                  ALU.mult, ALU.mult)
        for kc in range(8):
            pt = pst[kc % 2]
            for s in range(NS):
                p.tr(pt[:, s * 128:(s + 1) * 128], hb[:, s, kc * 128:(kc + 1) * 128], identb[:])
            p.cp("act" if kc % 2 == 0 else "dve", hT[:, kc, :], pt[:, 0:TT])

    def post_norm_residual(pm, xres, gi, s, stk_tiles):
        junk, ss1, tmp = stk_tiles
        p.act(junk[:], pm, AF.Square, accum_out=ss1[:, s:s + 1])
        p.ts("dve", ss1[:, s:s + 1], ss1[:, s:s + 1], 1.0 / D, EPS, ALU.mult, ALU.add)
        p.act(ss1[:, s:s + 1], ss1[:, s:s + 1], AF.Sqrt)
        p.op("dve", "reciprocal", ss1[:, s:s + 1], ss1[:, s:s + 1])
        p.stt("dve", tmp[:], pm, ss1[:, s:s + 1], gbc[:, gi, :], ALU.mult, ALU.mult)
        p.tt("pool", xres[:, s, :], xres[:, s, :], tmp[:], ALU.add)

    def ffn(layer, xres, gcarry, first_tile):
        with contextlib.ExitStack() as stk:
            aT = sb("f_aT", [128, NFC, TT], BF16, stk)
            with contextlib.ExitStack() as stk2:
                pst = [ps(f"f_pt{i}", [128, 2 * TT], BF16, stk2) for i in range(2)]
                hT = sb("f_hT", [128, 8, TT], BF16, stk2)
                rmsnorm_to_hT(xres, 2, hT, stk2, pst)
                k.maybe_stop("f1")
                wu = [sb(f"f_wu{i}", [128, 8, 256], BF16, stk2) for i in range(3)]
                gb = [sb(f"f_gb{i}", [128, TT + 2], F32, stk2) for i in range(2)]
                cv = [sb(f"f_cv{i}", [128, TT], F32, stk2) for i in range(2)]
                ge = [sb(f"f_ge{i}", [128, TT], F32, stk2) for i in range(2)]
                pg = [ps(f"f_pg{i}", [128, TT], F32, stk2) for i in range(2)]
                pv = [ps(f"f_pv{i}", [128, TT], F32, stk2) for i in range(2)]
                wup = w_upb[layer].rearrange("(kc p) c -> p kc c", p=128)
                for fc in range(NFC):
                    w = wu[fc % 3]
                    p.dma("sp", w[:, :, 0:128], wup[:, :, fc * 128:(fc + 1) * 128])
                    p.dma("sp", w[:, :, 128:256], wup[:, :, DFF + fc * 128:DFF + (fc + 1) * 128])
                    g_, v_, b_, c_, e_ = pg[fc % 2], pv[fc % 2], gb[fc % 2], cv[fc % 2], ge[fc % 2]
                    for kc in range(8):
                        p.mm(g_[:], w[:, kc, 0:128], hT[:, kc, :], start=(kc == 0), stop=(kc == 7))
                    for kc in range(8):
                        p.mm(v_[:], w[:, kc, 128:256], hT[:, kc, :], start=(kc == 0), stop=(kc == 7))
                    if first_tile:
                        p.op("pool", "memset", b_[:, 0:2], 0.0)
                    else:
                        p.cp("pool", b_[:, 0:2], gcarry[:, fc, :])
                    p.cp("dve", b_[:, 2:TT + 2], g_[:])
                    p.cp("pool", gcarry[:, fc, :], b_[:, TT:TT + 2])
                    pc = fpp[:, fc * 4:fc * 4 + 4]
                    p.ts("dve", c_[:], g_[:], pc[:, 2:3], pc[:, 3:4], ALU.mult, ALU.add)
                    p.stt("dve", c_[:], b_[:, 1:TT + 1], pc[:, 1:2], c_[:], ALU.mult, ALU.add)
                    p.stt("pool", c_[:], b_[:, 0:TT], pc[:, 0:1], c_[:], ALU.mult, ALU.add)
                    p.act(e_[:], c_[:], AF.Gelu_apprx_tanh)
                    p.tt("dve", aT[:, fc, :], e_[:], v_[:], ALU.mult)
                    k.maybe_stop("f2")
                p.barrier()
            pacc = [ps(f"f_pa{i}", [128, 2, 512], F32, stk) for i in range(2)]
            k.maybe_stop("f3")
            wd = [sb(f"f_wd{i}", [128, 2, D], BF16, stk) for i in range(3)]
            junk = sb("f_junk", [128, D], BF16, stk)
            ss1 = sb("f_ss1", [128, NS], F32, stk)
            tmp = sb("f_tmp", [128, D], F32, stk)
            wdn = w_downb[layer].rearrange("(fc p) c -> p fc c", p=128)
            wi = 0
            for half in range(2):
                for f0 in range(0, NFC, 2):
                    w = wd[wi % 3]
                    wi += 1
                    p.dma("sp", w[:], wdn[:, f0:f0 + 2, :])
                    for ff in range(2):
                        fc = f0 + ff
                        for si in range(2):
                            s = half * 2 + si
                            for nh in range(2):
                                p.mm(pacc[si][:, nh, :], aT[:, fc, s * 128:(s + 1) * 128], w[:, ff, nh * 512:(nh + 1) * 512],
                                     start=(fc == 0), stop=(fc == NFC - 1))
                for si in range(2):
                    post_norm_residual(pacc[si][:], xres, 3, half * 2 + si, (junk, ss1, tmp))
                    k.maybe_stop("f4")
            p.barrier()

    def layer0():
        with contextlib.ExitStack() as L0:
            layer0_body(L0)
            p.barrier()

    def layer0_body(L0):
        k.zs = [sb(f"zs{j}", [128, TT], F32, L0) for j in range(14)]
        load_gains(0)
        p.dma("sp", fpp[:], ffn_pp[0])
        KaO = sb("KaO", [66, 2, TT], BF16, L0)
        KaP = sb("KaP", [66, 2, TT + 128], BF16, L0)
        Vsw = sb("Vsw", [128, NS + 1, 2, 65], BF16, L0)
        for g in range(2):
            p.dma("sp", KaO[64:66, g, :], c_swa_ko)
            for s in range(NS + 1):
                p.dma("sp", KaP[64:66, g, s * 128:(s + 1) * 128], c_swa_kp)
        p.op("pool", "memset", Vsw[:, :, :, 64:65], 1.0)
        zcarry = sb("zcarry", [128, 14], F32, L0)
        ST = sb("ST", [128, 4, 64], F32, L0)
        gcarry = sb("gcarry", [128, NFC, 2], F32, L0)
        oT = sb("oT", [128, 8, TT], BF16, L0)
        xt = sb("xt", [128, NS, D], F32, L0)
        win = w_in0b.rearrange("(kc p) c -> p kc c", p=128)

        for b in range(NB):
            for ti in range(NT):
                first = (ti == 0)
                t0 = ti * TT
                with contextlib.ExitStack() as stk:
                    pst = [ps(f"p1_pt{i}", [128, 2 * TT], BF16, stk) for i in range(2)]
                    pz = [ps(f"p1_pz{i}", [128, TT], F32, stk) for i in range(2)]
                    Qa = sb("Qa", [66, 8, TT], BF16, stk)
                    p.dma("sp", Qa[64:66, :, :], c_swa_q.rearrange("r (h t) -> r h t", h=8))
                    p.dma("sp", xt[:], x[b, t0:t0 + TT, :].rearrange("(s p) d -> p s d", p=128))
                    hT = sb("p1_hT", [128, 8, TT], BF16, stk)
                    rmsnorm_to_hT(xt, 0, hT, stk, pst)
                    maybe_stop("p1a", xt[:])
                    wq = [sb(f"p1_w{i}", [128, 8, 128], BF16, stk) for i in range(4)]
                    wcnt = [0]

                    def wload(c0, n):
                        w = wq[wcnt[0] % 4]
                        wcnt[0] += 1
                        p.dma("sp", w[:, :, 0:n], win[:, :, c0:c0 + n])
                        return w

                    def proj_fm(w, c0, n, dst_ps):
                        for kc in range(8):
                            p.mm(dst_ps, w[:, kc, c0:c0 + n], hT[:, kc, :], start=(kc == 0), stop=(kc == 7))

                    for hp in range(4):
                        w = wload(hp * 128, 128)
                        for hh in range(2):
                            h = hp * 2 + hh
                            z = pz[h % 2]
                            proj_fm(w, hh * 64, 64, z[0:64, :])
                            p.act(Qa[0:64, h, :], z[0:64, :], AF.Copy, scale=0.125)
                    maybe_stop("p1b1", xt[:])
                    if not first:
                        for g in range(2):
                            p.cp("pool", KaP[0:64, g, 0:128], KaP[0:64, g, TT:TT + 128])
                        p.cp("pool", Vsw[:, 0, :, 0:64], Vsw[:, NS, :, 0:64])
                    w = wload(512, 128)
                    for g in range(2):
                        z = pz[g % 2]
                        proj_fm(w, g * 64, 64, z[0:64, :])
                        p.cp("dve", KaO[0:64, g, :], z[0:64, :])
                        p.cp("dve", KaP[0:64, g, 128:128 + TT], z[0:64, :])
                    maybe_stop("p1b2", xt[:])
                    w = wload(640, 128)
                    for s in range(NS):
                        z = pz[s % 2]
                        for kc in range(8):
                            p.mm(z[:, 0:128], hT[:, kc, s * 128:(s + 1) * 128], w[:, kc, :], start=(kc == 0), stop=(kc == 7))
                        p.cp("dve", Vsw[:, s + 1, :, 0:64], z[:, 0:128].rearrange("p (g d) -> p g d", g=2))
                    maybe_stop("p1b", xt[:])
                    zs = k.zs
                    zb = [sb(f"p1_zb{i}", [128, TT + 1], F32, stk) for i in range(2)]
                    ztmp = [sb(f"p1_zt{i}", [128, TT], F32, stk) for i in range(2)]
                    for j in range(14):
                        w = wload(768 + j * 128, 128)
                        z = pz[j % 2]
                        proj_fm(w, 0, 128, z[:])
                        b_ = zb[j % 2]
                        tm = ztmp[j % 2]
                        p.act(b_[:, 1:TT + 1], z[:], AF.Copy)
                        if first:
                            p.op("pool", "memset", b_[:, 0:1], 0.0)
                        else:
                            p.cp("pool", b_[:, 0:1], zcarry[:, j:j + 1])
                        p.cp("pool", zcarry[:, j:j + 1], b_[:, TT:TT + 1])
                        p.ts("dve", tm[:], b_[:, 0:TT], ppt[:, MU + j:MU + j + 1], None, ALU.mult)
                        p.stt("dve", zs[j][:], b_[:, 1:TT + 1], omu[:, j:j + 1], tm[:], ALU.mult, ALU.add)
                    maybe_stop("p1c", xt[:])
                    pss = [ps(f"p1_ps{i}", [128, 4, 128], F32, stk) for i in range(2)]
                    po = [ps(f"p1_po{i}", [128, 4, 65], F32, stk) for i in range(2)]
                    pe_ = [sb(f"p1_pe{i}", [128, 4, 128], BF16, stk) for i in range(3)]
                    oa = sb("p1_oa", [128, 8, 64], BF16, stk)
                    den = sb("p1_den", [128, 8], F32, stk)
                    ecnt = 0
                    for s in range(NS):
                        has_prev = not (first and s == 0)
                        for g in range(2):
                            pog = po[g]
                            steps = ([(KaP[:, g, s * 128:(s + 1) * 128], Vsw[:, s, g, :], mp)] if has_prev else []) + \
                                    [(KaO[:, g, s * 128:(s + 1) * 128], Vsw[:, s + 1, g, :], mo)]
                            for si, (kap, vap, msk) in enumerate(steps):
                                sc = pss[ecnt % 2]
                                pt_ = pe_[ecnt % 3]
                                ecnt += 1
                                p.mm(sc[:], kap, Qa[:, g * 4:(g + 1) * 4, s * 128:(s + 1) * 128])
                                p.act(pt_[:], sc[:], AF.Exp)
                                p.tt("dve", pt_[:], pt_[:], bc(msk[:].unsqueeze(1), [128, 4, 128]), ALU.mult)
                                for hh in range(4):
                                    p.mm(pog[:, hh, :], pt_[:, hh, :], vap, start=(si == 0 and hh == 0), stop=(si == len(steps) - 1), sgc=True)
                            p.tt("dve", den[:, g * 4:(g + 1) * 4], pog[:, :, 64], esink[:, g * 4:(g + 1) * 4], ALU.add)
                            p.op("dve", "reciprocal", den[:, g * 4:(g + 1) * 4], den[:, g * 4:(g + 1) * 4])
                            p.tt("dve", oa[:, g * 4:(g + 1) * 4, :], pog[:, :, 0:64],
                                 bc(den[:, g * 4:(g + 1) * 4].unsqueeze(2), [128, 4, 64]), ALU.mult)
                        pt = pst[s % 2]
                        for kc in range(4):
                            p.tr(pt[:, kc * 128:(kc + 1) * 128], oa[:, 2 * kc:2 * kc + 2, :].rearrange("p h d -> p (h d)"), identb[:])
                        p.cp("act", oT[:, 0:4, s * 128:(s + 1) * 128], pt[:, 0:512].rearrange("p (c t) -> p c t", c=4))
                    p.barrier()
                maybe_stop("p1", xt[:])
                for hg in range(2):
                    with contextlib.ExitStack() as stk:
                        rwkv_tile(first, oT, ST, stk, hg)
                        p.barrier()
                maybe_stop("p2", xt[:])
                with contextlib.ExitStack() as stk:
                    pacc = [ps(f"p3_pa{i}", [128, 2, 512], F32, stk) for i in range(2)]
                    wout = sb("wout", [128, 8, D], BF16, stk)
                    p.dma("sp", wout[:], w_out0b.rearrange("(kc p) c -> p kc c", p=128))
                    junk = sb("p3_junk", [128, D], BF16, stk)
                    ss1 = sb("p3_ss1", [128, NS], F32, stk)
                    tmp = sb("p3_tmp", [128, D], F32, stk)
                    for s in range(NS):
                        pm = pacc[s % 2]
                        for nh in range(2):
                            for kc in range(8):
                                p.mm(pm[:, nh, :], oT[:, kc, s * 128:(s + 1) * 128], wout[:, kc, nh * 512:(nh + 1) * 512],
                                     start=(kc == 0), stop=(kc == 7))
                        post_norm_residual(pm[:], xt, 1, s, (junk, ss1, tmp))
                    p.barrier()
                maybe_stop("p3", xt[:])
                ffn(0, xt, gcarry, first)
                dst = x1 if n_layers > 1 else out
                p.dma("sp", dst[b, t0:t0 + TT, :].rearrange("(s p) d -> p s d", p=128), xt[:])
                p.barrier()


    def layer1():
        with contextlib.ExitStack() as L1:
            layer1_body(L1)
            p.barrier()

    def layer1_body(L1):
        load_gains(1)
        p.dma("sp", fpp[:], ffn_pp[1])
        NKT = T // 128
        xt = sb("n_xt", [128, NS, D], F32, L1)
        oT = sb("n_oT", [128, 8, TT], BF16, L1)
        gcarry = sb("n_gcarry", [128, NFC, 2], F32, L1)
        Ks = sb("n_Ks", [69, 4, T], BF16, L1)
        Vs = sb("n_Vs", [128, NKT, 4, 65], BF16, L1)
        Kw = sb("n_Kw", [69, 4, 6 * 128], BF16, L1)
        Vw = sb("n_Vw", [128, 6, 4, 65], BF16, L1)
        Kc = sb("n_Kc", [69, 4, 3, 128], BF16, L1)
        Vc = sb("n_Vc", [128, 3, 4, 129], BF16, L1)
        KC2 = sb("n_KC2", [128, 8, 16 + TT], BF16, L1)
        Ebl = sb("n_E", [64, T], BF16, L1)
        hbias = sb("n_hbias", [128, 4], F32, L1)
        w2c = sb("n_w2c", [128, 2, 2, 64], BF16, L1)
        p.dma("sp", Ebl[:], c_E)
        p.op("pool", "memset", Vs[:, :, :, 64:65], 1.0)
        p.op("pool", "memset", Vw[:, :, :, 64:65], 1.0)
        p.op("pool", "memset", Kc[:], 0.0)
        p.op("pool", "memset", Vc[:], 0.0)
        for g in range(4):
            p.dma("sp", Ks[64:69, g, :], c_nsa_k)
            p.dma("sp", Kc[64:69, g, :, :], c_cmp_k.rearrange("r (c s) -> r c s", c=3))
            for ch in range(3):
                p.dma("sp", Vc[:, ch, g, 65:129], c_mpool[:, ch * 64:(ch + 1) * 64])
        p.op("pool", "memset", Vc[:, :, :, 64:65], 1.0)
        win = nsa_w_inb.rearrange("(kc p) c -> p kc c", p=128)
        with contextlib.ExitStack() as stk:
            w1t = sb("n_w1t", [128, 16, 256], BF16, stk)
            posf = sb("n_posf", [128, 32], F32, stk)
            posb = sb("n_posb", [128, 32], BF16, stk)
            w2f = sb("n_w2f", [128, 2, 2, 64], F32, stk)
            pb_ = ps("n_pb", [128, 8], F32, stk)
            p.dma("sp", posf[:], cposT)
            p.cp("dve", posb[:], posf[:])
            for kv in range(2):
                p.dma("sp", w2f[:, kv, :, :], cw2[kv].rearrange("(c p) d -> p c d", p=128))
            p.cp("dve", w2c[:], w2f[:])
            for kv in range(2):
                p.dma("sp", w1t[:], cw1b[kv].rearrange("(m p) f -> p m f", p=128))
                for fx in range(2):
                    for m in range(16):
                        p.mm(pb_[:, kv * 2 + fx:kv * 2 + fx + 1], w1t[:, m, fx * 128:(fx + 1) * 128],
                             posb[:, kv * 16 + m:kv * 16 + m + 1], start=(m == 0), stop=(m == 15))
            p.cp("dve", hbias[:], pb_[:, 0:4])
            p.barrier()

        for b in range(NB):
            for ti in range(NT):
                first = (ti == 0)
                t0 = ti * TT
                with contextlib.ExitStack() as stkQ:
                    Qa = sb("n_Qa", [69, 16, TT], BF16, stkQ)
                    gates = sb("n_gates", [128, NS, 48], F32, stkQ)
                    with contextlib.ExitStack() as stk:
                        pst = [ps(f"n1_pt{i}", [128, 2 * TT], BF16, stk) for i in range(2)]
                        pz = [ps(f"n1_pz{i}", [128, TT], F32, stk) for i in range(2)]
                        p.dma("sp", Qa[64:69, :, :], c_nsa_q.rearrange("r (h t) -> r h t", h=16)[:, :, t0:t0 + TT])
                        p.dma("sp", xt[:], x1[b, t0:t0 + TT, :].rearrange("(s p) d -> p s d", p=128))
                        hT = sb("n1_hT", [128, 8, TT], BF16, stk)
                        rmsnorm_to_hT(xt, 0, hT, stk, pst)
                        wq = [sb(f"n1_w{i}", [128, 8, 128], BF16, stk) for i in range(4)]
                        wcnt = [0]

                        def wload(cols):
                            w = wq[wcnt[0] % 4]
                            wcnt[0] += 1
                            o = 0
                            for c0, n in cols:
                                p.dma("sp", w[:, :, o:o + n], win[:, :, c0:c0 + n])
                                o += n
                            return w

                        def proj_fm(w, c0, n, dst_ps):
                            for kc in range(8):
                                p.mm(dst_ps, w[:, kc, c0:c0 + n], hT[:, kc, :], start=(kc == 0), stop=(kc == 7))

                        zi = 0
                        for hp in range(8):
                            w = wload([(hp * 128, 128)])
                            for hh in range(2):
                                z = pz[zi % 2]; zi += 1
                                proj_fm(w, hh * 64, 64, z[0:64, :])
                                p.act(Qa[0:64, hp * 2 + hh, :], z[0:64, :], AF.Copy, scale=0.125)
                        for kv in range(2):
                            for g in range(4):
                                c0 = 1024 + kv * 256 + g * 64
                                w = wload([(c0, 64), (c0, 64)])
                                z = pz[zi % 2]; zi += 1
                                proj_fm(w, 0, 128, z[:])
                                kt_ = KC2[:, kv * 4 + g, :]
                                if first:
                                    p.op("pool", "memset", kt_[:, 0:16], 0.0)
                                else:
                                    p.cp("pool", kt_[0:64, 0:16], kt_[0:64, TT:TT + 16])
                                    p.cp("pool", kt_[64:128, 0:15], kt_[64:128, TT:TT + 15])
                                p.cp("dve", kt_[0:64, 16:16 + TT], z[0:64, :])
                                p.cp("dve", kt_[64:128, 15:15 + TT], z[64:128, :])
                        for which, dstK, cbase in (("s", Ks, 1536), ("w", Kw, 2048)):
                            if which == "w":
                                if not first:
                                    p.cp("pool", Kw[:, :, 0:256], Kw[:, :, 512:768])
                                    p.cp("pool", Vw[:, 0:2, :, 0:64], Vw[:, 4:6, :, 0:64])
                                for g in range(4):
                                    p.dma("sp", Kw[64:69, g, 256:768], c_nsa_k[:, t0:t0 + TT])
                                    if not first:
                                        p.dma("sp", Kw[64:69, g, 0:256], c_nsa_k[:, t0 - 256:t0])
                            for gp in range(2):
                                w = wload([(cbase + gp * 128, 128)])
                                for gg in range(2):
                                    g = gp * 2 + gg
                                    z = pz[zi % 2]; zi += 1
                                    proj_fm(w, gg * 64, 64, z[0:64, :])
                                    if which == "s":
                                        p.cp("dve", Ks[0:64, g, t0:t0 + TT], z[0:64, :])
                                    else:
                                        p.cp("dve", Kw[0:64, g, 256:768], z[0:64, :])
                        for which, cbase in (("s", 1792), ("w", 2304)):
                            wa = wload([(cbase, 128)])
                            wb_ = wload([(cbase + 128, 128)])
                            for s in range(NS):
                                z = pz[zi % 2]; zi += 1
                                for half, w in enumerate((wa, wb_)):
                                    for kc in range(8):
                                        p.mm(z[:, half * 128:(half + 1) * 128], hT[:, kc, s * 128:(s + 1) * 128], w[:, kc, :],
                                             start=(kc == 0), stop=(kc == 7))
                                src_ = z[:, 0:256].rearrange("p (g d) -> p g d", g=4)
                                if which == "s":
                                    p.cp("dve", Vs[:, ti * NS + s, :, 0:64], src_)
                                else:
                                    p.cp("dve", Vw[:, 2 + s, :, 0:64], src_)
                        w = wload([(2560, 48)])
                        for s in range(NS):
                            z = pz[zi % 2]; zi += 1
                            for kc in range(8):
                                p.mm(z[:, 0:48], hT[:, kc, s * 128:(s + 1) * 128], w[:, kc, 0:48], start=(kc == 0), stop=(kc == 7))
                            p.act(gates[:, s, :], z[:, 0:48], AF.Sigmoid)
                        p.barrier()
                    with contextlib.ExitStack() as stk:
                        w1t = sb("nc_w1t", [128, 16, 256], BF16, stk)
                        hid = sb("nc_hid", [128, 2, 32], BF16, stk)
                        ph = [ps(f"nc_ph{i}", [128, 32], F32, stk) for i in range(2)]
                        pk = ps("nc_pk", [128, 64], F32, stk)
                        chn, q3 = ti // 3, ti % 3
                        for kv in range(2):
                            p.dma("sp", w1t[:], cw1b[kv].rearrange("(m p) f -> p m f", p=128))
                            for g in range(4):
                                kt_ = KC2[:, kv * 4 + g, :]
                                for fx in range(2):
                                    for m in range(16):
                                        if 2 * m < 16:
                                            rhs = kt_[:, 0:TT].rearrange("p (j s) -> p j s", s=16)[:, :, 2 * m]
                                        else:
                                            rhs = kt_[:, 16:16 + TT].rearrange("p (j s) -> p j s", s=16)[:, :, 2 * m - 16]
                                        p.mm(ph[fx][:], w1t[:, m, fx * 128:(fx + 1) * 128], rhs, start=(m == 0), stop=(m == 15))
                                    p.act(hid[:, fx, :], ph[fx][:], AF.Gelu_apprx_tanh, bias=hbias[:, kv * 2 + fx:kv * 2 + fx + 1])
                                if kv == 0:
                                    for fx in range(2):
                                        p.mm(pk[0:64, 0:32], w2c[:, 0, fx, :], hid[:, fx, :], start=(fx == 0), stop=(fx == 1))
                                    p.cp("dve", Kc[0:64, g, chn, q3 * 32:(q3 + 1) * 32], pk[0:64, 0:32])
                                else:
                                    P32 = slice(q3 * 32, (q3 + 1) * 32)
                                    for fx in range(2):
                                        p.mm(pk[P32, 0:64], hid[:, fx, :], w2c[:, 1, fx, :], start=(fx == 0), stop=(fx == 1))
                                    p.cp("dve", Vc[P32, chn, g, 0:64], pk[P32, 0:64])
                        p.barrier()
                    with contextlib.ExitStack() as stk:
                        pss = [ps(f"n2_ps{i}", [128, 4, 128], F32, stk) for i in range(2)]
                        pm = ps("n2_pm", [128, 4, 128], F32, stk)
                        poA = ps("n2_poA", [128, 4, 65], F32, stk)
                        poB = ps("n2_poB", [128, 4, 64], F32, stk)
                        pos_ = ps("n2_pos", [128, 4, 65], F32, stk)
                        pow_ = ps("n2_pow", [128, 4, 65], F32, stk)
                        ptt = ps("n2_pt", [128, 2 * TT], BF16, stk)
                        pe_ = [sb(f"n2_pe{i}", [128, 4, 128], BF16, stk) for i in range(3)]
                        cm = [sb(f"n2_cm{i}", [128, 128], BF16, stk) for i in range(3)]
                        Ft = sb("n2_F", [128, 64], F32, stk)
                        imp = sb("n2_imp", [128, 64], F32, stk)
                        m8 = sb("n2_m8", [128, 8], F32, stk)
                        sel = sb("n2_sel", [128, 64], BF16, stk)
                        selT = sb("n2_selT", [64, 128], BF16, stk)
                        mdg = sb("n2_mdg", [128, 128], BF16, stk)
                        rc = sb("n2_rc", [128, 3, 4], F32, stk)
                        cf = sb("n2_cf", [128, 3, 4], F32, stk)
                        ot = sb("n2_ot", [128, 16, 64], BF16, stk)
                        o1 = sb("n2_o1", [128, 4, 64], F32, stk)
                        o2 = sb("n2_o2", [128, 4, 64], F32, stk)
                        ec = [0]

                        def score(st):
                            sc = pss[ec[0] % 2]
                            pt_ = pe_[ec[0] % 3]
                            ec[0] += 1
                            if st.get("pre"):
                                st["pre"]()
                            p.mm(sc[:], st["kap"], st["qap"])
                            if st["clamp"]:
                                p.ts("dve", sc[:], sc[:], 60.0, None, ALU.min)
                            return sc, pt_

                        def finish(st, bufs):
                            sc, pt_ = bufs
                            p.act(pt_[:], sc[:], AF.Exp)
                            for msk in st["masks"]:
                                p.tt("dve", pt_[:], pt_[:], bc(msk.unsqueeze(1), [128, 4, 128]), ALU.mult)
                            for (po_, wdt, rows, rhs) in st["pv"]:
                                for hh in range(4):
                                    p.mm(po_[:, hh, 0:wdt], pt_[0:rows, hh, :], rhs, start=(st["first"] and hh == 0), stop=st["last"], sgc=True)

                        def run_steps(seq):
                            pending = None
                            for st in seq:
                                if callable(st):
                                    if pending:
                                        finish(*pending)
                                        pending = None
                                    st()
                                    continue
                                bufs = score(st)
                                if pending:
                                    finish(*pending)
                                pending = (st, bufs)
                            if pending:
                                finish(*pending)

                        for s in range(NS):
                            qt = ti * NS + s
                            qs = slice(s * 128, (s + 1) * 128)
                            p.dma("sp", Ft[:], c_sel_F[qt])
                            for g in range(4):
                                qap = Qa[:, g * 4:(g + 1) * 4, qs]
                                seq = []
                                chmax = ((8 * qt + 7) // 32) // 3
                                for ch in range(chmax + 1):
                                    cmk = cm[ch % 3]

                                    def pre_cmp(cmk=cmk, ch=ch):
                                        p.dma("sp", cmk[:], c_cmp_mask[ch, :, qt * 128:(qt + 1) * 128])
                                    seq.append(dict(kap=Kc[:, g, ch, :], qap=qap, clamp=True, masks=[cmk[:]], pre=pre_cmp,
                                                    pv=[(poA, 65, 96, Vc[0:96, ch, g, 0:65]), (poB, 64, 96, Vc[0:96, ch, g, 65:129])],
                                                    first=(ch == 0), last=(ch == chmax)))

                                def selection():
                                    p.ts("dve", rc[:, 0, :], poA[:, :, 64], 1e-30, None, ALU.max)
                                    p.op("dve", "reciprocal", rc[:, 0, :], rc[:, 0, :])
                                    p.ts("dve", imp[:], poB[:, 0, :], rc[:, 0, 0:1], None, ALU.mult)
                                    for hh in range(1, 4):
                                        p.stt("dve", imp[:], poB[:, hh, :], rc[:, 0, hh:hh + 1], imp[:], ALU.mult, ALU.add)
                                    p.tt("dve", imp[:], imp[:], Ft[:], ALU.add)
                                    p.op("dve", "max", m8[:], imp[:])
                                    p.ts("dve", sel[:], imp[:], m8[:, 7:8], None, ALU.is_ge)
                                    p.tr(ptt[0:64, 0:128], sel[:], identb[:])
                                    p.cp("dve", selT[:], ptt[0:64, 0:128])
                                seq.append(selection)
                                kts = [kk_ for kk_ in (qt - 2, qt - 1, qt) if kk_ >= 0]
                                for i_, kt in enumerate(kts):
                                    dl = qt - kt
                                    slot = 2 + s - dl
                                    masks = [mo[:]] if dl == 0 else ([mp[:]] if dl == 2 else [])
                                    seq.append(dict(kap=Kw[:, g, slot * 128:(slot + 1) * 128], qap=qap, clamp=(dl == 0), masks=masks,
                                                    pv=[(pow_, 65, 128, Vw[:, slot, g, :])], first=(i_ == 0), last=(i_ == len(kts) - 1)))
                                for kt in range(qt + 1):
                                    mslot = pm[:, kt % 4, :]

                                    def pre_slc(kt=kt, mslot=mslot):
                                        p.mm(mslot, Ebl[:, kt * 128:(kt + 1) * 128], selT[:])
                                        if kt == qt:
                                            p.tt("dve", mdg[:], mslot, mo[:], ALU.mult)
                                    seq.append(dict(kap=Ks[:, g, kt * 128:(kt + 1) * 128], qap=qap, clamp=(kt == qt),
                                                    masks=[mdg[:] if kt == qt else mslot], pre=pre_slc,
                                                    pv=[(pos_, 65, 128, Vs[:, kt, g, :])], first=(kt == 0), last=(kt == qt)))
                                run_steps(seq)
                                p.op("dve", "reciprocal", rc[:, 1, :], pos_[:, :, 64])
                                p.op("dve", "reciprocal", rc[:, 2, :], pow_[:, :, 64])
                                gv = gates[:, s, g * 12:(g + 1) * 12].rearrange("p (r n) -> p n r", n=3)
                                p.tt("dve", cf[:], rc[:], gv, ALU.mult)
                                p.tt("dve", o1[:], poA[:, :, 0:64], bc(cf[:, 0, :].unsqueeze(2), [128, 4, 64]), ALU.mult)
                                p.tt("dve", o2[:], pos_[:, :, 0:64], bc(cf[:, 1, :].unsqueeze(2), [128, 4, 64]), ALU.mult)
                                p.tt("pool", o1[:], o1[:], o2[:], ALU.add)
                                p.tt("dve", o2[:], pow_[:, :, 0:64], bc(cf[:, 2, :].unsqueeze(2), [128, 4, 64]), ALU.mult)
                                p.tt("pool", ot[:, g * 4:(g + 1) * 4, :], o1[:], o2[:], ALU.add)
                            for kc in range(8):
                                p.tr(ptt[:, kc * 128:(kc + 1) * 128], ot[:, 2 * kc:2 * kc + 2, :].rearrange("p h d -> p (h d)"), identb[:])
                            p.cp("act", oT[:, :, qs], ptt[:, 0:1024].rearrange("p (c t) -> p c t", c=8))
                        p.barrier()
                with contextlib.ExitStack() as stk:
                    pacc = [ps(f"n3_pa{i}", [128, 2, 512], F32, stk) for i in range(2)]
                    wout = sb("n_wout", [128, 8, D], BF16, stk)
                    p.dma("sp", wout[:], nsa_w_outb.rearrange("(kc p) c -> p kc c", p=128))
                    junk = sb("n3_junk", [128, D], BF16, stk)
                    ss1 = sb("n3_ss1", [128, NS], F32, stk)
                    tmp = sb("n3_tmp", [128, D], F32, stk)
                    for s in range(NS):
                        pm_ = pacc[s % 2]
                        for nh in range(2):
                            for kc in range(8):
                                p.mm(pm_[:, nh, :], oT[:, kc, s * 128:(s + 1) * 128], wout[:, kc, nh * 512:(nh + 1) * 512],
                                     start=(kc == 0), stop=(kc == 7))
                        post_norm_residual(pm_[:], xt, 1, s, (junk, ss1, tmp))
                    p.barrier()
                ffn(1, xt, gcarry, first)
                p.dma("sp", out[b, t0:t0 + TT, :].rearrange("(s p) d -> p s d", p=128), xt[:])
                p.barrier()

    def rwkv_tile(first, oT, ST, stk, hg):
        zs = k.zs
        NCH = TT // 64
        pa = [ps(f"r_pa{i}", [128, TT], F32, stk) for i in range(2)]
        pq = ps("r_pq", [128, 4, 64], F32, stk)
        pqt = ps("r_pqt", [128, 4, 64], F32, stk)
        pp_ = ps("r_pp", [128, 2, 4, 64], F32, stk)
        pg1 = ps("r_pg1", [128, 4, 128], F32, stk)
        pg2 = ps("r_pg2", [128, 4, 128], F32, stk)
        py = ps("r_py", [128, 2, 4, 64], F32, stk)
        AR = sb("r_AR", [128, 2, NCH, 2, 64], BF16, stk)
        BK = sb("r_BK", [128, 2, NCH, 2, 64], BF16, stk)
        TM = sb("r_TM", [128, NS, 3, 256], BF16, stk)
        STb = sb("r_STb", [128, 4, 64], BF16, stk)
        GC = sb("r_GC", [128, 2, NCH], F32, stk)
        bvT = sb("r_bvT", [128, 2, TT], BF16, stk)
        gT = sb("r_gT", [128, 2, TT], BF16, stk)
        if first and hg == 0:
            p.op("pool", "memset", ST[:], 0.0)
        twa = sb("r_twa", [128, TT], BF16, stk)
        sgl = sb("r_sgl", [128, TT], BF16, stk)
        p.act(twa[0:64, :], zs[12][0:64, :], AF.Tanh)
        p.cp("dve", twa[64:128, :], zs[12][64:128, :])
        p.act(sgl[:], zs[13][:], AF.Sigmoid)
        k.maybe_stop("r1")
        t = {n: sb("r_" + n, [128, TT], F32, stk) for n in
             ("e", "L", "a", "kk", "nrm", "kkn", "tmp", "kmod", "ka", "G1", "Gx", "bh", "kh")}
        v3 = lambda a_: a_.rearrange("p (c t) -> p c t", t=64)
        for cl in range(2):
            cc = hg * 2 + cl
            r_s, k_s, v_s = zs[cc], zs[4 + cc], zs[8 + cc]
            csl = slice(cc * 128, (cc + 1) * 128)
            lsl = slice(cl * 128, (cl + 1) * 128)
            p.mm(pa[0][:], w2b[0:64, csl], twa[0:64, :])
            p.act(t["e"][:], pa[0][:], AF.Sigmoid, bias=ppt[:, W0 + cc:W0 + cc + 1])
            p.ts("dve", t["e"][:], t["e"][:], -float(np.exp(-0.5)), None, ALU.mult)
            p.op("dve", "tensor_tensor_scan", t["L"][:], mask01[:], t["e"][:], 0.0, ALU.mult, ALU.add)
            p.mm(pa[1][:], w2b[64:128, csl], twa[64:128, :])
            p.act(t["a"][:], pa[1][:], AF.Sigmoid, bias=ppt[:, A0 + cc:A0 + cc + 1])
            p.mm(pa[0][:], g2b[:, csl], sgl[:])
            p.cp("act", gT[:, cl, :], pa[0][:])
            p.ts("pool", t["kk"][:], k_s[:], ppt[:, KK + cc:KK + cc + 1], None, ALU.mult)
            p.tt("pool", t["tmp"][:], t["kk"][:], t["kk"][:], ALU.mult)
            p.mm(pa[1][:], onesblk[:], t["tmp"][:])
            p.act(t["nrm"][:], pa[1][:], AF.Sqrt)
            p.ts("dve", t["nrm"][:], t["nrm"][:], 1e-12, None, ALU.max)
            p.op("dve", "reciprocal", t["nrm"][:], t["nrm"][:])
            p.tt("dve", t["kkn"][:], t["kk"][:], t["nrm"][:], ALU.mult)
            p.ts("pool", t["tmp"][:], t["a"][:], ppt[:, KA + cc:KA + cc + 1], omka[:, cc:cc + 1], ALU.mult, ALU.add)
            p.tt("pool", t["kmod"][:], k_s[:], t["tmp"][:], ALU.mult)
            p.stt("dve", t["tmp"][:], r_s[:], ppt[:, RK + cc:RK + cc + 1], t["kmod"][:], ALU.mult, ALU.mult)
            p.mm(pa[0][:], onesblk[:], t["tmp"][:])
            p.tt("dve", bvT[:, cl, :], pa[0][:], v_s[:], ALU.mult)
            p.tt("pool", t["ka"][:], t["kkn"][:], t["a"][:], ALU.mult)
            p.act(t["G1"][:], t["L"][:], AF.Exp)
            p.cp("pool", GC[:, cl, :], v3(t["G1"][:])[:, :, 63])
            p.tt("pool", AR[:, cl, :, 1, :], v3(r_s[:]), v3(t["G1"][:]), ALU.mult)
            p.tt("pool", t["tmp"][:], t["L"][:], t["e"][:], ALU.subtract)
            p.act(t["Gx"][:], t["tmp"][:], AF.Exp)
            p.stt("dve", AR[:, cl, :, 0, :], v3(t["kkn"][:]), -1.0, v3(t["Gx"][:]), ALU.mult, ALU.mult)
            p.act(t["Gx"][:], t["L"][:], AF.Exp, scale=-1.0)
            p.tt("dve", BK[:, cl, :, 0, :], v3(t["ka"][:]), v3(t["Gx"][:]), ALU.mult)
            p.tt("pool", BK[:, cl, :, 1, :], v3(t["kmod"][:]), v3(t["Gx"][:]), ALU.mult)
            L3 = v3(t["L"][:])
            p.tt("dve", v3(t["tmp"][:]), bc(L3[:, :, 63:64], [128, NCH, 64]), L3, ALU.subtract)
            p.act(t["Gx"][:], t["tmp"][:], AF.Exp)
            p.tt("dve", t["bh"][:], t["ka"][:], t["Gx"][:], ALU.mult)
            p.tt("pool", t["kh"][:], t["kmod"][:], t["Gx"][:], ALU.mult)
            for qi, src_ in enumerate((t["bh"], t["kh"], v_s)):
                pz = pa[qi % 2]
                for s in range(NS):
                    p.tr(pz[:, s * 128:(s + 1) * 128], src_[:, s * 128:(s + 1) * 128], identf[:])
                p.cp("act" if qi % 2 == 0 else "dve", TM[:, :, qi, lsl], pz[:].rearrange("p (s c) -> p s c", s=NS))
            k.maybe_stop("r2")
        Gs1 = [sb(f"r_Gs1_{i}", [128, 4, 128], BF16, stk) for i in range(2)]
        Gs2 = [sb(f"r_Gs2_{i}", [128, 4, 128], BF16, stk) for i in range(2)]
        Qb = [sb(f"r_Q{i}", [128, 4, 64], BF16, stk) for i in range(2)]
        QTb = [sb(f"r_QT{i}", [128, 4, 64], BF16, stk) for i in range(2)]
        Pm = [sb(f"r_P{i}", [128, 4, 64], F32, stk) for i in range(2)]
        Pb = [sb(f"r_Pb{i}", [128, 4, 64], BF16, stk) for i in range(2)]
        RH = sb("r_RH", [128, 4, 64], BF16, stk)
        UT = sb("r_UT", [128, 4, 64], BF16, stk)
        ysb = sb("r_ysb", [128, 4, 64], F32, stk)
        ysq = sb("r_ysq", [128, 4, 64], F32, stk)
        st1 = sb("r_st1", [128, 4], F32, stk)
        st2 = sb("r_st2", [128, 4], F32, stk)
        st3 = sb("r_st3", [128, 4], F32, stk)
        ob = sb("r_ob", [128, 128], F32, stk)
        HI = [(h // 2, (h % 2) * 64) for h in range(4)]
        p.cp("pool", STb[:], ST[:])
        Tb = [sb(f"r_Tb{i}", [128, 4, 64], BF16, stk) for i in range(2)]
        prh = pa[1][:].rearrange("p (a h d) -> p a h d", a=2, h=4)
        pu = pa[0][:, 128:384].rearrange("p (h d) -> p h d", h=4)

        def prepT(c):
            pc = (c % 2) * 64
            R = slice(pc, pc + 64)
            G1s, G2s = Gs1[c % 2], Gs2[c % 2]
            for h in (0, 2, 1, 3):
                cl, pb = HI[h]
                H = slice(pb, pb + 64)
                ar = AR[H, cl, c, :, :].rearrange("p a t -> p (a t)")
                p.mm(pg1[R, h, :], BK[H, cl, c, 0, :], ar)
                p.mm(pg2[R, h, :], BK[H, cl, c, 1, :], ar)
                p.mm(pqt[R, h, :], AR[H, cl, c, 0, :], BK[H, cl, c, 0, :])
            p.tt("dve", G1s[R, :, :], pg1[R, :, :], bc(maskG[R, :].unsqueeze(1), [64, 4, 128]), ALU.mult)
            p.tt("dve", G2s[R, :, :], pg2[R, :, :], bc(maskG[R, :].unsqueeze(1), [64, 4, 128]), ALU.mult)
            Q, QT, P_ = Qb[0], QTb[0], Pm[0]
            p.tt("dve", QT[R, :, :], pqt[R, :, :], bc(maskAT[R, :].unsqueeze(1), [64, 4, 64]), ALU.mult)
            p.cp("act", Q[R, :, :], G1s[R, :, 0:64])
            p.tt("pool", P_[R, :, :], G1s[R, :, 0:64], bc(identf[R, pc:pc + 64].unsqueeze(1), [64, 4, 64]), ALU.add)
            p.tt("pool", Pb[0][R, :, :], G1s[R, :, 0:64], bc(identf[R, pc:pc + 64].unsqueeze(1), [64, 4, 64]), ALU.add)
            yield
            cur = 0
            for lv in range(1, 6):
                Qo, QTo = Qb[cur], QTb[cur]
                Qn, QTn = Qb[1 - cur], QTb[1 - cur]
                for h in (0, 2, 1, 3):
                    if lv < 5:
                        p.mm(pq[R, h, :], QTo[R, h, :], Qo[R, h, :])
                    p.mm(pqt[R, h, :], Qo[R, h, :], QTo[R, h, :])
                if lv < 5:
                    p.cp("act", Qn[R, :, :], pq[R, :, :])
                p.cp("dve", QTn[R, :, :], pqt[R, :, :])
                yield
                Po, Pn = Pm[cur], Pm[1 - cur]
                for h in (0, 2, 1, 3):
                    p.mm(pp_[R, 0, h, :], QTn[R, h, :], Pb[cur][R, h, :])
                if lv < 5:
                    p.tt("dve", Pn[R, :, :], pp_[R, 0, :, :], Po[R, :, :], ALU.add)
                    p.tt("dve", Pb[1 - cur][R, :, :], pp_[R, 0, :, :], Po[R, :, :], ALU.add)
                else:
                    p.tt("dve", Tb[c % 2][R, :, :], pp_[R, 0, :, :], Po[R, :, :], ALU.add)
                cur = 1 - cur
                yield

        def chain(c):
            pc = (c % 2) * 64
            R = slice(pc, pc + 64)
            s = c // 2
            G1s, G2s = Gs1[c % 2], Gs2[c % 2]
            Tm = Tb[c % 2]
            for h in (0, 2, 1, 3):
                cl, pb = HI[h]
                H = slice(pb, pb + 64)
                cc = hg * 2 + cl
                p.mm(prh[R, 0, h, :], AR[H, cl, c, 0, :], STb[H, cc, :])
                p.mm(prh[R, 1, h, :], G2s[R, h, 0:64], TM[R, s, 2, h * 64:(h + 1) * 64])
            p.cp("act", RH[R, :, :], prh[R, 0, :, :])
            p.tt("dve", RH[R, :, :], RH[R, :, :], prh[R, 1, :, :], ALU.add)
            yield
            for h in (0, 2, 1, 3):
                p.mm(pu[R, h, :], Tm[R, h, :], RH[R, h, :])
            p.cp("dve", UT[R, :, :], pu[R, :, :])
            yield
            for h in (0, 2, 1, 3):
                cl, pb = HI[h]
                H = slice(pb, pb + 64)
                cc = hg * 2 + cl
                p.mm(py[R, 0, h, :], AR[H, cl, c, 1, :], STb[H, cc, :])
            for h in (0, 2, 1, 3):
                p.mm(py[R, 1, h, :], G1s[R, h, 64:128], UT[R, h, :], start=True, stop=False)
                p.mm(py[R, 1, h, :], G2s[R, h, 64:128], TM[R, s, 2, h * 64:(h + 1) * 64], start=False, stop=True)
            for h in (0, 2, 1, 3):
                cl, pb = HI[h]
                H = slice(pb, pb + 64)
                p.mm(pa[0][H, cl * 64:(cl + 1) * 64], TM[R, s, 0, h * 64:(h + 1) * 64], UT[R, h, :], start=True, stop=False)
                p.mm(pa[0][H, cl * 64:(cl + 1) * 64], TM[R, s, 1, h * 64:(h + 1) * 64], TM[R, s, 2, h * 64:(h + 1) * 64], start=False, stop=True)
            for cl in range(2):
                cc = hg * 2 + cl
                p.stt("dve", ST[:, cc, :], ST[:, cc, :], GC[:, cl, c:c + 1], pa[0][:, cl * 64:(cl + 1) * 64], ALU.mult, ALU.add)
                p.cp("pool", STb[:, cc, :], ST[:, cc, :])
            yield
            if c % 2 == 1:
                p.cp("act", ysb[:], py[:, 0, :, :])
                p.tt("dve", ysb[:], ysb[:], py[:, 1, :, :], ALU.add)
                p.op("dve", "tensor_reduce", st1[:], ysb[:], AX.X, ALU.add)
                p.act(ysq[:], ysb[:], AF.Square)
                p.op("dve", "tensor_reduce", st2[:], ysq[:], AX.X, ALU.add)
                p.ts("dve", st1[:], st1[:], 1.0 / 64, None, ALU.mult)
                p.tt("dve", st3[:], st1[:], st1[:], ALU.mult)
                p.stt("dve", st2[:], st2[:], 1.0 / 64, st3[:], ALU.mult, ALU.subtract)
                p.ts("dve", st2[:], st2[:], GN_EPS, None, ALU.add)
                p.act(st2[:], st2[:], AF.Sqrt)
                p.op("dve", "reciprocal", st2[:], st2[:])
                p.tt("dve", ysb[:], ysb[:], bc(st1[:].unsqueeze(2), [128, 4, 64]), ALU.subtract)
                p.tt("dve", ysb[:], ysb[:], bc(st2[:].unsqueeze(2), [128, 4, 64]), ALU.mult)
                yield
                pz = pa[1]
                for cl in range(2):
                    p.tr(pz[:, cl * 128:(cl + 1) * 128], ysb[:, 2 * cl:2 * cl + 2, :].rearrange("p h d -> p (h d)"), identf[:])
                for cl in range(2):
                    cc = hg * 2 + cl
                    tsl = slice(s * 128, (s + 1) * 128)
                    p.ts("dve", ob[:], pz[:, cl * 128:(cl + 1) * 128], ppt[:, LG + cc:LG + cc + 1], ppt[:, LB + cc:LB + cc + 1], ALU.mult, ALU.add)
                    p.tt("pool", ob[:], ob[:], bvT[:, cl, tsl], ALU.add)
                    p.tt("pool", oT[:, 4 + cc, tsl], ob[:], gT[:, cl, tsl], ALU.mult)
                yield

        for _ in prepT(0):
            pass
        for c in range(NCH):
            gA = chain(c)
            gB = prepT(c + 1) if c + 1 < NCH else iter(())
            doneA = doneB = False
            while not (doneA and doneB):
                if not doneB:
                    try:
                        next(gB)
                    except StopIteration:
                        doneB = True
                if not doneA:
                    try:
                        next(gA)
                    except StopIteration:
                        doneA = True

    k.maybe_stop = maybe_stop
    try:
        maybe_stop("prepass")
        layer0()
        if n_layers > 1:
            layer1()
    except Stop:
        pass
    p.barrier()
    k.ninst = p.ninst
    return nc, k


def _consts():
    c = {}
    c["c_identb"] = np.eye(128, dtype=np.float32).astype(NPBF)
    c["c_identf"] = np.eye(128, dtype=np.float32)
    ob = np.zeros((128, 128), np.float32)
    ob[:64, :64] = 1
    ob[64:, 64:] = 1
    c["c_onesblk"] = ob
    m = np.ones((128, TT), np.float32)
    m[:, ::64] = 0
    c["c_mask01"] = m
    s = (np.arange(128) % 64)[:, None]
    t = np.arange(64)[None, :]
    c["c_maskG"] = np.concatenate([(s < t), (s <= t)], axis=1).astype(np.float32)
    c["c_maskAT"] = (t < s).astype(np.float32)
    slopes = 2.0 ** (-(np.arange(8) + 1.0))
    ql = (np.arange(TT) % 128).astype(np.float32)
    q = np.zeros((2, 8, TT), np.float32)
    q[0] = slopes[:, None]
    q[1] = -slopes[:, None] * ql[None, :]
    c["c_swa_q"] = q.reshape(2, 8 * TT).astype(NPBF)
    c["c_swa_ko"] = np.stack([ql, np.ones(TT, np.float32)]).astype(NPBF)
    c["c_swa_kp"] = np.stack([np.arange(128, dtype=np.float32) - 128, np.ones(128, np.float32)]).astype(NPBF)
    kl = np.arange(128)[:, None]
    qq = np.arange(128)[None, :]
    c["c_swa_mo"] = (kl <= qq).astype(np.float32).astype(NPBF)
    c["c_swa_mp"] = (kl > qq).astype(np.float32).astype(NPBF)
    return c


def _bf_split(v):
    hi = v.astype(NPBF)
    lo = (v - hi.astype(np.float32)).astype(NPBF)
    return hi, lo


def _consts_nsa(T):
    c = {}
    slopes = (2.0 ** (-8.0 * (np.arange(16) + 1.0) / 16)).astype(np.float32)
    s1, s2 = _bf_split(slopes)
    t = np.arange(T, dtype=np.float32)
    q = np.zeros((5, 16, T), np.float32)
    q[0] = s1.astype(np.float32)[:, None]
    q[1] = s2.astype(np.float32)[:, None]
    q[2] = q[0]
    q[3] = q[1]
    q[4] = -slopes[:, None] * t[None, :]
    c["c_nsa_q"] = q.reshape(5, 16 * T).astype(NPBF)

    def krows(pos):
        lo = pos % 128
        hi = pos - lo
        return np.stack([lo, lo, hi, hi, np.ones_like(pos)]).astype(np.float32)

    c["c_nsa_k"] = krows(t).astype(NPBF)
    sp_slot = -np.ones(384, np.int64)
    for ti in range(8):
        for j in range(32):
            sp_slot[(ti // 3) * 128 + 32 * (ti % 3) + j] = 32 * ti + j
    valid_sp = sp_slot >= 1
    posc = np.where(valid_sp, 16.0 * sp_slot + 15.0, 0.0).astype(np.float32)
    kr = krows(posc)
    kr[:, ~valid_sp] = 0
    c["c_cmp_k"] = kr.astype(NPBF)
    tt_ = np.arange(T)[None, :]
    msk = valid_sp[:, None] & (tt_ >= (16 * sp_slot[:, None] + 15))
    c["c_cmp_mask"] = msk.astype(np.float32).reshape(3, 128, T).astype(NPBF)
    mp_ = np.zeros((384, 64), np.float32)
    wts = {-1: 1.0, 0: 2.0, 1: 2.0, 2: 2.0, 3: 1.0}
    for sp in range(384):
        if sp_slot[sp] >= 1:
            cidx = sp_slot[sp] - 1
            for blk in range(64):
                m = cidx - 4 * blk
                if m in wts:
                    mp_[sp, blk] = wts[m]
    c["c_mpool"] = np.ascontiguousarray(mp_.reshape(3, 128, 64).transpose(1, 0, 2).reshape(128, 192)).astype(NPBF)
    F = np.zeros((T // 128, 128, 64), np.float32)
    for qt in range(T // 128):
        cur = (qt * 128 + np.arange(128)) // 64
        blk = np.arange(64)[None, :]
        forced = (blk == 0) | (blk == cur[:, None]) | (blk == cur[:, None] - 1)
        F[qt] = np.where(forced, 1e30, np.where(blk > cur[:, None], -1e30, 0.0))
    c["c_sel_F"] = F
    c["c_E"] = (np.arange(T)[None, :] // 64 == np.arange(64)[:, None]).astype(np.float32).astype(NPBF)
    return c


def _prep_shared(inp, n_layers=2):
    f = lambda a: np.ascontiguousarray(np.asarray(a, dtype=np.float32))
    sh = dict(_consts())
    T = np.asarray(inp["x"]).shape[1]
    gains = np.zeros((8, D), np.float32)
    for l in range(2):
        gains[l * 4 + 0] = inp["mix_pre_g"][l]
        gains[l * 4 + 1] = inp["mix_post_g"][l]
        gains[l * 4 + 2] = inp["ffn_pre_g"][l]
        gains[l * 4 + 3] = inp["ffn_post_g"][l]
    sh["gains"] = gains
    sh["hy_w_in"] = f(inp["hy_w_in"][0])
    sh["hy_w_out"] = f(inp["hy_w_out"][0])
    sh["ffn_w_up"] = f(inp["ffn_w_up"])
    sh["ffn_w_down"] = f(inp["ffn_w_down"])
    fp = np.zeros((2, 128, NFC, 4), np.float32)
    for l in range(2):
        cw = np.asarray(inp["ffn_conv_w"][l]).reshape(3, NFC, 128)
        fp[l, :, :, 0:3] = cw.transpose(2, 1, 0)
        fp[l, :, :, 3] = np.asarray(inp["ffn_conv_b"][l]).reshape(NFC, 128).T
    sh["ffn_pp"] = fp.reshape(2, 128, NFC * 4)
    pp = np.zeros((128, 42), np.float32)
    pp[:, 0:14] = np.asarray(inp["rwkv_mu"][0]).reshape(14, 128).T
    for i, nm in enumerate(["rwkv_w0", "rwkv_a0", "rwkv_k_k", "rwkv_k_a", "rwkv_r_k", "rwkv_ln_g", "rwkv_ln_b"]):
        pp[:, 14 + 4 * i:18 + 4 * i] = np.asarray(inp[nm][0]).reshape(4, 128).T
    sh["pp0"] = pp
    sh["swa_sinks"] = f(inp["swa_sinks"])
    sh["rwkv_w2"] = f(inp["rwkv_w2"][0])
    sh["rwkv_a2"] = f(inp["rwkv_a2"][0])
    sh["rwkv_g2"] = f(inp["rwkv_g2"][0])
    if n_layers > 1:
        sh["nsa_w_in"] = f(inp["nsa_w_in"][0])
        sh["nsa_w_out"] = f(inp["nsa_w_out"][0])
        sh["cmp_w1_k"] = f(inp["nsa_cmp_w1_k"][0]).reshape(2048, 256)
        sh["cmp_w1_v"] = f(inp["nsa_cmp_w1_v"][0]).reshape(2048, 256)
        sh["cmp_w2_k"] = f(inp["nsa_cmp_w2_k"][0])
        sh["cmp_w2_v"] = f(inp["nsa_cmp_w2_v"][0])
        pt = np.zeros((128, 32), np.float32)
        for kv, nm in enumerate(["nsa_cmp_pos_k", "nsa_cmp_pos_v"]):
            pos = np.asarray(inp[nm][0], np.float32)
            pt[:, kv * 16:(kv + 1) * 16] = pos.reshape(16, 2, 64).transpose(1, 2, 0).reshape(128, 16)
        sh["cmp_posT"] = pt
        sh.update(_consts_nsa(T))
    return sh


def kernel(**inputs):
    x = np.asarray(inputs["x"], dtype=np.float32)
    B, T, _ = x.shape
    n = 8
    NB = B // n
    nc, k = build(T=T, NB=NB, n_layers=2)
    sh = _prep_shared(inputs)
    in_maps = []
    for c in range(n):
        m = dict(sh)
        m["x"] = np.ascontiguousarray(x[c * NB:(c + 1) * NB])
        in_maps.append(m)
    res = run_bass_kernel_spmd(nc, in_maps, core_ids=list(range(n)))
    return np.concatenate([r["out"] for r in res.results], axis=0)
```
